# Optimizing a Trainium2 kernel written in Bass

```python
import math
import jax, jax.numpy as jnp
from jax import lax
import numpy as np

D_MODEL = 1024
BATCH = 16
SEQ = 2048
DEPTH = 2

GRID_W = 64
CTX_LEN = 256
N_EVEN = (DEPTH + 1) // 2
N_ODD = DEPTH // 2
EPS = 1e-6

ROPE_HEAD_DIM = 64
ROPE_FREQS = ROPE_HEAD_DIM // 4
ROPE_BASE = 10000.0

SC_WIDTH = D_MODEL // 2
DA_HEADS = 4
DA_HEAD_DIM = ROPE_HEAD_DIM
DA_V_DIM = 2 * DA_HEAD_DIM
DA_QK_WIDTH = DA_HEADS * 2 * DA_HEAD_DIM
DA_WIDTH = DA_HEADS * DA_V_DIM
Q_BLOCK = 128
EVEN_SPLITS = (SC_WIDTH, 2 * SC_WIDTH, 3 * SC_WIDTH, 3 * SC_WIDTH + DA_QK_WIDTH, 3 * SC_WIDTH + 2 * DA_QK_WIDTH)
EVEN_IN = 3 * SC_WIDTH + 2 * DA_QK_WIDTH + DA_WIDTH
EVEN_OUT = SC_WIDTH + DA_WIDTH

POOL_WINDOWS = (2, 4, 8, 16)
POOL_RADII = tuple(w // 2 for w in POOL_WINDOWS)
POOL_GROUPS = 4
POOL_WIDTH = D_MODEL // 2
POOL_GROUP_DIM = POOL_WIDTH // POOL_GROUPS
RET_HEADS = 4
RET_QK_DIM = ROPE_HEAD_DIM
RET_V_DIM = 2 * RET_QK_DIM
RET_QK_WIDTH = RET_HEADS * RET_QK_DIM
RET_WIDTH = RET_HEADS * RET_V_DIM
RET_CHUNK = 128
ODD_SPLITS = (POOL_WIDTH, POOL_WIDTH + RET_QK_WIDTH, POOL_WIDTH + 2 * RET_QK_WIDTH, POOL_WIDTH + 2 * RET_QK_WIDTH + RET_WIDTH)
ODD_IN = POOL_WIDTH + 2 * RET_QK_WIDTH + 2 * RET_WIDTH
ODD_OUT = POOL_WIDTH + RET_WIDTH

FFN_DIM = 2816

kernel_name = 'hybrid_conv_diffattn_pool_retention_dit'


def rms_norm(x, g):
    xf = x.astype(jnp.float32)
    y = xf * lax.rsqrt(jnp.mean(xf * xf, axis=-1, keepdims=True) + EPS)
    return (y * g.astype(jnp.float32)).astype(x.dtype)


def modulate(h, shift, scale):
    return h * (1 + scale) + shift


def dwconv3(x, w):
    xp = jnp.pad(x, ((0, 0), (1, 1), (0, 0)))
    return xp[:, :-2] * w[0] + xp[:, 1:-1] * w[1] + xp[:, 2:] * w[2]


def rope_tables(rows):
    row = jnp.repeat(jnp.arange(rows, dtype=jnp.float32), GRID_W)
    col = jnp.tile(jnp.arange(GRID_W, dtype=jnp.float32), rows)
    inv = ROPE_BASE ** (-jnp.arange(ROPE_FREQS, dtype=jnp.float32) / ROPE_FREQS)
    ang = jnp.stack([row[:, None] * inv, col[:, None] * inv], axis=1)
    return jnp.cos(ang), jnp.sin(ang)


def apply_rope(x, cos, sin):
    shp = x.shape
    xr = x.reshape(*shp[:-1], 2, 2, ROPE_FREQS)
    bc = (shp[1],) + (1,) * (x.ndim - 3) + (2, ROPE_FREQS)
    c = cos.reshape(bc).astype(x.dtype)
    s = sin.reshape(bc).astype(x.dtype)
    x1, x2 = xr[..., 0, :], xr[..., 1, :]
    out = jnp.stack([x1 * c - x2 * s, x1 * s + x2 * c], axis=-2)
    return out.reshape(shp)


def conv_ffn(h, w_up, conv_w, conv_b, w_down):
    a, b = jnp.split(h @ w_up, 2, axis=-1)
    a = dwconv3(a, conv_w) + conv_b
    return (jax.nn.silu(a) * b) @ w_down


def diff_lambda_init(layer_idx):
    return 0.8 - 0.6 * math.exp(-0.3 * layer_idx)


def diff_attend(q, k, v, lam):
    s = jnp.einsum('bhmqd,bhmkd->bhmqk', q, k).astype(jnp.float32) * (DA_HEAD_DIM ** -0.5)
    p = jax.nn.softmax(s, axis=-1)
    p = p[:, :, 0] - lam * p[:, :, 1]
    return jnp.einsum('bhqk,bhkv->bhqv', p.astype(v.dtype), v)


def conv_diff_mixer(h_ctx, h_lat, w_in, w_out, sc_conv_w, q_g, k_g, lq1, lk1, lq2, lk2, subln_g, lam_init, cos, sin):
    f32 = jnp.float32
    lam = (jnp.exp(jnp.sum(lq1.astype(f32) * lk1.astype(f32)))
           - jnp.exp(jnp.sum(lq2.astype(f32) * lk2.astype(f32))) + lam_init)

    def project(h, rotate):
        b_g, c_g, xv, q, k, v = jnp.split(h @ w_in, EVEN_SPLITS, axis=-1)
        B_, N = h.shape[:2]
        q = rms_norm(q.reshape(B_, N, DA_HEADS, 2, DA_HEAD_DIM), q_g)
        k = rms_norm(k.reshape(B_, N, DA_HEADS, 2, DA_HEAD_DIM), k_g)
        if rotate:
            q = apply_rope(q, cos, sin)
            k = apply_rope(k, cos, sin)
        q = q.transpose(0, 2, 3, 1, 4)
        k = k.transpose(0, 2, 3, 1, 4)
        v = v.reshape(B_, N, DA_HEADS, DA_V_DIM).transpose(0, 2, 1, 3)
        return (b_g, c_g, xv), q, k, v

    def finish(gates, o):
        b_g, c_g, xv = gates
        y_conv = b_g * dwconv3(c_g * xv, sc_conv_w)
        B_, _, N, _ = o.shape
        o = rms_norm(o.transpose(0, 2, 1, 3), subln_g) * (1.0 - lam_init)
        return jnp.concatenate([y_conv, o.reshape(B_, N, DA_WIDTH).astype(y_conv.dtype)], axis=-1) @ w_out

    gates_c, qc, kc, vc = project(h_ctx, False)
    gates_l, ql, kl, vl = project(h_lat, True)
    k_all = jnp.concatenate([kl, kc], axis=3)
    v_all = jnp.concatenate([vl, vc], axis=2)
    B_, H, _, N, d = ql.shape
    nb = N // Q_BLOCK
    q_blocks = jnp.moveaxis(ql.reshape(B_, H, 2, nb, Q_BLOCK, d), 3, 0)
    o_blocks = lax.map(lambda qb: diff_attend(qb, k_all, v_all, lam), q_blocks)
    o_lat = jnp.moveaxis(o_blocks, 0, 2).reshape(B_, H, N, DA_V_DIM)
    y_lat = finish(gates_l, o_lat)
    y_ctx = finish(gates_c, diff_attend(qc, kc, vc, lam))
    return y_ctx, y_lat


def multiscale_pool(v, pool_w, pool_scale):
    B_, N, _ = v.shape
    vg = v.reshape(B_, N, POOL_GROUPS, POOL_GROUP_DIM).astype(jnp.float32)
    cs = jnp.concatenate([jnp.zeros_like(vg[:, :1]), jnp.cumsum(vg, axis=1)], axis=1)
    t = jnp.arange(N)[:, None]
    r = jnp.array(POOL_RADII)[None, :]
    lo = jnp.clip(t - r, 0, N)
    hi = jnp.clip(t + r + 1, 0, N)
    g = jnp.arange(POOL_GROUPS)[None, :]
    mean = (cs[:, hi, g] - cs[:, lo, g]) / (hi - lo).astype(jnp.float32)[:, :, None]
    y = (mean - vg).astype(v.dtype)
    y = jnp.einsum('bngc,gcd->bngd', y, pool_w)
    return y.reshape(B_, N, POOL_WIDTH) * pool_scale


def retention_scan(q, k, v, lg, s0):
    B_, H, N, _ = q.shape
    dv = v.shape[-1]
    nc = N // RET_CHUNK

    def chunks(t):
        return jnp.moveaxis(t.reshape(B_, H, nc, RET_CHUNK, t.shape[-1]), 2, 0)

    pos = jnp.arange(RET_CHUNK, dtype=jnp.float32)
    diff = pos[:, None] - pos[None, :]
    intra = jnp.where(diff >= 0, jnp.exp(lg[:, None, None] * jnp.maximum(diff, 0.0)), 0.0)
    q_dec = jnp.exp(lg[:, None] * (pos + 1.0))[None, :, :, None]
    k_dec = jnp.exp(lg[:, None] * (RET_CHUNK - 1.0 - pos))[None, :, :, None]
    c_dec = jnp.exp(lg * RET_CHUNK)[None, :, None, None]

    def step(s, qkv):
        qc, kc, vc = qkv
        att = jnp.einsum('bhid,bhjd->bhij', qc, kc) * intra
        o = jnp.einsum('bhij,bhjv->bhiv', att, vc) + jnp.einsum('bhid,bhdv->bhiv', qc * q_dec, s)
        s = s * c_dec + jnp.einsum('bhjd,bhjv->bhdv', kc * k_dec, vc)
        return s, o

    _, o = lax.scan(step, s0, (chunks(q), chunks(k), chunks(v)))
    return jnp.moveaxis(o, 0, 2).reshape(B_, H, N, dv)


def retention_bidir(q, k, v, lg_f, lg_b, s_f, s_b):
    o_f = retention_scan(q, k, v, lg_f, s_f)
    flip = lambda t: jnp.flip(t, axis=2)
    o_b = flip(retention_scan(flip(q), flip(k), flip(v), lg_b, s_b))
    return o_f + o_b


def context_states(k, v, lg_f, lg_b):
    L = k.shape[2]
    j = jnp.arange(L, dtype=jnp.float32)
    w_f = jnp.exp(lg_f[:, None] * (L - 1.0 - j))
    w_b = jnp.exp(lg_b[:, None] * j)
    s_f = jnp.einsum('bhjd,hj,bhjv->bhdv', k, w_f, v)
    s_b = jnp.einsum('bhjd,hj,bhjv->bhdv', k, w_b, v)
    return s_f, s_b


def pool_retention_mixer(h_ctx, h_lat, w_in, w_out, pool_w, pool_scale, dec_f, dec_b, gn_g, cos, sin, with_ctx):
    f32 = jnp.float32
    lg_f = -jnp.exp(dec_f.astype(f32))
    lg_b = -jnp.exp(dec_b.astype(f32))

    def heads(t, dh):
        B_, N = t.shape[:2]
        return t.reshape(B_, N, RET_HEADS, dh)

    def qkv_heads(q, k, v, rotate):
        q, k, v = heads(q, RET_QK_DIM), heads(k, RET_QK_DIM), heads(v, RET_V_DIM)
        if rotate:
            q = apply_rope(q, cos, sin)
            k = apply_rope(k, cos, sin)
        q = (q.astype(f32) * (RET_QK_DIM ** -0.5)).transpose(0, 2, 1, 3)
        k = k.astype(f32).transpose(0, 2, 1, 3)
        v = v.astype(f32).transpose(0, 2, 1, 3)
        return q, k, v

    def finish(pv, g, o):
        B_, _, N, _ = o.shape
        y_r = rms_norm(o.transpose(0, 2, 1, 3), gn_g).reshape(B_, N, RET_WIDTH).astype(g.dtype) * jax.nn.silu(g)
        y_p = multiscale_pool(pv, pool_w, pool_scale)
        return jnp.concatenate([y_p, y_r], axis=-1) @ w_out

    pv_l, q_l, k_l, v_l, g_l = jnp.split(h_lat @ w_in, ODD_SPLITS, axis=-1)
    ql, kl, vl = qkv_heads(q_l, k_l, v_l, True)
    if with_ctx:
        pv_c, q_c, k_c, v_c, g_c = jnp.split(h_ctx @ w_in, ODD_SPLITS, axis=-1)
        qc, kc, vc = qkv_heads(q_c, k_c, v_c, False)
    else:
        k_c, v_c = jnp.split(h_ctx @ w_in[:, ODD_SPLITS[1]:ODD_SPLITS[3]], [RET_QK_WIDTH], axis=-1)
        kc = heads(k_c, RET_QK_DIM).astype(f32).transpose(0, 2, 1, 3)
        vc = heads(v_c, RET_V_DIM).astype(f32).transpose(0, 2, 1, 3)
    s_f, s_b = context_states(kc, vc, lg_f, lg_b)
    y_lat = finish(pv_l, g_l, retention_bidir(ql, kl, vl, lg_f, lg_b, s_f, s_b))
    y_ctx = None
    if with_ctx:
        zero = jnp.zeros_like(s_f)
        y_ctx = finish(pv_c, g_c, retention_bidir(qc, kc, vc, lg_f, lg_b, zero, zero))
    return y_ctx, y_lat


def setup_inputs(seed: int = 0) -> dict:
    key = jax.random.key(seed)
    ks = iter(jax.random.split(key, 40))

    def nrm(shape, scale=1.0):
        return scale * jax.random.normal(next(ks), shape, jnp.float32)

    def gain(shape):
        return 1.0 + nrm(shape, 0.05)

    base_decay = jnp.log(-jnp.log1p(-(2.0 ** (-5.0 - jnp.arange(RET_HEADS, dtype=jnp.float32)))))
    return {
        'x': nrm((BATCH, SEQ, D_MODEL)),
        'c': nrm((BATCH, D_MODEL)),
        'ctx': nrm((BATCH, CTX_LEN, D_MODEL)),
        'c_ctx': nrm((D_MODEL,)),
        'mod_w': nrm((DEPTH, D_MODEL, 6 * D_MODEL), 0.5 * D_MODEL ** -0.5),
        'mod_b': nrm((DEPTH, 6 * D_MODEL), 0.01),
        'norm1_g': gain((DEPTH, D_MODEL)),
        'norm2_g': gain((DEPTH, D_MODEL)),
        'ev_w_in': nrm((N_EVEN, D_MODEL, EVEN_IN), D_MODEL ** -0.5),
        'ev_w_out': nrm((N_EVEN, EVEN_OUT, D_MODEL), EVEN_OUT ** -0.5),
        'sc_conv_w': nrm((N_EVEN, 3, SC_WIDTH), 3 ** -0.5),
        'da_q_norm': gain((N_EVEN, DA_HEAD_DIM)),
        'da_k_norm': gain((N_EVEN, DA_HEAD_DIM)),
        'da_lq1': nrm((N_EVEN, DA_HEAD_DIM), 0.1),
        'da_lk1': nrm((N_EVEN, DA_HEAD_DIM), 0.1),
        'da_lq2': nrm((N_EVEN, DA_HEAD_DIM), 0.1),
        'da_lk2': nrm((N_EVEN, DA_HEAD_DIM), 0.1),
        'da_subln_g': gain((N_EVEN, DA_V_DIM)),
        'od_w_in': nrm((N_ODD, D_MODEL, ODD_IN), D_MODEL ** -0.5),
        'od_w_out': nrm((N_ODD, ODD_OUT, D_MODEL), ODD_OUT ** -0.5),
        'pool_w': nrm((N_ODD, POOL_GROUPS, POOL_GROUP_DIM, POOL_GROUP_DIM), POOL_GROUP_DIM ** -0.5),
        'pool_scale': gain((N_ODD, POOL_WIDTH)),
        'ret_decay_f': base_decay + nrm((N_ODD, RET_HEADS), 0.1),
        'ret_decay_b': base_decay + nrm((N_ODD, RET_HEADS), 0.1),
        'ret_gn_g': gain((N_ODD, RET_V_DIM)),
        'ffn_w_up': nrm((DEPTH, D_MODEL, 2 * FFN_DIM), D_MODEL ** -0.5),
        'ffn_conv_w': nrm((DEPTH, 3, FFN_DIM), 3 ** -0.5),
        'ffn_conv_b': nrm((DEPTH, FFN_DIM), 0.01),
        'ffn_w_down': nrm((DEPTH, FFN_DIM, D_MODEL), FFN_DIM ** -0.5),
    }


def reference(x, c, ctx, c_ctx, mod_w, mod_b, norm1_g, norm2_g, ev_w_in, ev_w_out, sc_conv_w, da_q_norm, da_k_norm,
              da_lq1, da_lk1, da_lq2, da_lk2, da_subln_g, od_w_in, od_w_out, pool_w, pool_scale, ret_decay_f,
              ret_decay_b, ret_gn_g, ffn_w_up, ffn_conv_w, ffn_conv_b, ffn_w_down):
    ROWS = x.shape[1] // GRID_W
    cos, sin = rope_tables(ROWS)
    silu_c = jax.nn.silu(c)
    silu_cc = jax.nn.silu(c_ctx)
    for i in range(DEPTH):
        last = i == DEPTH - 1
        j = i // 2
        mod_l = (silu_c @ mod_w[i] + mod_b[i])[:, None, :]
        mod_c = silu_cc @ mod_w[i] + mod_b[i]
        sh1, sc1, g1, sh2, sc2, g2 = jnp.split(mod_l, 6, axis=-1)
        sh1c, sc1c, g1c, sh2c, sc2c, g2c = jnp.split(mod_c, 6, axis=-1)
        h_lat = modulate(rms_norm(x, norm1_g[i]), sh1, sc1)
        h_ctx = modulate(rms_norm(ctx, norm1_g[i]), sh1c, sc1c)
        if i % 2 == 0:
            y_ctx, y_lat = conv_diff_mixer(h_ctx, h_lat, ev_w_in[j], ev_w_out[j], sc_conv_w[j], da_q_norm[j],
                                           da_k_norm[j], da_lq1[j], da_lk1[j], da_lq2[j], da_lk2[j],
                                           da_subln_g[j], diff_lambda_init(i), cos, sin)
        else:
            y_ctx, y_lat = pool_retention_mixer(h_ctx, h_lat, od_w_in[j], od_w_out[j], pool_w[j], pool_scale[j],
                                                ret_decay_f[j], ret_decay_b[j], ret_gn_g[j], cos, sin, not last)
        x = x + g1 * y_lat
        x = x + g2 * conv_ffn(modulate(rms_norm(x, norm2_g[i]), sh2, sc2),
                              ffn_w_up[i], ffn_conv_w[i], ffn_conv_b[i], ffn_w_down[i])
        if not last:
            ctx = ctx + g1c * y_ctx
            ctx = ctx + g2c * conv_ffn(modulate(rms_norm(ctx, norm2_g[i]), sh2c, sc2c),
                                       ffn_w_up[i], ffn_conv_w[i], ffn_conv_b[i], ffn_w_down[i])
    return x
```

```python
import math
from contextlib import ExitStack

import numpy as np
import concourse.bass as bass
import concourse.mybir as mybir
from concourse.bass_utils import run_bass_kernel_spmd

F32 = mybir.dt.float32
BF16 = mybir.dt.bfloat16
AF = mybir.ActivationFunctionType
ALU = mybir.AluOpType

D = 1024
KC = 8
NCTX = 256
NLAT = 2048
T = NCTX + NLAT
PT = T + 3
BLKS = [(0, 256), (256, 512), (768, 512), (1280, 512), (1792, 512)]
LATBLKS = BLKS[1:]
FF = 2816
FJ = 22
EPS = 1e-6
NB = 2


def poff(off):
    return off + 1 if off < NCTX else off + 2


class Sem:
    def __init__(self, nc, name):
        self.h = nc.alloc_semaphore(name=name)
        self.cnt = 0
        self.name = name


class Buf:
    def __init__(self, name="", excl=False):
        self.w = {}
        self.r = {}
        self.name = name
        self.excl = excl


class Eng:
    def __init__(self, nc, name, obj, is_pe=False):
        self.obj = obj
        self.sem = Sem(nc, "e_" + name)
        self.waited = {}
        self.is_pe = is_pe
        self.name = name


class FW:
    def __init__(self, nc):
        self.nc = nc
        self.E = {
            "pe": Eng(nc, "pe", nc.tensor, True),
            "act": Eng(nc, "act", nc.scalar),
            "dve": Eng(nc, "dve", nc.vector),
            "pool": Eng(nc, "pool", nc.gpsimd),
            "sp": Eng(nc, "sp", nc.sync),
        }
        self.enabled = True
        self.dsems = []
        self.free_sems = {"sp": [], "pool": []}
        self.nsem = 0

    def dsem(self, q="sp"):
        if self.free_sems[q]:
            return self.free_sems[q].pop()
        self.nsem += 1
        x = Sem(self.nc, f"d{q}{self.nsem}")
        x.q = q
        self.dsems.append(x)
        return x

    def _wait(self, E, reads, writes, skip_sem=None):
        deps = {}
        for b in reads:
            for k, v in b.w.items():
                if v > deps.get(k, 0):
                    deps[k] = v
            if b.excl:
                for k, v in b.r.items():
                    if k is not E.sem and v > deps.get(k, 0):
                        deps[k] = v
        for b in writes:
            for k, v in b.w.items():
                if v > deps.get(k, 0):
                    deps[k] = v
            for k, v in b.r.items():
                if v > deps.get(k, 0):
                    deps[k] = v
        for k, v in deps.items():
            if k is skip_sem:
                continue
            if E.is_pe and k is E.sem:
                continue
            if E.waited.get(k, 0) >= v:
                continue
            E.obj.wait_ge(k.h, v)
            E.waited[k] = v

    def op(self, eng, fn, reads=(), writes=()):
        if not self.enabled:
            return None
        E = self.E[eng]
        self._wait(E, reads, writes)
        ins = fn(E.obj)
        E.sem.cnt += 1
        ins.then_inc(E.sem.h, 1)
        c = E.sem.cnt
        for b in reads:
            b.r[E.sem] = c
        for b in writes:
            b.w = {E.sem: c}
            b.r = {}
        return ins

    def mm(self, out_ap, pairs, reads, writes, start=True, stop=True):
        if not self.enabled:
            return
        E = self.E["pe"]
        self._wait(E, reads, writes)
        n = len(pairs)
        ins = None
        for i, (l, r) in enumerate(pairs):
            ins = E.obj.matmul(out_ap, l, r, start=(start and i == 0), stop=(stop and i == n - 1))
        E.sem.cnt += 1
        ins.then_inc(E.sem.h, 1)
        c = E.sem.cnt
        for b in reads:
            b.r[E.sem] = c
        for b in writes:
            b.w = {E.sem: c}
            b.r = {}

    def dma(self, q, out_ap, in_ap, reads, writes, sem, merge=False):
        if not self.enabled:
            return
        E = self.E[q]
        assert getattr(sem, "q", q) == q, (sem.name, q)
        self._wait(E, reads, writes, skip_sem=sem if merge else None)
        ins = E.obj.dma_start(out=out_ap, in_=in_ap)
        sem.cnt += 16
        ins.then_inc(sem.h, 16)
        for b in reads:
            b.r[sem] = sem.cnt
        for b in writes:
            if merge:
                b.w[sem] = sem.cnt
            else:
                b.w = {sem: sem.cnt}
                b.r = {}

    def barrier(self):
        allsems = [E.sem for E in self.E.values()] + self.dsems
        for E in self.E.values():
            for k in allsems:
                if k is E.sem:
                    continue
                if k.cnt > E.waited.get(k, 0):
                    E.obj.wait_ge(k.h, k.cnt)
                    E.waited[k] = k.cnt

    def final_wait(self, sems):
        E = self.E["sp"]
        for k in sems:
            if k.cnt > E.waited.get(k, 0):
                E.obj.wait_ge(k.h, k.cnt)
                E.waited[k] = k.cnt


class Tile:
    def __init__(self, fw, es, name, shape, dt, psum=False, dma=False, dma2=False):
        nc = fw.nc
        fw.ntile = getattr(fw, "ntile", 0) + 1
        name = f"{name}_{fw.ntile}"
        if psum:
            self.t = es.enter_context(nc.psum_tensor(name, shape, dt))
        else:
            self.t = es.enter_context(nc.sbuf_tensor(name, shape, dt))
        self.b = Buf(name, excl=psum)
        self.sem = None
        self.sem2 = None
        if dma:
            q = dma if isinstance(dma, str) else "sp"
            self.sem = fw.dsem(q)
            es.callback(fw.free_sems[q].append, self.sem)
        if dma2:
            self.sem2 = fw.dsem("sp")
            es.callback(fw.free_sems["sp"].append, self.sem2)

    def __getitem__(self, idx):
        return self.t[idx]


class Ring:
    def __init__(self, fw, es, name, n, shape, dt, psum=False, dma=False, dma2=False):
        self.tiles = [Tile(fw, es, f"{name}{i}", shape, dt, psum=psum, dma=dma, dma2=dma2) for i in range(n)]
        self.i = 0

    def next(self):
        t = self.tiles[self.i % len(self.tiles)]
        self.i += 1
        return t


def build_program(cfg=None):
    cfg = cfg or {}
    nlayers = cfg.get("nlayers", 2)
    nb = cfg.get("nb", NB)
    stop_after = cfg.get("stop_after", None)
    only = cfg.get("only", None)

    def on(ph):
        return only is None or ph in only
    skip = set(cfg.get("skip", []))
    nc = bass.Bass("TRN2", target_bir_lowering=False)
    fw = FW(nc)
    ES = ExitStack()

    def din(name, shape, dt=F32):
        return nc.dram_tensor(name, list(shape), dt, kind="ExternalInput").ap()

    xT = din("xT", [NB, D, NLAT])
    ctxT = din("ctxT", [NB, D, NCTX])
    cT = din("cT", [128, KC * 3])
    modw = din("modw", [2, 48, 128, KC * 128])
    modb = din("modb", [128, 2 * 48])
    ng = din("ng", [128, 2 * 2 * KC])
    w_in0 = din("w_in0", [28, 128, KC * 128])
    w_v0 = din("w_v0", [128, KC * 512])
    w_out = din("w_out", [2, 128, KC * 1024])
    scw = din("scw", [128, 4 * 3])
    qkg = din("qkg", [128, 4])
    lamv = din("lamv", [128, 4 * 64])
    subg = din("subg", [128, 1])
    ropec = din("ropec", [128, T])
    ropes = din("ropes", [128, T])
    w_up = din("w_up", [2, FJ, 128, 2 * KC * 128])
    w_dn = din("w_dn", [2, FJ, 128, 1024])
    fcw = din("fcw", [128, 2 * FJ * 3])
    fcb = din("fcb", [128, 2 * FJ])
    w_in1 = din("w_in1", [16, 128, KC * 128])
    w_v1 = din("w_v1", [128, KC * 512])
    poolw = din("poolw", [128, 4 * 128])
    pools = din("pools", [128, 4])
    decp = din("decp", [128, 4])
    decr = din("decr", [128, 8])
    gng = din("gng", [128, 1])
    rtab = din("rtab", [128, 2 * 128 + 2 + 5 * 128])
    pcorr = din("pcorr", [128, 4 * 2 * 8])
    outT = nc.dram_tensor("outT", [NB, D, NLAT], F32, kind="ExternalOutput").ap()
    dbg = None
    if cfg.get("dbg"):
        dbg = nc.dram_tensor("dbg", list(cfg["dbg"]), F32, kind="ExternalOutput").ap()
    resT = nc.dram_tensor("resT", [NB, KC, 128, T], F32).ap()
    uT = nc.dram_tensor("uT", [NB, FJ, 128, T], BF16).ap()
    res_b = [[Buf(f"res{bi}_{k}") for k in range(len(BLKS))] for bi in range(NB)]
    u_b = [[Buf(f"u{bi}_{k}") for k in range(len(BLKS))] for bi in range(NB)]
    out_sems = []

    def T_(name, shape, dt, es=ES, **kw):
        return Tile(fw, es, name, shape, dt, **kw)

    PS = [T_(f"ps{i}", [128, 512], F32, psum=True) for i in range(8)]
    psi = [0]

    def ps_next():
        t = PS[psi[0] % 8]
        psi[0] += 1
        return t

    hT = T_("hT", [128, KC, T], BF16)
    hT_b = [Buf(f"hT{k}") for k in range(len(BLKS))]
    ones_n = T_("ones_n", [128, 128], BF16)
    ones_h = T_("ones_h", [128, 128], BF16)
    ones_1 = T_("ones_1", [128, 128], BF16)
    ones_g = T_("ones_g", [128, 128], BF16)
    epsT = T_("epsT", [128, 1], F32)
    cst = T_("cst", [128, KC * 3], F32, dma=True)
    sct = T_("sct", [128, KC * 3], F32)
    modT = T_("modT", [128, 2 * 48 * 3], F32)
    modbT = T_("modbT", [128, 96], F32, dma=True)
    ngT = T_("ngT", [128, 32], F32, dma=True)
    AT = T_("AT", [128, 2 * 2 * KC * 3], F32)
    smallc = T_("smallc", [128, 12 + 4 + 256 + 1 + 132 + 44], F32, dma=True)
    o_scw, o_qkg, o_lamv, o_subg, o_fcw, o_fcb = 0, 12, 16, 272, 273, 405
    lam_t = T_("lam_t", [128, 8], F32)

    def C(o, n=1):
        return smallc[:, o:o + n]

    def mod_ap(l, grp, kc, j):
        i = (l * 48 + grp * 8 + kc) * 3 + j
        return modT[:, i:i + 1]

    def A_ap(l, n, kc, j):
        i = ((l * 2 + n) * KC + kc) * 3 + j
        return AT[:, i:i + 1]

    fw.op("dve", lambda e: e.memset(ones_n[:], 1.0 / 1024.0), writes=[ones_n.b])
    fw.op("dve", lambda e: e.memset(ones_h[:], 1.0 / 128.0), writes=[ones_h.b])
    fw.op("dve", lambda e: e.memset(ones_1[:], 1.0), writes=[ones_1.b])
    fw.op("dve", lambda e: e.memset(ones_g[:], 0.0), writes=[ones_g.b])
    fw.op("dve", lambda e: e.memset(ones_g[0:64, 0:64], 1.0 / 64.0), writes=[ones_g.b])
    fw.op("dve", lambda e: e.memset(ones_g[64:128, 64:128], 1.0 / 64.0), writes=[ones_g.b])
    fw.op("dve", lambda e: e.memset(epsT[:], EPS), writes=[epsT.b])
    fw.dma("sp", cst[:], cT, [], [cst.b], cst.sem)
    fw.dma("sp", modbT[:], modb, [], [modbT.b], modbT.sem)
    fw.dma("sp", ngT[:], ng, [], [ngT.b], ngT.sem)
    for (o, src, n) in [(o_scw, scw, 12), (o_qkg, qkg, 4), (o_lamv, lamv, 256), (o_subg, subg, 1),
                        (o_fcw, fcw, 132), (o_fcb, fcb, 44)]:
        fw.dma("sp", smallc[:, o:o + n], src, [], [smallc.b], smallc.sem, merge=True)

    fw.op("act", lambda e: e.activation(out=sct[:], in_=cst[:], func=AF.Silu), reads=[cst.b], writes=[sct.b])
    with ExitStack() as es:
        mw = Ring(fw, es, "mw", 8, [128, KC * 128], F32, dma=True)
        for l in range(2):
            pst = ps_next()
            for oc in range(48):
                w = mw.next()
                fw.dma("sp", w[:], modw[l, oc], [], [w.b], w.sem)
                fw.mm(pst[:, oc * 3:oc * 3 + 3],
                      [(w[:, kc * 128:(kc + 1) * 128], sct[:, kc * 3:kc * 3 + 3]) for kc in range(KC)],
                      reads=[w.b, sct.b], writes=[pst.b])
            for j in range(3):
                fw.op("dve", lambda e, l=l, j=j, pst=pst: e.tensor_tensor(
                    out=modT[:, l * 144:(l + 1) * 144].rearrange("p (o j) -> p o j", j=3)[:, :, j],
                    in0=pst[:, 0:144].rearrange("p (o j) -> p o j", j=3)[:, :, j],
                    in1=modbT[:, l * 48:(l + 1) * 48], op=ALU.add),
                    reads=[pst.b, modbT.b], writes=[modT.b])
        for l in range(2):
            for n in range(2):
                grp = 1 if n == 0 else 4
                for j in range(3):
                    base_m = (l * 48 + grp * 8) * 3
                    base_a = ((l * 2 + n) * KC) * 3
                    fw.op("dve", lambda e, base_m=base_m, base_a=base_a, j=j, l=l, n=n: e.scalar_tensor_tensor(
                        out=AT[:, base_a:base_a + 24].rearrange("p (k j) -> p k j", j=3)[:, :, j],
                        in0=modT[:, base_m:base_m + 24].rearrange("p (k j) -> p k j", j=3)[:, :, j],
                        scalar=1.0,
                        in1=ngT[:, (l * 2 + n) * KC:(l * 2 + n + 1) * KC],
                        op0=ALU.add, op1=ALU.mult),
                        reads=[modT.b, ngT.b], writes=[AT.b])
        fw.barrier()

    def dbg_dump(tile_ap, rd, rows, cols, row0=0):
        s = fw.dsem()
        fw.dma("sp", dbg[row0:row0 + rows, 0:cols], tile_ap, rd, [], s)
        out_sems.append(s)

    def finish():
        fw.barrier()
        fw.final_wait(out_sems)
        ES.close()
        return nc

    if stop_after == "M":
        dbg_dump(modT[:], [modT.b], 128, 288)
        dbg_dump(AT[:], [AT.b], 128, 96, row0=128)
        return finish()

    def norm_block(es_t, x_t, n, off, k, l, nidx, j):
        sq, lnb, rs, tt = es_t["sq"], es_t["lnb"], es_t["rs"], es_t["tt"]
        fw.op("act", lambda e: e.activation(out=sq[:, :, 0:n], in_=x_t[:, :, 0:n], func=AF.Square),
              reads=[x_t.b], writes=[sq.b])
        pst = ps_next()
        fw.mm(pst[:, 0:n], [(ones_n[:], sq[:, kc, 0:n]) for kc in range(KC)],
              reads=[ones_n.b, sq.b], writes=[pst.b])
        fw.op("act", lambda e: e.activation(out=lnb[:, 0:n], in_=pst[:, 0:n], func=AF.Ln, bias=epsT[:], scale=1.0),
              reads=[pst.b, epsT.b], writes=[lnb.b])
        fw.op("act", lambda e: e.activation(out=rs[:, 0:n], in_=lnb[:, 0:n], func=AF.Exp, scale=-0.5),
              reads=[lnb.b], writes=[rs.b])
        for kc in range(KC):
            t = tt.next()
            fw.op("dve", lambda e, kc=kc, t=t: e.scalar_tensor_tensor(
                out=t[:, 0:n], in0=x_t[:, kc, 0:n], scalar=A_ap(l, nidx, kc, j), in1=rs[:, 0:n],
                op0=ALU.mult, op1=ALU.mult), reads=[x_t.b, AT.b, rs.b], writes=[t.b])
            grp = 0 if nidx == 0 else 3
            fw.op("act", lambda e, kc=kc, t=t: e.activation(
                out=hT[:, kc, off:off + n], in_=t[:, 0:n], func=AF.Identity, bias=mod_ap(l, grp, kc, j), scale=1.0),
                reads=[t.b, modT.b], writes=[hT_b[k]])

    def norm_tiles(es, pfx):
        return {
            "sq": T_(pfx + "sq", [128, KC, 512], BF16, es=es),
            "lnb": T_(pfx + "lnb", [128, 512], F32, es=es),
            "rs": T_(pfx + "rs", [128, 512], F32, es=es),
            "tt": Ring(fw, es, pfx + "tt", 2, [128, 512], F32),
        }

    def x_src(l, bi, k):
        off, n = BLKS[k]
        if l == 0:
            if k == 0:
                return ctxT[bi].rearrange("(c p) t -> p c t", p=128), None
            return xT[bi].rearrange("(c p) t -> p c t", p=128)[:, :, off - NCTX:off - NCTX + n], None
        return resT[bi].rearrange("c p t -> p c t")[:, :, off:off + n], res_b[bi][k]

    LAM0 = 0.8 - 0.6 * math.exp(-0.3 * 0)
    with ExitStack() as es:
        ltmp = T_("ltmp", [128, 128], F32, es=es)
        fw.op("dve", lambda e: e.tensor_tensor(out=ltmp[:, 0:64], in0=C(o_lamv, 64), in1=C(o_lamv + 64, 64), op=ALU.mult),
              reads=[smallc.b], writes=[ltmp.b])
        fw.op("dve", lambda e: e.tensor_tensor(out=ltmp[:, 64:128], in0=C(o_lamv + 128, 64), in1=C(o_lamv + 192, 64), op=ALU.mult),
              reads=[smallc.b], writes=[ltmp.b])
        fw.op("dve", lambda e: e.reduce_sum(out=lam_t[:, 0:1], in_=ltmp[:, 0:64], axis=mybir.AxisListType.X),
              reads=[ltmp.b], writes=[lam_t.b])
        fw.op("dve", lambda e: e.reduce_sum(out=lam_t[:, 1:2], in_=ltmp[:, 64:128], axis=mybir.AxisListType.X),
              reads=[ltmp.b], writes=[lam_t.b])
        fw.op("act", lambda e: e.activation(out=lam_t[:, 2:4], in_=lam_t[:, 0:2], func=AF.Exp), reads=[lam_t.b], writes=[lam_t.b])
        fw.op("dve", lambda e: e.tensor_tensor(out=lam_t[:, 4:5], in0=lam_t[:, 3:4], in1=lam_t[:, 2:3], op=ALU.subtract),
              reads=[lam_t.b], writes=[lam_t.b])
        fw.op("dve", lambda e: e.tensor_scalar(out=lam_t[:, 4:5], in0=lam_t[:, 4:5], scalar1=-LAM0, scalar2=None, op0=ALU.add),
              reads=[lam_t.b], writes=[lam_t.b])
        fw.op("dve", lambda e: e.tensor_scalar(out=lam_t[:, 5:6], in0=C(o_subg), scalar1=1.0 - LAM0, scalar2=None, op0=ALU.mult),
              reads=[smallc.b, lam_t.b], writes=[lam_t.b])
        fw.barrier()
    NEGLAM = lam_t[:, 4:5]
    SUBG = lam_t[:, 5:6]

    dumps = cfg.get("dumps", {})

    def dump_bf16(name, src3, rd, nch, cols):
        if name not in dumps:
            return
        fw.barrier()
        with ExitStack() as es:
            dt_ = T_("dbg_" + name, [128, cols], F32, es=es)
            for ch in range(nch):
                fw.op("dve", lambda e, ch=ch: e.tensor_copy(out=dt_[:], in_=src3[:, ch, :]), reads=rd, writes=[dt_.b])
                dbg_dump(dt_[:], [dt_.b], 128, cols, row0=dumps[name] + ch * 128)
            fw.barrier()

    def dump_res(name, bi):
        if name not in dumps:
            return
        fw.barrier()
        s = fw.dsem()
        r0 = dumps[name]
        fw.dma("sp", dbg[r0:r0 + 1024, 0:T], resT[bi].rearrange("c p t -> (c p) t"), res_b[bi], [], s)
        out_sems.append(s)
        fw.barrier()

    def ffn(l, bi, blks, jl):
        eso = ExitStack()
        fw.enabled = on("F2")
        wd = T_("wd", [128, FJ * 1024], BF16, es=eso, dma="pool")
        for jf in range(FJ):
            fw.dma("pool", wd[:, jf * 1024:(jf + 1) * 1024], w_dn[l, jf], [], [wd.b], wd.sem, merge=True)
        with ExitStack() as es:
            fw.enabled = on("F1")
            wab = Ring(fw, es, "wab", 3, [128, 2 * KC * 128], BF16, dma="pool")
            abuf = Ring(fw, es, "abuf", 2, [128, PT], F32)
            bbuf = Ring(fw, es, "bbuf", 2, [128, PT], F32)
            cbuf = T_("cbuf", [128, PT], F32, es=es)
            ubuf = Ring(fw, es, "ubuf", 2, [128, T], BF16, dma=True)
            for a in abuf.tiles:
                fw.op("dve", lambda e, a=a: e.memset(a[:], 0.0), writes=[a.b])
            lo = 1 if blks[0][0] == 0 else NCTX + 2
            hi = PT - 1
            for jf in range(FJ):
                w = wab.next()
                fw.dma("pool", w[:], w_up[l, jf], [], [w.b], w.sem)
                ab = abuf.next()
                bb = bbuf.next()
                for (off, n) in blks:
                    k = BLKS.index((off, n))
                    po = poff(off)
                    pa = ps_next()
                    fw.mm(pa[:, 0:n], [(w[:, kc * 128:(kc + 1) * 128], hT[:, kc, off:off + n]) for kc in range(KC)],
                          reads=[w.b, hT_b[k]], writes=[pa.b])
                    pb = ps_next()
                    fw.mm(pb[:, 0:n], [(w[:, 1024 + kc * 128:1024 + (kc + 1) * 128], hT[:, kc, off:off + n]) for kc in range(KC)],
                          reads=[w.b, hT_b[k]], writes=[pb.b])
                    fw.op("act", lambda e, pa=pa, ab=ab, po=po, n=n: e.activation(out=ab[:, po:po + n], in_=pa[:, 0:n], func=AF.Identity),
                          reads=[pa.b], writes=[ab.b])
                    fw.op("act", lambda e, pb=pb, bb=bb, po=po, n=n: e.activation(out=bb[:, po:po + n], in_=pb[:, 0:n], func=AF.Identity),
                          reads=[pb.b], writes=[bb.b])
                wbase = o_fcw + (l * FJ + jf) * 3
                fw.op("dve", lambda e, ab=ab, wbase=wbase: e.tensor_scalar(
                    out=cbuf[:, lo:hi], in0=ab[:, lo - 1:hi - 1], scalar1=C(wbase), scalar2=None, op0=ALU.mult),
                    reads=[ab.b, smallc.b], writes=[cbuf.b])
                fw.op("dve", lambda e, ab=ab, wbase=wbase: e.scalar_tensor_tensor(
                    out=cbuf[:, lo:hi], in0=ab[:, lo:hi], scalar=C(wbase + 1), in1=cbuf[:, lo:hi], op0=ALU.mult, op1=ALU.add),
                    reads=[ab.b, smallc.b, cbuf.b], writes=[cbuf.b])
                fw.op("dve", lambda e, ab=ab, wbase=wbase: e.scalar_tensor_tensor(
                    out=cbuf[:, lo:hi], in0=ab[:, lo + 1:hi + 1], scalar=C(wbase + 2), in1=cbuf[:, lo:hi], op0=ALU.mult, op1=ALU.add),
                    reads=[ab.b, smallc.b, cbuf.b], writes=[cbuf.b])
                if "F1silu" not in skip:
                    fw.op("act", lambda e, jf=jf: e.activation(out=cbuf[:, lo:hi], in_=cbuf[:, lo:hi], func=AF.Silu,
                                                               bias=C(o_fcb + l * FJ + jf), scale=1.0),
                          reads=[cbuf.b, smallc.b], writes=[cbuf.b])
                u = ubuf.next()
                if lo == 1:
                    fw.op("dve", lambda e, u=u, bb=bb: e.tensor_tensor(out=u[:, 0:NCTX], in0=cbuf[:, 1:1 + NCTX], in1=bb[:, 1:1 + NCTX], op=ALU.mult),
                          reads=[cbuf.b, bb.b], writes=[u.b])
                fw.op("dve", lambda e, u=u, bb=bb: e.tensor_tensor(out=u[:, NCTX:T], in0=cbuf[:, NCTX + 2:PT - 1], in1=bb[:, NCTX + 2:PT - 1], op=ALU.mult),
                      reads=[cbuf.b, bb.b], writes=[u.b])
                t0 = 0 if lo == 1 else NCTX
                if "F1store" not in skip:
                    fw.dma("sp", uT[bi, jf][:, t0:T], u[:, t0:T], [u.b], [u_b[bi][BLKS.index(b)] for b in blks], u.sem, merge=True)
            fw.barrier()
        with ExitStack() as es:
            fw.enabled = on("F2")
            us = Ring(fw, es, "us", 2, [128, FJ, 512], BF16, dma=True)
            xs = Ring(fw, es, "f2x", 2, [128, KC, 512], F32, dma=True, dma2=True)
            nt = norm_tiles(es, "f2") if l == 0 else None
            pendn = [None]
            for (off, n) in blks:
                k = BLKS.index((off, n))
                j = 2 if k == 0 else jl
                u_t = us.next()
                fw.dma("sp", u_t[:, :, 0:n], uT[bi].rearrange("j p t -> p j t")[:, :, off:off + n], [u_b[bi][k]], [u_t.b], u_t.sem)
                x_t = xs.next()
                fw.dma("sp", x_t[:, :, 0:n], resT[bi].rearrange("c p t -> p c t")[:, :, off:off + n], [res_b[bi][k]], [x_t.b], x_t.sem)
                for oc in range(KC):
                    pst = ps_next()
                    fw.mm(pst[:, 0:n], [(wd[:, jf * 1024 + oc * 128:jf * 1024 + (oc + 1) * 128], u_t[:, jf, 0:n]) for jf in range(FJ)],
                          reads=[wd.b, u_t.b], writes=[pst.b])
                    fw.op("dve", lambda e, pst=pst, oc=oc, x_t=x_t, n=n, j=j: e.scalar_tensor_tensor(
                        out=x_t[:, oc, 0:n], in0=pst[:, 0:n], scalar=mod_ap(l, 5, oc, j), in1=x_t[:, oc, 0:n], op0=ALU.mult, op1=ALU.add),
                        reads=[pst.b, modT.b, x_t.b], writes=[x_t.b])
                if l == 0:
                    fw.dma("sp", resT[bi].rearrange("c p t -> p c t")[:, :, off:off + n], x_t[:, :, 0:n], [x_t.b], [res_b[bi][k]], x_t.sem2, merge=True)
                    if pendn[0] is not None:
                        pendn[0]()
                    pendn[0] = (lambda x_t=x_t, n=n, off=off, k=k, j=j: norm_block(nt, x_t, n, off, k, 1, 0, j))
                else:
                    fw.dma("sp", outT[bi].rearrange("(c p) t -> p c t", p=128)[:, :, off - NCTX:off - NCTX + n], x_t[:, :, 0:n],
                           [x_t.b], [], x_t.sem2, merge=True)
                    if x_t.sem2 not in out_sems:
                        out_sems.append(x_t.sem2)
            if pendn[0] is not None:
                pendn[0]()
            fw.barrier()
        eso.close()
        fw.enabled = True

    def load_wo(l, es):
        wo = T_("wo", [128, KC * 1024], BF16, es=es, dma="pool")
        fw.dma("pool", wo[:], w_out[l], [], [wo.b], wo.sem)
        return wo

    def outproj(l, bi, blks, jl, yT, yT_b, wo):
        with ExitStack() as es:
            xs = Ring(fw, es, "p4x", 2, [128, KC, 512], F32, dma=True, dma2=True)
            nt = norm_tiles(es, "p4")
            pendn = [None]
            for (off, n) in blks:
                k = BLKS.index((off, n))
                j = 2 if k == 0 else jl
                x_t = xs.next()
                src, sb = x_src(l, bi, k)
                fw.dma("sp", x_t[:, :, 0:n], src, [sb] if sb else [], [x_t.b], x_t.sem)
                for oc in range(KC):
                    pst = ps_next()
                    fw.mm(pst[:, 0:n], [(wo[:, kc * 1024 + oc * 128:kc * 1024 + (oc + 1) * 128], yT[:, kc, off:off + n]) for kc in range(KC)],
                          reads=[wo.b] + yT_b, writes=[pst.b])
                    fw.op("dve", lambda e, pst=pst, oc=oc, x_t=x_t, n=n, j=j: e.scalar_tensor_tensor(
                        out=x_t[:, oc, 0:n], in0=pst[:, 0:n], scalar=mod_ap(l, 2, oc, j), in1=x_t[:, oc, 0:n], op0=ALU.mult, op1=ALU.add),
                        reads=[pst.b, modT.b, x_t.b], writes=[x_t.b])
                fw.dma("sp", resT[bi].rearrange("c p t -> p c t")[:, :, off:off + n], x_t[:, :, 0:n], [x_t.b], [res_b[bi][k]], x_t.sem2, merge=True)
                if pendn[0] is not None:
                    pendn[0]()
                pendn[0] = (lambda x_t=x_t, n=n, off=off, k=k, j=j: norm_block(nt, x_t, n, off, k, l, 1, j))
            if pendn[0] is not None:
                pendn[0]()
            fw.barrier()

    def mixer1(bi, jl):
        RT = 2 * 128 + 2 + 5 * 128
        o_pq0, o_pq1, o_kpos, o_dpos, o_dneg, o_lge, o_ule, o_id = 0, 128, 256, 258, 386, 514, 642, 770
        with ExitStack() as esm:
            yT = T_("yT1", [128, KC, T], BF16, es=esm)
            yT_b = [Buf(f"yT1_{c}") for c in range(KC)]
            wch = Ring(fw, esm, "wch1", 3, [128, KC * 128], BF16, dma="pool")
            with ExitStack() as es:
                NE = NLAT + 16
                pv = T_("pv", [128, NE], F32, es=es)
                A = [T_("pA0", [128, NE], F32, es=es), T_("pA1", [128, NE], F32, es=es)]
                ws = T_("pws", [128, NLAT], F32, es=es)
                yp = T_("pyp", [128, NLAT], BF16, es=es)
                pw = T_("ppw", [128, 512], BF16, es=es, dma="pool")
                fw.dma("pool", pw[:], poolw, [], [pw.b], pw.sem)
                pct = T_("pct", [128, 68], F32, es=es, dma=True)
                fw.dma("sp", pct[:, 0:64], pcorr, [], [pct.b], pct.sem, merge=True)
                fw.dma("sp", pct[:, 64:68], pools, [], [pct.b], pct.sem, merge=True)
                fw.op("dve", lambda e: e.memset(pv[:], 0.0), writes=[pv.b])
                for g in range(4):
                    r = (1, 2, 4, 8)[g]
                    w = wch.next()
                    fw.dma("pool", w[:], w_in1[g], [], [w.b], w.sem)
                    for (off, n) in LATBLKS:
                        k = BLKS.index((off, n))
                        pst = ps_next()
                        fw.mm(pst[:, 0:n], [(w[:, kc * 128:(kc + 1) * 128], hT[:, kc, off:off + n]) for kc in range(KC)],
                              reads=[w.b, hT_b[k]], writes=[pst.b])
                        c0 = 8 + off - NCTX
                        fw.op("act", lambda e, pst=pst, c0=c0, n=n: e.activation(out=pv[:, c0:c0 + n], in_=pst[:, 0:n], func=AF.Identity),
                              reads=[pst.b], writes=[pv.b])
                    src = pv
                    width = 1
                    ai = 0
                    while width < 2 * r:
                        dst = A[ai % 2]
                        ai += 1
                        cnt = NE - 2 * width + 1
                        fw.op("dve", lambda e, dst=dst, src=src, cnt=cnt, width=width: e.tensor_tensor(
                            out=dst[:, 0:cnt], in0=src[:, 0:cnt], in1=src[:, width:width + cnt], op=ALU.add),
                            reads=[src.b], writes=[dst.b])
                        src = dst
                        width *= 2
                    fw.op("dve", lambda e, src=src, r=r: e.tensor_tensor(
                        out=ws[:], in0=src[:, 8 - r:8 - r + NLAT], in1=pv[:, 8 + r:8 + r + NLAT], op=ALU.add),
                        reads=[src.b, pv.b], writes=[ws.b])
                    fw.op("dve", lambda e, g=g: e.tensor_tensor(out=ws[:, 0:8], in0=ws[:, 0:8], in1=pct[:, (g * 2) * 8:(g * 2) * 8 + 8], op=ALU.mult),
                          reads=[ws.b, pct.b], writes=[ws.b])
                    fw.op("dve", lambda e, g=g: e.tensor_tensor(out=ws[:, NLAT - 8:NLAT], in0=ws[:, NLAT - 8:NLAT],
                                                                in1=pct[:, (g * 2 + 1) * 8:(g * 2 + 1) * 8 + 8], op=ALU.mult),
                          reads=[ws.b, pct.b], writes=[ws.b])
                    fw.op("dve", lambda e, r=r: e.scalar_tensor_tensor(out=yp[:], in0=ws[:], scalar=1.0 / (2 * r + 1), in1=pv[:, 8:8 + NLAT],
                                                                       op0=ALU.mult, op1=ALU.subtract),
                          reads=[ws.b, pv.b], writes=[yp.b])
                    for (off, n) in LATBLKS:
                        pst = ps_next()
                        fw.mm(pst[:, 0:n], [(pw[:, g * 128:(g + 1) * 128], yp[:, off - NCTX:off - NCTX + n])], reads=[pw.b, yp.b], writes=[pst.b])
                        fw.op("act", lambda e, pst=pst, off=off, n=n, g=g: e.activation(
                            out=yT[:, g, off:off + n], in_=pst[:, 0:n], func=AF.Identity, scale=pct[:, 64 + g:65 + g]),
                            reads=[pst.b, pct.b], writes=[yT_b[g]])
                fw.barrier()
            with ExitStack() as es:
                qT = T_("rqT", [128, 2, NLAT], BF16, es=es)
                qdf = T_("rqdf", [128, 2, NLAT], BF16, es=es)
                qdb = T_("rqdb", [128, 2, NLAT], BF16, es=es)
                kT = T_("rkT", [128, 2, T], BF16, es=es)
                Vtok = T_("rVtok", [128, 18, 512], BF16, es=es)
                Sall = [T_("rSf", [128, 2, 16, 128], BF16, es=es), T_("rSb", [128, 2, 16, 128], BF16, es=es)]
                q_b = [Buf() for _ in range(2)]
                k_b = [Buf() for _ in range(2)]
                rt = T_("rt", [128, RT], F32, es=es, dma=True)
                fw.dma("sp", rt[:], rtab, [], [rt.b], rt.sem)
                dct = T_("dct", [128, 13], F32, es=es, dma=True)
                fw.dma("sp", dct[:, 0:4], decp, [], [dct.b], dct.sem, merge=True)
                fw.dma("sp", dct[:, 4:12], decr, [], [dct.b], dct.sem, merge=True)
                fw.dma("sp", dct[:, 12:13], gng, [], [dct.b], dct.sem, merge=True)
                lg = T_("lg", [128, 12], F32, es=es)
                dk = T_("dk", [128, 12], F32, es=es)
                qdecr = T_("qdecr", [128, 4, 512], F32, es=es)
                maskr = T_("maskr", [128, 4, 512], F32, es=es)
                ident = T_("identb", [128, 128], BF16, es=es)
                fw.op("act", lambda e: e.activation(out=lg[:], in_=dct[:, 0:12], func=AF.Exp), reads=[dct.b], writes=[lg.b])
                fw.op("dve", lambda e: e.tensor_scalar(out=lg[:], in0=lg[:], scalar1=-1.0, scalar2=None, op0=ALU.mult), reads=[lg.b], writes=[lg.b])
                fw.op("dve", lambda e: e.tensor_copy(out=ident[:], in_=rt[:, o_id:o_id + 128]), reads=[rt.b], writes=[ident.b])
                for d_ in range(2):
                    fw.op("act", lambda e, d_=d_: e.activation(out=dk[:, d_ * 4:d_ * 4 + 4], in_=lg[:, 4 + d_ * 4:8 + d_ * 4], func=AF.Exp,
                                                               scale=rt[:, o_kpos + d_:o_kpos + d_ + 1]),
                          reads=[lg.b, rt.b], writes=[dk.b])
                fw.op("act", lambda e: e.activation(out=dk[:, 8:12], in_=lg[:, 0:4], func=AF.Exp, scale=128.0), reads=[lg.b], writes=[dk.b])
                for d_ in range(2):
                    for hp in range(2):
                        for rep in range(4):
                            fw.op("act", lambda e, d_=d_, hp=hp, rep=rep: e.activation(
                                out=qdecr[:, d_ * 2 + hp, rep * 128:(rep + 1) * 128], in_=rt[:, o_pq0 + d_ * 128:o_pq0 + (d_ + 1) * 128],
                                func=AF.Exp, scale=lg[:, d_ * 2 + hp:d_ * 2 + hp + 1]), reads=[rt.b, lg.b], writes=[qdecr.b])
                with ExitStack() as es2:
                    m1 = T_("m1", [128, 128], F32, es=es2)
                    m2 = T_("m2", [128, 128], F32, es=es2)
                    for h in range(4):
                        fw.op("act", lambda e, h=h: e.activation(out=m1[:], in_=rt[:, o_dpos:o_dpos + 128], func=AF.Exp, scale=lg[:, 4 + h:5 + h]),
                              reads=[rt.b, lg.b], writes=[m1.b])
                        fw.op("dve", lambda e: e.tensor_tensor(out=m1[:], in0=m1[:], in1=rt[:, o_lge:o_lge + 128], op=ALU.mult),
                              reads=[m1.b, rt.b], writes=[m1.b])
                        fw.op("act", lambda e, h=h: e.activation(out=m2[:], in_=rt[:, o_dneg:o_dneg + 128], func=AF.Exp, scale=lg[:, 8 + h:9 + h]),
                              reads=[rt.b, lg.b], writes=[m2.b])
                        fw.op("dve", lambda e: e.tensor_tensor(out=m2[:], in0=m2[:], in1=rt[:, o_ule:o_ule + 128], op=ALU.mult),
                              reads=[m2.b, rt.b], writes=[m2.b])
                        for rep in range(4):
                            fw.op("dve", lambda e, h=h, rep=rep: e.tensor_tensor(out=maskr[:, h, rep * 128:(rep + 1) * 128], in0=m1[:], in1=m2[:], op=ALU.add),
                                  reads=[m1.b, m2.b], writes=[maskr.b])
                    fw.barrier()
                with ExitStack() as es2:
                    cosT = T_("rcos", [128, T], F32, es=es2, dma=True)
                    sinT = T_("rsin", [128, T], F32, es=es2, dma=True)
                    fw.dma("sp", cosT[:], ropec, [], [cosT.b], cosT.sem)
                    fw.dma("sp", sinT[:], ropes, [], [sinT.b], sinT.sem)
                    t1r = Ring(fw, es2, "rt1", 2, [128, 512], F32)
                    t2r = Ring(fw, es2, "rt2", 2, [128, 512], F32)
                    for which in ("q", "k"):
                        base = 4 if which == "q" else 8
                        for hp in range(2):
                            wm = wch.next()
                            fw.dma("pool", wm[:], w_in1[base + hp], [], [wm.b], wm.sem)
                            wr = wch.next()
                            fw.dma("pool", wr[:], w_in1[base + 2 + hp], [], [wr.b], wr.sem)
                            for (off, n) in (LATBLKS if which == "q" else BLKS):
                                k = BLKS.index((off, n))
                                p1 = ps_next()
                                fw.mm(p1[:, 0:n], [(wm[:, kc * 128:(kc + 1) * 128], hT[:, kc, off:off + n]) for kc in range(KC)],
                                      reads=[wm.b, hT_b[k]], writes=[p1.b])
                                p2 = ps_next()
                                fw.mm(p2[:, 0:n], [(wr[:, kc * 128:(kc + 1) * 128], hT[:, kc, off:off + n]) for kc in range(KC)],
                                      reads=[wr.b, hT_b[k]], writes=[p2.b])
                                t1 = t1r.next()
                                t2 = t2r.next()
                                sc_ = 0.125 if which == "q" else 1.0
                                fw.op("dve", lambda e, t1=t1, p1=p1, n=n, off=off, sc_=sc_: e.scalar_tensor_tensor(
                                    out=t1[:, 0:n], in0=p1[:, 0:n], scalar=sc_, in1=cosT[:, off:off + n], op0=ALU.mult, op1=ALU.mult),
                                    reads=[p1.b, cosT.b], writes=[t1.b])
                                fw.op("dve", lambda e, t2=t2, p2=p2, n=n, off=off, sc_=sc_: e.scalar_tensor_tensor(
                                    out=t2[:, 0:n], in0=p2[:, 0:n], scalar=sc_, in1=sinT[:, off:off + n], op0=ALU.mult, op1=ALU.mult),
                                    reads=[p2.b, sinT.b], writes=[t2.b])
                                if which == "k":
                                    fw.op("dve", lambda e, t1=t1, t2=t2, n=n, off=off, hp=hp: e.tensor_tensor(
                                        out=kT[:, hp, off:off + n], in0=t1[:, 0:n], in1=t2[:, 0:n], op=ALU.add),
                                        reads=[t1.b, t2.b], writes=[k_b[hp]])
                                else:
                                    lo_ = off - NCTX
                                    fw.op("dve", lambda e, t1=t1, t2=t2, n=n: e.tensor_tensor(out=t1[:, 0:n], in0=t1[:, 0:n], in1=t2[:, 0:n], op=ALU.add),
                                          reads=[t1.b, t2.b], writes=[t1.b])
                                    fw.op("act", lambda e, t1=t1, n=n, lo_=lo_, hp=hp: e.activation(out=qT[:, hp, lo_:lo_ + n], in_=t1[:, 0:n], func=AF.Identity),
                                          reads=[t1.b], writes=[q_b[hp]])
                                    fw.op("dve", lambda e, t1=t1, n=n, lo_=lo_, hp=hp: e.tensor_tensor(
                                        out=qdf[:, hp, lo_:lo_ + n], in0=t1[:, 0:n], in1=qdecr[:, hp, 0:n], op=ALU.mult),
                                        reads=[t1.b, qdecr.b], writes=[q_b[hp]])
                                    fw.op("dve", lambda e, t1=t1, n=n, lo_=lo_, hp=hp: e.tensor_tensor(
                                        out=qdb[:, hp, lo_:lo_ + n], in0=t1[:, 0:n], in1=qdecr[:, 2 + hp, 0:n], op=ALU.mult),
                                        reads=[t1.b, qdecr.b], writes=[q_b[hp]])
                    wv = T_("rwv", [128, KC * 512], BF16, es=es2, dma="pool")
                    fw.dma("pool", wv[:], w_v1, [], [wv.b], wv.sem)
                    for tt_ in range(18):
                        kb = 0 if tt_ < 2 else 1 + (tt_ - 2) // 4
                        pst = ps_next()
                        fw.mm(pst[:, 0:512], [(hT[:, kc, tt_ * 128:(tt_ + 1) * 128], wv[:, kc * 512:(kc + 1) * 512]) for kc in range(KC)],
                              reads=[wv.b, hT_b[kb]], writes=[pst.b])
                        fw.op("act", lambda e, pst=pst, tt_=tt_: e.activation(out=Vtok[:, tt_, :], in_=pst[:, 0:512], func=AF.Identity),
                              reads=[pst.b], writes=[Vtok.b])
                    fw.barrier()
                with ExitStack() as es2:
                    Ktok = [T_("rKf", [128, 18, 256], BF16, es=es2), T_("rKb", [128, 18, 256], BF16, es=es2)]
                    for tt_ in range(18):
                        for hp in range(2):
                            pst = ps_next()
                            fw.mm(pst[:, 0:128], [(kT[:, hp, tt_ * 128:(tt_ + 1) * 128], ident[:])], reads=[k_b[hp], ident.b], writes=[pst.b])
                            for d_ in range(2):
                                for hh in range(2):
                                    h = hp * 2 + hh
                                    eng = "act" if d_ == 0 else "dve"
                                    if eng == "act":
                                        fw.op("act", lambda e, pst=pst, tt_=tt_, h=h, hh=hh, d_=d_: e.activation(
                                            out=Ktok[d_][:, tt_, h * 64:(h + 1) * 64], in_=pst[:, hh * 64:(hh + 1) * 64], func=AF.Identity,
                                            scale=dk[:, d_ * 4 + h:d_ * 4 + h + 1]), reads=[pst.b, dk.b], writes=[Ktok[d_].b])
                                    else:
                                        fw.op("dve", lambda e, pst=pst, tt_=tt_, h=h, hh=hh, d_=d_: e.tensor_scalar(
                                            out=Ktok[d_][:, tt_, h * 64:(h + 1) * 64], in0=pst[:, hh * 64:(hh + 1) * 64],
                                            scalar1=dk[:, d_ * 4 + h:d_ * 4 + h + 1], scalar2=None, op0=ALU.mult),
                                            reads=[pst.b, dk.b], writes=[Ktok[d_].b])
                    St = [[T_(f"rS{d_}{hp}", [128, 128], F32, es=es2) for hp in range(2)] for d_ in range(2)]
                    orders = {0: list(range(18)), 1: [1, 0] + list(range(17, 1, -1))}
                    chains = [(d_, hp) for d_ in range(2) for hp in range(2)]
                    for (d_, hp) in chains:
                        S = St[d_][hp]
                        fw.op("dve", lambda e, S=S: e.memset(S[:], 0.0), writes=[S.b])
                    for step in range(18):
                        for (d_, hp) in chains:
                            S = St[d_][hp]
                            n_ = orders[d_][step]
                            if n_ >= 2:
                                fw.op("act", lambda e, S=S, d_=d_, hp=hp, n_=n_: e.activation(out=Sall[d_][:, hp, n_ - 2, :], in_=S[:], func=AF.Identity),
                                      reads=[S.b], writes=[Sall[d_].b])
                            if step == 17:
                                continue
                            pkv = ps_next()
                            fw.mm(pkv[:, 0:256], [(Ktok[d_][:, n_, hp * 128:(hp + 1) * 128], Vtok[:, n_, hp * 256:(hp + 1) * 256])],
                                  reads=[Ktok[d_].b, Vtok.b], writes=[pkv.b])
                            cd = dk[:, 8 + d_ * 2 + hp:9 + d_ * 2 + hp]
                            fw.op("dve", lambda e, S=S, pkv=pkv, cd=cd: e.scalar_tensor_tensor(
                                out=S[0:64, :], in0=S[0:64, :], scalar=cd[0:64, :], in1=pkv[0:64, 0:128], op0=ALU.mult, op1=ALU.add),
                                reads=[S.b, pkv.b, dk.b], writes=[S.b])
                            fw.op("dve", lambda e, S=S, pkv=pkv, cd=cd: e.scalar_tensor_tensor(
                                out=S[64:128, :], in0=S[64:128, :], scalar=cd[64:128, :], in1=pkv[64:128, 128:256], op0=ALU.mult, op1=ALU.add),
                                reads=[S.b, pkv.b, dk.b], writes=[S.b])
                    fw.barrier()
                with ExitStack() as es2:
                    wg = [T_(f"rwg{h}", [128, KC * 128], BF16, es=es2, dma="pool") for h in range(4)]
                    for h in range(4):
                        fw.dma("pool", wg[h][:], w_in1[12 + h], [], [wg[h].b], wg[h].sem)
                    Abf = Ring(fw, es2, "rAbf", 3, [128, 512], BF16)
                    sqrr = Ring(fw, es2, "rsq", 2, [128, 512], BF16)
                    lnr = T_("rln", [128, 512], F32, es=es2)
                    rsr = T_("rrs", [128, 512], F32, es=es2)
                    yrr = T_("ryr", [128, 512], F32, es=es2)
                    sgr = T_("rsg", [128, 512], F32, es=es2)
                    R3U = [(h, bidx) for h in range(4) for bidx in range(len(LATBLKS))]
                    st3 = {}

                    def s12(u):
                        h, bidx = R3U[u]
                        hp, r0 = h // 2, (h % 2) * 64
                        pa = ps_next()
                        for cc in range(4):
                            c = bidx * 4 + cc
                            n_ = c + 2
                            fw.mm(pa[:, cc * 128:(cc + 1) * 128], [(kT[r0:r0 + 64, hp, n_ * 128:(n_ + 1) * 128], qT[r0:r0 + 64, hp, c * 128:(c + 1) * 128])],
                                  reads=[k_b[hp], q_b[hp]], writes=[pa.b])
                        ab = Abf.next()
                        fw.op("dve", lambda e: e.tensor_tensor(out=ab[:], in0=pa[:], in1=maskr[:, h, :], op=ALU.mult),
                              reads=[pa.b, maskr.b], writes=[ab.b])
                        st3[u] = {"ab": ab}

                    def s34(u):
                        h, bidx = R3U[u]
                        hp, r0 = h // 2, (h % 2) * 64
                        ab = st3[u]["ab"]
                        po = ps_next()
                        for cc in range(4):
                            c = bidx * 4 + cc
                            n_ = c + 2
                            fw.mm(po[:, cc * 128:(cc + 1) * 128],
                                  [(Vtok[:, n_, h * 128:(h + 1) * 128], ab[:, cc * 128:(cc + 1) * 128]),
                                   (Sall[0][r0:r0 + 64, hp, c, :], qdf[r0:r0 + 64, hp, c * 128:(c + 1) * 128]),
                                   (Sall[1][r0:r0 + 64, hp, c, :], qdb[r0:r0 + 64, hp, c * 128:(c + 1) * 128])],
                                  reads=[Vtok.b, ab.b, Sall[0].b, Sall[1].b, q_b[hp]], writes=[po.b])
                        sq = sqrr.next()
                        fw.op("act", lambda e: e.activation(out=sq[:], in_=po[:], func=AF.Square), reads=[po.b], writes=[sq.b])
                        st3[u]["po"] = po
                        st3[u]["sq"] = sq

                    def s56(u):
                        h, bidx = R3U[u]
                        off, n = LATBLKS[bidx]
                        k = BLKS.index((off, n))
                        po, sq = st3[u]["po"], st3[u]["sq"]
                        pm = ps_next()
                        fw.mm(pm[:], [(ones_h[:], sq[:])], reads=[ones_h.b, sq.b], writes=[pm.b])
                        pg = ps_next()
                        fw.mm(pg[:, 0:n], [(wg[h][:, kc * 128:(kc + 1) * 128], hT[:, kc, off:off + n]) for kc in range(KC)],
                              reads=[wg[h].b, hT_b[k]], writes=[pg.b])
                        fw.op("act", lambda e: e.activation(out=lnr[:], in_=pm[:], func=AF.Ln, bias=epsT[:], scale=1.0),
                              reads=[pm.b, epsT.b], writes=[lnr.b])
                        fw.op("act", lambda e: e.activation(out=rsr[:], in_=lnr[:], func=AF.Exp, scale=-0.5), reads=[lnr.b], writes=[rsr.b])
                        fw.op("dve", lambda e: e.scalar_tensor_tensor(out=yrr[:], in0=po[:], scalar=dct[:, 12:13], in1=rsr[:],
                                                                      op0=ALU.mult, op1=ALU.mult),
                              reads=[po.b, dct.b, rsr.b], writes=[yrr.b])
                        fw.op("act", lambda e: e.activation(out=sgr[:], in_=pg[:], func=AF.Silu), reads=[pg.b], writes=[sgr.b])
                        fw.op("dve", lambda e: e.tensor_tensor(out=yT[:, 4 + h, off:off + n], in0=yrr[:], in1=sgr[:], op=ALU.mult),
                              reads=[yrr.b, sgr.b], writes=[yT_b[4 + h]])
                        del st3[u]

                    NU = len(R3U)
                    for it in range(NU + 2):
                        if 0 <= it - 2 < NU:
                            s56(it - 2)
                        if 0 <= it - 1 < NU:
                            s34(it - 1)
                        if it < NU:
                            s12(it)
                    fw.barrier()
            dump_bf16("y1", yT, yT_b, 8, T)
            wo1 = load_wo(1, esm)
            outproj(1, bi, LATBLKS, jl, yT, yT_b, wo1)

    for bi in range(nb):
        jl = bi
        with ExitStack() as es:
            xs = Ring(fw, es, "p1x", 2, [128, KC, 512], F32, dma=True)
            nt = norm_tiles(es, "p1")
            for k, (off, n) in enumerate(BLKS):
                x_t = xs.next()
                src, sb = x_src(0, bi, k)
                fw.dma("sp", x_t[:, :, 0:n], src, [sb] if sb else [], [x_t.b], x_t.sem)
                norm_block(nt, x_t, n, off, k, 0, 0, 2 if k == 0 else jl)
            fw.barrier()
        with ExitStack() as esm:
            yT = T_("yT", [128, KC, T], BF16, es=esm)
            yT_b = [Buf(f"yT{c}") for c in range(KC)]
            wch = Ring(fw, esm, "wch", 3, [128, KC * 128], BF16, dma="pool")
            with ExitStack() as es:
                fw.enabled = on("P2a")
                mbuf = T_("mbuf", [128, PT], F32, es=es)
                cgf = T_("cgf", [128, PT], F32, es=es)
                bgf = T_("bgf", [128, PT], F32, es=es)
                ybuf = T_("ybuf", [128, PT], F32, es=es)
                fw.op("dve", lambda e: e.memset(mbuf[:], 0.0), writes=[mbuf.b])
                for jc in range(4):
                    for typ, c in (("cg", 4 + jc), ("xv", 8 + jc), ("bg", jc)):
                        w = wch.next()
                        fw.dma("pool", w[:], w_in0[c], [], [w.b], w.sem)
                        for k, (off, n) in enumerate(BLKS):
                            po = poff(off)
                            pst = ps_next()
                            fw.mm(pst[:, 0:n], [(w[:, kc * 128:(kc + 1) * 128], hT[:, kc, off:off + n]) for kc in range(KC)],
                                  reads=[w.b, hT_b[k]], writes=[pst.b])
                            if typ == "cg":
                                fw.op("act", lambda e, pst=pst, po=po, n=n: e.activation(out=cgf[:, po:po + n], in_=pst[:, 0:n], func=AF.Identity),
                                      reads=[pst.b], writes=[cgf.b])
                            elif typ == "xv":
                                fw.op("dve", lambda e, pst=pst, po=po, n=n: e.tensor_tensor(out=mbuf[:, po:po + n], in0=pst[:, 0:n], in1=cgf[:, po:po + n], op=ALU.mult),
                                      reads=[pst.b, cgf.b], writes=[mbuf.b])
                            else:
                                fw.op("act", lambda e, pst=pst, po=po, n=n: e.activation(out=bgf[:, po:po + n], in_=pst[:, 0:n], func=AF.Identity),
                                      reads=[pst.b], writes=[bgf.b])
                    wb = o_scw + jc * 3
                    fw.op("dve", lambda e, wb=wb: e.tensor_scalar(out=ybuf[:, 1:PT - 1], in0=mbuf[:, 0:PT - 2], scalar1=C(wb), scalar2=None, op0=ALU.mult),
                          reads=[mbuf.b, smallc.b], writes=[ybuf.b])
                    fw.op("dve", lambda e, wb=wb: e.scalar_tensor_tensor(out=ybuf[:, 1:PT - 1], in0=mbuf[:, 1:PT - 1], scalar=C(wb + 1), in1=ybuf[:, 1:PT - 1],
                                                                         op0=ALU.mult, op1=ALU.add), reads=[mbuf.b, smallc.b, ybuf.b], writes=[ybuf.b])
                    fw.op("dve", lambda e, wb=wb: e.scalar_tensor_tensor(out=ybuf[:, 1:PT - 1], in0=mbuf[:, 2:PT], scalar=C(wb + 2), in1=ybuf[:, 1:PT - 1],
                                                                         op0=ALU.mult, op1=ALU.add), reads=[mbuf.b, smallc.b, ybuf.b], writes=[ybuf.b])
                    fw.op("dve", lambda e, jc=jc: e.tensor_tensor(out=yT[:, jc, 0:NCTX], in0=ybuf[:, 1:1 + NCTX], in1=bgf[:, 1:1 + NCTX], op=ALU.mult),
                          reads=[ybuf.b, bgf.b], writes=[yT_b[jc]])
                    fw.op("dve", lambda e, jc=jc: e.tensor_tensor(out=yT[:, jc, NCTX:T], in0=ybuf[:, NCTX + 2:PT - 1], in1=bgf[:, NCTX + 2:PT - 1], op=ALU.mult),
                          reads=[ybuf.b, bgf.b], writes=[yT_b[jc]])
                fw.barrier()
            fw.enabled = on("P4")
            wo0 = load_wo(0, esm)
            with ExitStack() as es:
                fw.enabled = on("P2b")
                qT = T_("qT", [128, 4, T], BF16, es=es)
                kT = T_("kT", [128, 4, T], BF16, es=es)
                qT_b = [Buf() for _ in range(4)]
                kT_b = [Buf() for _ in range(4)]
                Vtok = T_("Vtok", [128, 18, 512], BF16, es=es)
                with ExitStack() as es2:
                    cosT = T_("cosT", [128, T], F32, es=es2, dma=True)
                    sinT = T_("sinT", [128, T], F32, es=es2, dma=True)
                    fw.dma("sp", cosT[:], ropec, [], [cosT.b], cosT.sem)
                    fw.dma("sp", sinT[:], ropes, [], [sinT.b], sinT.sem)
                    sqk = Ring(fw, es2, "sqk", 2, [128, 512], BF16)
                    lnk = Ring(fw, es2, "lnk", 2, [128, 512], F32)
                    rsk = Ring(fw, es2, "rsk", 2, [128, 512], F32)
                    t1r = Ring(fw, es2, "t1r", 2, [128, 512], F32)
                    t2r = Ring(fw, es2, "t2r", 2, [128, 512], F32)
                    for (cm, cr, gcol, dst, dst_b) in (() if "P2bqk" in skip else ((12, 16, o_qkg, qT, qT_b), (20, 24, o_qkg + 2, kT, kT_b))):
                        for h in range(4):
                            wm = wch.next()
                            fw.dma("pool", wm[:], w_in0[cm + h], [], [wm.b], wm.sem)
                            wr = wch.next()
                            fw.dma("pool", wr[:], w_in0[cr + h], [], [wr.b], wr.sem)
                            def stage_a(k, off, n, wm=wm, wr=wr):
                                p1 = ps_next()
                                fw.mm(p1[:, 0:n], [(wm[:, kc * 128:(kc + 1) * 128], hT[:, kc, off:off + n]) for kc in range(KC)],
                                      reads=[wm.b, hT_b[k]], writes=[p1.b])
                                p2 = ps_next()
                                fw.mm(p2[:, 0:n], [(wr[:, kc * 128:(kc + 1) * 128], hT[:, kc, off:off + n]) for kc in range(KC)],
                                      reads=[wr.b, hT_b[k]], writes=[p2.b])
                                sq = sqk.next()
                                fw.op("act", lambda e: e.activation(out=sq[:, 0:n], in_=p1[:, 0:n], func=AF.Square),
                                      reads=[p1.b], writes=[sq.b])
                                return p1, p2, sq

                            def stage_b(k, off, n, p1, p2, sq, h=h, gcol=gcol, dst=dst, dst_b=dst_b):
                                p3 = ps_next()
                                fw.mm(p3[:, 0:n], [(ones_g[:], sq[:, 0:n])], reads=[ones_g.b, sq.b], writes=[p3.b])
                                ln = lnk.next()
                                rs = rsk.next()
                                fw.op("act", lambda e: e.activation(out=ln[:, 0:n], in_=p3[:, 0:n], func=AF.Ln, bias=epsT[:], scale=1.0),
                                      reads=[p3.b, epsT.b], writes=[ln.b])
                                fw.op("act", lambda e: e.activation(out=rs[:, 0:n], in_=ln[:, 0:n], func=AF.Exp, scale=-0.5),
                                      reads=[ln.b], writes=[rs.b])
                                t1 = t1r.next()
                                t2 = t2r.next()
                                fw.op("dve", lambda e: e.scalar_tensor_tensor(
                                    out=t1[:, 0:n], in0=p1[:, 0:n], scalar=C(gcol), in1=cosT[:, off:off + n], op0=ALU.mult, op1=ALU.mult),
                                    reads=[p1.b, smallc.b, cosT.b], writes=[t1.b])
                                fw.op("dve", lambda e: e.scalar_tensor_tensor(
                                    out=t2[:, 0:n], in0=p2[:, 0:n], scalar=C(gcol + 1), in1=sinT[:, off:off + n], op0=ALU.mult, op1=ALU.mult),
                                    reads=[p2.b, smallc.b, sinT.b], writes=[t2.b])
                                fw.op("dve", lambda e: e.tensor_tensor(out=t1[:, 0:n], in0=t1[:, 0:n], in1=t2[:, 0:n], op=ALU.add),
                                      reads=[t1.b, t2.b], writes=[t1.b])
                                fw.op("dve", lambda e: e.tensor_tensor(
                                    out=dst[:, h, off:off + n], in0=t1[:, 0:n], in1=rs[:, 0:n], op=ALU.mult),
                                    reads=[t1.b, rs.b], writes=[dst_b[h]])

                            prev = stage_a(0, *BLKS[0])
                            for k, (off, n) in enumerate(BLKS):
                                nxt = stage_a(k + 1, *BLKS[k + 1]) if k + 1 < len(BLKS) else None
                                stage_b(k, off, n, *prev)
                                prev = nxt
                    wv = T_("wv", [128, KC * 512], BF16, es=es2, dma="pool")
                    fw.dma("pool", wv[:], w_v0, [], [wv.b], wv.sem)
                    for tt_ in (() if "P2bv" in skip else range(18)):
                        kb = 0 if tt_ < 2 else 1 + (tt_ - 2) // 4
                        pst = ps_next()
                        fw.mm(pst[:, 0:512], [(hT[:, kc, tt_ * 128:(tt_ + 1) * 128], wv[:, kc * 512:(kc + 1) * 512]) for kc in range(KC)],
                              reads=[wv.b, hT_b[kb]], writes=[pst.b])
                        fw.op("act", lambda e, pst=pst, tt_=tt_: e.activation(out=Vtok[:, tt_, :], in_=pst[:, 0:512], func=AF.Identity),
                              reads=[pst.b], writes=[Vtok.b])
                    fw.barrier()
                dump_bf16("qT", qT, qT_b, 4, T)
                dump_bf16("kT", kT, kT_b, 4, T)
                with ExitStack() as es2:
                    fw.enabled = on("P3")
                    Er = Ring(fw, es2, "Er", 8, [128, 512], BF16)
                    r1 = T_("r1", [128, 512], F32, es=es2)
                    r2 = T_("r2", [128, 512], F32, es=es2)
                    o1 = T_("o1", [128, 512], F32, es=es2)
                    o2 = T_("o2", [128, 512], F32, es=es2)
                    osq = T_("osq", [128, 512], BF16, es=es2)
                    oln = T_("oln", [128, 512], F32, es=es2)
                    ors = T_("ors", [128, 512], F32, es=es2)
                    STP = [(PS[0], PS[1]), (PS[6], PS[7])]
                    stc = [0]
                    pend = [None]
                    accE = [T_("accE0", [128, 512], F32, es=es2), T_("accE1", [128, 512], F32, es=es2)]
                    ones_f = T_("ones_f", [128, 128], F32, es=es2)
                    fw.op("dve", lambda e: e.memset(ones_f[:], 1.0), writes=[ones_f.b])

                    def epi_a(n):
                        for m in range(2):
                            fw.mm(PS[4 + m][:, 0:n], [(ones_f[:], accE[m][:, 0:n])], reads=[ones_f.b, accE[m].b], writes=[PS[4 + m].b])
                        fw.op("dve", lambda e: e.reciprocal(out=r1[:, 0:n], in_=PS[4][:, 0:n]), reads=[PS[4].b], writes=[r1.b])
                        fw.op("dve", lambda e: e.reciprocal(out=r2[:, 0:n], in_=PS[5][:, 0:n]), reads=[PS[5].b], writes=[r2.b])
                        fw.op("dve", lambda e: e.tensor_tensor(out=o1[:, 0:n], in0=PS[2][:, 0:n], in1=r1[:, 0:n], op=ALU.mult),
                              reads=[PS[2].b, r1.b], writes=[o1.b])
                        fw.op("dve", lambda e: e.tensor_tensor(out=o2[:, 0:n], in0=PS[3][:, 0:n], in1=r2[:, 0:n], op=ALU.mult),
                              reads=[PS[3].b, r2.b], writes=[o2.b])
                        fw.op("dve", lambda e: e.scalar_tensor_tensor(out=o1[:, 0:n], in0=o2[:, 0:n], scalar=NEGLAM, in1=o1[:, 0:n],
                                                                      op0=ALU.mult, op1=ALU.add),
                              reads=[o1.b, o2.b, lam_t.b], writes=[o1.b])
                        fw.op("act", lambda e: e.activation(out=osq[:, 0:n], in_=o1[:, 0:n], func=AF.Square), reads=[o1.b], writes=[osq.b])

                    def epi_b(h, off, n):
                        fw.mm(PS[4][:, 0:n], [(ones_h[:], osq[:, 0:n])], reads=[ones_h.b, osq.b], writes=[PS[4].b])
                        fw.op("act", lambda e: e.activation(out=oln[:, 0:n], in_=PS[4][:, 0:n], func=AF.Ln, bias=epsT[:], scale=1.0),
                              reads=[PS[4].b, epsT.b], writes=[oln.b])
                        fw.op("act", lambda e: e.activation(out=ors[:, 0:n], in_=oln[:, 0:n], func=AF.Exp, scale=-0.5),
                              reads=[oln.b], writes=[ors.b])
                        fw.op("dve", lambda e: e.scalar_tensor_tensor(
                            out=yT[:, 4 + h, off:off + n], in0=o1[:, 0:n], scalar=SUBG, in1=ors[:, 0:n], op0=ALU.mult, op1=ALU.mult),
                            reads=[o1.b, lam_t.b, ors.b], writes=[yT_b[4 + h]])

                    for h in range(4):
                        for k, (off, n) in enumerate(BLKS):
                            kts = [0, 1] if k == 0 else list(range(18))
                            sts = {}

                            def emit_st(i, h=h, off=off, n=n, kts=kts, sts=sts):
                                kt = kts[i]
                                pair = STP[stc[0] % 2]
                                stc[0] += 1
                                for m in range(2):
                                    fw.mm(pair[m][:, 0:n], [(kT[m * 64:(m + 1) * 64, h, kt * 128:(kt + 1) * 128], qT[m * 64:(m + 1) * 64, h, off:off + n])],
                                          reads=[kT_b[h], qT_b[h]], writes=[pair[m].b])
                                sts[i] = pair

                            emit_st(0)
                            trig = min(4, len(kts) - 1)
                            for i, kt in enumerate(kts):
                                if i + 1 < len(kts):
                                    emit_st(i + 1)
                                pair = sts.pop(i)
                                ees = []
                                for m in range(2):
                                    ee = Er.next()
                                    fw.op("act", lambda e, ee=ee, st=pair[m], n=n: e.activation(out=ee[:, 0:n], in_=st[:, 0:n], func=AF.Exp, scale=0.125),
                                          reads=[pair[m].b], writes=[ee.b])
                                    ees.append(ee)
                                for m in range(2):
                                    fw.mm(PS[2 + m][:, 0:n], [(Vtok[:, kt, h * 128:(h + 1) * 128], ees[m][:, 0:n])],
                                          reads=[Vtok.b, ees[m].b], writes=[PS[2 + m].b], start=(i == 0), stop=(i == len(kts) - 1))
                                for m in range(2):
                                    eng_ = "dve" if m == 0 else "pool"
                                    if i == 0:
                                        fw.op(eng_, lambda e, m=m, ee=ees[m], n=n: e.tensor_copy(out=accE[m][:, 0:n], in_=ee[:, 0:n]),
                                              reads=[ees[m].b], writes=[accE[m].b])
                                    else:
                                        fw.op(eng_, lambda e, m=m, ee=ees[m], n=n: e.tensor_tensor(out=accE[m][:, 0:n], in0=accE[m][:, 0:n], in1=ee[:, 0:n], op=ALU.add),
                                              reads=[ees[m].b, accE[m].b], writes=[accE[m].b])
                                if i == trig and pend[0] is not None:
                                    pend[0]()
                                    pend[0] = None
                            if pend[0] is not None:
                                pend[0]()
                                pend[0] = None
                            epi_a(n)
                            pend[0] = (lambda h=h, off=off, n=n: epi_b(h, off, n))
                    if pend[0] is not None:
                        pend[0]()
                        pend[0] = None
                    fw.barrier()
            dump_bf16("yT", yT, yT_b, 8, T)
            fw.enabled = on("P4")
            outproj(0, bi, BLKS, jl, yT, yT_b, wo0)
        fw.enabled = True
        dump_res("xmid0", bi)
        ffn(0, bi, BLKS, jl)
        dump_res("xout0", bi)
        if nlayers == 1:
            continue
        mixer1(bi, jl)
        dump_res("xmid1", bi)
        ffn(1, bi, LATBLKS, jl)

    return finish()


def _chunk_cols(w, cols):
    sel = w[:, cols]
    n = sel.shape[1] // 128
    t = sel.reshape(KC, 128, n, 128).transpose(2, 1, 0, 3)
    return np.ascontiguousarray(t.reshape(n, 128, KC * 128))


def _rope_partner(d):
    return d + 16 if (d % 32) < 16 else d - 16


def _rope_tables():
    inv = (10000.0 ** (-np.arange(16, dtype=np.float32) / 16)).astype(np.float32)
    t = np.arange(NLAT)
    row = (t // 64).astype(np.float32)
    col = (t % 64).astype(np.float32)
    c = np.ones((128, T), np.float32)
    s = np.zeros((128, T), np.float32)
    for p in range(128):
        d = p % 64
        axis, half, f = d // 32, (d % 32) // 16, d % 16
        ang = (row if axis == 0 else col) * inv[f]
        c[p, NCTX:] = np.cos(ang.astype(np.float32))
        sn = np.sin(ang.astype(np.float32))
        s[p, NCTX:] = -sn if half == 0 else sn
    return c, s


def _ret_tables():
    i = np.arange(128, dtype=np.float32)
    posq = np.concatenate([np.tile(i + 1.0, (128, 1)), np.tile(128.0 - i, (128, 1))], axis=1)
    kpos = np.stack([127.0 - i, i], axis=1)
    jj = i[:, None]
    ii = i[None, :]
    dpos = np.maximum(ii - jj, 0.0)
    dneg = np.maximum(jj - ii, 0.0)
    lge = (ii >= jj).astype(np.float32)
    ule = (jj >= ii).astype(np.float32)
    return np.ascontiguousarray(np.concatenate([posq, kpos, dpos, dneg, lge, ule, np.eye(128, dtype=np.float32)], axis=1).astype(np.float32))


def _pool_corr():
    out = np.ones((128, 4, 2, 8), np.float32)
    for g, r in enumerate((1, 2, 4, 8)):
        for t in range(r):
            cnt = (t + r) - 0 + 1
            out[:, g, 0, t] = (2 * r + 1) / cnt
            out[:, g, 1, 7 - t] = (2 * r + 1) / cnt
    return out.reshape(128, 64)


def prepare_inputs(inputs):
    f = lambda a: np.ascontiguousarray(np.asarray(a, dtype=np.float32))
    x, c, ctx, c_ctx = f(inputs["x"]), f(inputs["c"]), f(inputs["ctx"]), f(inputs["c_ctx"])
    sh = {}
    mod_w = f(inputs["mod_w"])
    sh["modw"] = np.ascontiguousarray(
        mod_w.reshape(2, KC, 128, 48, 128).transpose(0, 3, 2, 1, 4).reshape(2, 48, 128, KC * 128))
    sh["modb"] = np.ascontiguousarray(f(inputs["mod_b"]).reshape(2, 48, 128).transpose(2, 0, 1).reshape(128, 96))
    n1, n2 = f(inputs["norm1_g"]), f(inputs["norm2_g"])
    ngs = np.stack([n1, n2], axis=1)
    sh["ng"] = np.ascontiguousarray(ngs.reshape(2, 2, KC, 128).transpose(3, 0, 1, 2).reshape(128, 32))
    w0 = f(inputs["ev_w_in"])[0]
    partner = np.array([_rope_partner(d) for d in range(64)])
    cols = list(range(0, 1536))
    qcols = np.arange(1536, 2048)
    kcols = np.arange(2048, 2560)
    perm512 = np.concatenate([g * 64 + partner for g in range(8)])
    allcols = np.concatenate([np.arange(0, 1536), qcols, qcols[0] + perm512, kcols, kcols[0] + perm512])
    sh["w_in0"] = _chunk_cols(w0, allcols)
    sh["w_v0"] = np.ascontiguousarray(w0[:, 2560:3072].reshape(KC, 128, 512).transpose(1, 0, 2).reshape(128, KC * 512))
    wo = np.stack([f(inputs["ev_w_out"])[0], f(inputs["od_w_out"])[0]])
    sh["w_out"] = np.ascontiguousarray(wo.reshape(2, KC, 128, 1024).transpose(0, 2, 1, 3).reshape(2, 128, KC * 1024))
    sh["scw"] = np.ascontiguousarray(f(inputs["sc_conv_w"])[0].reshape(3, 4, 128).transpose(2, 1, 0).reshape(128, 12))
    gq, gk = f(inputs["da_q_norm"])[0], f(inputs["da_k_norm"])[0]
    pidx = np.arange(128) % 64
    sh["qkg"] = np.ascontiguousarray(np.stack([gq[pidx], gq[partner[pidx]], gk[pidx], gk[partner[pidx]]], axis=1))
    lv = np.stack([f(inputs["da_lq1"])[0], f(inputs["da_lk1"])[0], f(inputs["da_lq2"])[0], f(inputs["da_lk2"])[0]])
    sh["lamv"] = np.ascontiguousarray(np.broadcast_to(lv.reshape(1, 256), (128, 256)))
    sh["subg"] = np.ascontiguousarray(f(inputs["da_subln_g"])[0].reshape(128, 1))
    rc, rs = _rope_tables()
    sh["ropec"], sh["ropes"] = rc, rs
    wu = f(inputs["ffn_w_up"])
    wu = wu.reshape(2, KC, 128, 2, FJ, 128).transpose(0, 4, 2, 3, 1, 5)
    sh["w_up"] = np.ascontiguousarray(wu.reshape(2, FJ, 128, 2 * KC * 128))
    wd = f(inputs["ffn_w_down"])
    sh["w_dn"] = np.ascontiguousarray(wd.reshape(2, FJ, 128, 1024))
    sh["fcw"] = np.ascontiguousarray(f(inputs["ffn_conv_w"]).reshape(2, 3, FJ, 128).transpose(3, 0, 2, 1).reshape(128, 132))
    sh["fcb"] = np.ascontiguousarray(f(inputs["ffn_conv_b"]).reshape(2, FJ, 128).transpose(2, 0, 1).reshape(128, 44))
    w1 = f(inputs["od_w_in"])[0]
    partner256 = np.concatenate([g * 64 + partner for g in range(4)])
    cols1 = np.concatenate([np.arange(0, 512), np.arange(512, 768), 512 + partner256, np.arange(768, 1024),
                            768 + partner256, np.arange(1536, 2048)])
    sh["w_in1"] = _chunk_cols(w1, cols1)
    sh["w_v1"] = np.ascontiguousarray(w1[:, 1024:1536].reshape(KC, 128, 512).transpose(1, 0, 2).reshape(128, KC * 512))
    sh["poolw"] = np.ascontiguousarray(f(inputs["pool_w"])[0].transpose(1, 0, 2).reshape(128, 512))
    sh["pools"] = np.ascontiguousarray(f(inputs["pool_scale"])[0].reshape(4, 128).T)
    df, db = f(inputs["ret_decay_f"])[0], f(inputs["ret_decay_b"])[0]
    hp_of = np.arange(128) // 64
    decp = np.zeros((128, 2, 2), np.float32)
    for hp in range(2):
        decp[:, 0, hp] = df[2 * hp + hp_of]
        decp[:, 1, hp] = db[2 * hp + hp_of]
    sh["decp"] = decp.reshape(128, 4)
    sh["decr"] = np.ascontiguousarray(np.broadcast_to(np.concatenate([df, db]).reshape(1, 8), (128, 8)))
    sh["gng"] = np.ascontiguousarray(f(inputs["ret_gn_g"])[0].reshape(128, 1))
    sh["rtab"] = _ret_tables()
    sh["pcorr"] = _pool_corr()
    per_core = []
    for i in range(8):
        b0 = 2 * i
        d = dict(sh)
        d["xT"] = np.ascontiguousarray(x[b0:b0 + 2].transpose(0, 2, 1))
        d["ctxT"] = np.ascontiguousarray(ctx[b0:b0 + 2].transpose(0, 2, 1))
        cs = np.stack([c[b0], c[b0 + 1], c_ctx])
        d["cT"] = np.ascontiguousarray(cs.reshape(3, KC, 128).transpose(2, 1, 0).reshape(128, KC * 3))
        per_core.append(d)
    return per_core


def kernel(**inputs):
    per_core = prepare_inputs(inputs)
    nc = build_program()
    res = run_bass_kernel_spmd(nc, per_core, core_ids=list(range(8)))
    out = np.empty((16, NLAT, D), np.float32)
    for i in range(8):
        o = res.results[i]["outT"]
        out[2 * i:2 * i + 2] = o.transpose(0, 2, 1)
    return out
```

```python
import math
from contextlib import ExitStack

import numpy as np
import concourse.bass as bass
import concourse.mybir as mybir
from concourse.bass_utils import run_bass_kernel_spmd

F32 = mybir.dt.float32
BF16 = mybir.dt.bfloat16
AF = mybir.ActivationFunctionType
ALU = mybir.AluOpType

D = 1024
KC = 8
NCTX = 256
NLAT = 2048
T = NCTX + NLAT
PT = T + 3
BLKS = [(0, 256), (256, 512), (768, 512), (1280, 512), (1792, 512)]
LATBLKS = BLKS[1:]
FF = 2816
FJ = 22
EPS = 1e-6
NB = 2


def poff(off):
    return off + 1 if off < NCTX else off + 2


class Sem:
    def __init__(self, nc, name):
        self.h = nc.alloc_semaphore(name=name)
        self.cnt = 0
        self.name = name


class Buf:
    def __init__(self, name="", excl=False):
        self.w = {}
        self.r = {}
        self.name = name
        self.excl = excl


class Eng:
    def __init__(self, nc, name, obj, is_pe=False):
        self.obj = obj
        self.sem = Sem(nc, "e_" + name)
        self.waited = {}
        self.is_pe = is_pe
        self.name = name


class FW:
    def __init__(self, nc):
        self.nc = nc
        self.E = {
            "pe": Eng(nc, "pe", nc.tensor, True),
            "act": Eng(nc, "act", nc.scalar),
            "dve": Eng(nc, "dve", nc.vector),
            "pool": Eng(nc, "pool", nc.gpsimd),
            "sp": Eng(nc, "sp", nc.sync),
        }
        self.enabled = True
        self.dsems = []
        self.free_sems = {"sp": [], "pool": []}
        self.nsem = 0

    def dsem(self, q="sp"):
        if self.free_sems[q]:
            return self.free_sems[q].pop()
        self.nsem += 1
        x = Sem(self.nc, f"d{q}{self.nsem}")
        x.q = q
        self.dsems.append(x)
        return x

    def _wait(self, E, reads, writes, skip_sem=None):
        deps = {}
        for b in reads:
            for k, v in b.w.items():
                if v > deps.get(k, 0):
                    deps[k] = v
            if b.excl:
                for k, v in b.r.items():
                    if k is not E.sem and v > deps.get(k, 0):
                        deps[k] = v
        for b in writes:
            for k, v in b.w.items():
                if v > deps.get(k, 0):
                    deps[k] = v
            for k, v in b.r.items():
                if v > deps.get(k, 0):
                    deps[k] = v
        for k, v in deps.items():
            if k is skip_sem:
                continue
            if E.is_pe and k is E.sem:
                continue
            if E.waited.get(k, 0) >= v:
                continue
            E.obj.wait_ge(k.h, v)
            E.waited[k] = v

    def op(self, eng, fn, reads=(), writes=()):
        if not self.enabled:
            return None
        E = self.E[eng]
        self._wait(E, reads, writes)
        ins = fn(E.obj)
        E.sem.cnt += 1
        ins.then_inc(E.sem.h, 1)
        c = E.sem.cnt
        for b in reads:
            b.r[E.sem] = c
        for b in writes:
            b.w = {E.sem: c}
            b.r = {}
        return ins

    def mm(self, out_ap, pairs, reads, writes, start=True, stop=True):
        if not self.enabled:
            return
        E = self.E["pe"]
        self._wait(E, reads, writes)
        n = len(pairs)
        ins = None
        for i, (l, r) in enumerate(pairs):
            ins = E.obj.matmul(out_ap, l, r, start=(start and i == 0), stop=(stop and i == n - 1))
        E.sem.cnt += 1
        ins.then_inc(E.sem.h, 1)
        c = E.sem.cnt
        for b in reads:
            b.r[E.sem] = c
        for b in writes:
            b.w = {E.sem: c}
            b.r = {}

    def dma(self, q, out_ap, in_ap, reads, writes, sem, merge=False):
        if not self.enabled:
            return
        E = self.E[q]
        assert getattr(sem, "q", q) == q, (sem.name, q)
        self._wait(E, reads, writes, skip_sem=sem if merge else None)
        ins = E.obj.dma_start(out=out_ap, in_=in_ap)
        sem.cnt += 16
        ins.then_inc(sem.h, 16)
        for b in reads:
            b.r[sem] = sem.cnt
        for b in writes:
            if merge:
                b.w[sem] = sem.cnt
            else:
                b.w = {sem: sem.cnt}
                b.r = {}

    def barrier(self):
        allsems = [E.sem for E in self.E.values()] + self.dsems
        for E in self.E.values():
            for k in allsems:
                if k is E.sem:
                    continue
                if k.cnt > E.waited.get(k, 0):
                    E.obj.wait_ge(k.h, k.cnt)
                    E.waited[k] = k.cnt

    def final_wait(self, sems):
        E = self.E["sp"]
        for k in sems:
            if k.cnt > E.waited.get(k, 0):
                E.obj.wait_ge(k.h, k.cnt)
                E.waited[k] = k.cnt


class Tile:
    def __init__(self, fw, es, name, shape, dt, psum=False, dma=False, dma2=False):
        nc = fw.nc
        fw.ntile = getattr(fw, "ntile", 0) + 1
        name = f"{name}_{fw.ntile}"
        if psum:
            self.t = es.enter_context(nc.psum_tensor(name, shape, dt))
        else:
            self.t = es.enter_context(nc.sbuf_tensor(name, shape, dt))
        self.b = Buf(name, excl=psum)
        self.sem = None
        self.sem2 = None
        if dma:
            q = dma if isinstance(dma, str) else "sp"
            self.sem = fw.dsem(q)
            es.callback(fw.free_sems[q].append, self.sem)
        if dma2:
            self.sem2 = fw.dsem("sp")
            es.callback(fw.free_sems["sp"].append, self.sem2)

    def __getitem__(self, idx):
        return self.t[idx]


class Ring:
    def __init__(self, fw, es, name, n, shape, dt, psum=False, dma=False, dma2=False):
        self.tiles = [Tile(fw, es, f"{name}{i}", shape, dt, psum=psum, dma=dma, dma2=dma2) for i in range(n)]
        self.i = 0

    def next(self):
        t = self.tiles[self.i % len(self.tiles)]
        self.i += 1
        return t


def build_program(cfg=None):
    cfg = cfg or {}
    nlayers = cfg.get("nlayers", 2)
    nb = cfg.get("nb", NB)
    stop_after = cfg.get("stop_after", None)
    only = cfg.get("only", None)

    def on(ph):
        return only is None or ph in only
    skip = set(cfg.get("skip", []))
    nc = bass.Bass("TRN2", target_bir_lowering=False)
    fw = FW(nc)
    ES = ExitStack()

    def din(name, shape, dt=F32):
        return nc.dram_tensor(name, list(shape), dt, kind="ExternalInput").ap()

    xT = din("xT", [NB, D, NLAT])
    ctxT = din("ctxT", [NB, D, NCTX])
    cT = din("cT", [128, KC * 3])
    modw = din("modw", [2, 48, 128, KC * 128])
    modb = din("modb", [128, 2 * 48])
    ng = din("ng", [128, 2 * 2 * KC])
    w_in0 = din("w_in0", [28, 128, KC * 128])
    w_v0 = din("w_v0", [128, KC * 512])
    w_out = din("w_out", [2, 128, KC * 1024])
    scw = din("scw", [128, 4 * 3])
    qkg = din("qkg", [128, 4])
    lamv = din("lamv", [128, 4 * 64])
    subg = din("subg", [128, 1])
    ropec = din("ropec", [128, T])
    ropes = din("ropes", [128, T])
    w_up = din("w_up", [2, FJ, 128, 2 * KC * 128])
    w_dn = din("w_dn", [2, FJ, 128, 1024])
    fcw = din("fcw", [128, 2 * FJ * 3])
    fcb = din("fcb", [128, 2 * FJ])
    w_in1 = din("w_in1", [16, 128, KC * 128])
    w_v1 = din("w_v1", [128, KC * 512])
    poolw = din("poolw", [128, 4 * 128])
    pools = din("pools", [128, 4])
    decp = din("decp", [128, 4])
    decr = din("decr", [128, 8])
    gng = din("gng", [128, 1])
    rtab = din("rtab", [128, 2 * 128 + 2 + 5 * 128])
    pcorr = din("pcorr", [128, 4 * 2 * 8])
    outT = nc.dram_tensor("outT", [NB, D, NLAT], F32, kind="ExternalOutput").ap()
    dbg = None
    if cfg.get("dbg"):
        dbg = nc.dram_tensor("dbg", list(cfg["dbg"]), F32, kind="ExternalOutput").ap()
    resT = nc.dram_tensor("resT", [NB, KC, 128, T], F32).ap()
    uT = nc.dram_tensor("uT", [NB, FJ, 128, T], BF16).ap()
    res_b = [[Buf(f"res{bi}_{k}") for k in range(len(BLKS))] for bi in range(NB)]
    u_b = [[Buf(f"u{bi}_{k}") for k in range(len(BLKS))] for bi in range(NB)]
    out_sems = []

    def T_(name, shape, dt, es=ES, **kw):
        return Tile(fw, es, name, shape, dt, **kw)

    PS = [T_(f"ps{i}", [128, 512], F32, psum=True) for i in range(8)]
    psi = [0]

    def ps_next():
        t = PS[psi[0] % 8]
        psi[0] += 1
        return t

    hT = T_("hT", [128, KC, T], BF16)
    hT_b = [Buf(f"hT{k}") for k in range(len(BLKS))]
    ones_n = T_("ones_n", [128, 128], BF16)
    ones_h = T_("ones_h", [128, 128], BF16)
    ones_1 = T_("ones_1", [128, 128], BF16)
    ones_g = T_("ones_g", [128, 128], BF16)
    epsT = T_("epsT", [128, 1], F32)
    cst = T_("cst", [128, KC * 3], F32, dma=True)
    sct = T_("sct", [128, KC * 3], F32)
    modT = T_("modT", [128, 2 * 48 * 3], F32)
    modbT = T_("modbT", [128, 96], F32, dma=True)
    ngT = T_("ngT", [128, 32], F32, dma=True)
    AT = T_("AT", [128, 2 * 2 * KC * 3], F32)
    smallc = T_("smallc", [128, 12 + 4 + 256 + 1 + 132 + 44], F32, dma=True)
    o_scw, o_qkg, o_lamv, o_subg, o_fcw, o_fcb = 0, 12, 16, 272, 273, 405
    lam_t = T_("lam_t", [128, 8], F32)

    def C(o, n=1):
        return smallc[:, o:o + n]

    def mod_ap(l, grp, kc, j):
        i = (l * 48 + grp * 8 + kc) * 3 + j
        return modT[:, i:i + 1]

    def A_ap(l, n, kc, j):
        i = ((l * 2 + n) * KC + kc) * 3 + j
        return AT[:, i:i + 1]

    fw.op("dve", lambda e: e.memset(ones_n[:], 1.0 / 1024.0), writes=[ones_n.b])
    fw.op("dve", lambda e: e.memset(ones_h[:], 1.0 / 128.0), writes=[ones_h.b])
    fw.op("dve", lambda e: e.memset(ones_1[:], 1.0), writes=[ones_1.b])
    fw.op("dve", lambda e: e.memset(ones_g[:], 0.0), writes=[ones_g.b])
    fw.op("dve", lambda e: e.memset(ones_g[0:64, 0:64], 1.0 / 64.0), writes=[ones_g.b])
    fw.op("dve", lambda e: e.memset(ones_g[64:128, 64:128], 1.0 / 64.0), writes=[ones_g.b])
    fw.op("dve", lambda e: e.memset(epsT[:], EPS), writes=[epsT.b])
    fw.dma("sp", cst[:], cT, [], [cst.b], cst.sem)
    fw.dma("sp", modbT[:], modb, [], [modbT.b], modbT.sem)
    fw.dma("sp", ngT[:], ng, [], [ngT.b], ngT.sem)
    for (o, src, n) in [(o_scw, scw, 12), (o_qkg, qkg, 4), (o_lamv, lamv, 256), (o_subg, subg, 1),
                        (o_fcw, fcw, 132), (o_fcb, fcb, 44)]:
        fw.dma("sp", smallc[:, o:o + n], src, [], [smallc.b], smallc.sem, merge=True)

    fw.op("act", lambda e: e.activation(out=sct[:], in_=cst[:], func=AF.Silu), reads=[cst.b], writes=[sct.b])
    with ExitStack() as es:
        mw = Ring(fw, es, "mw", 8, [128, KC * 128], F32, dma=True)
        for l in range(2):
            pst = ps_next()
            for oc in range(48):
                w = mw.next()
                fw.dma("sp", w[:], modw[l, oc], [], [w.b], w.sem)
                fw.mm(pst[:, oc * 3:oc * 3 + 3],
                      [(w[:, kc * 128:(kc + 1) * 128], sct[:, kc * 3:kc * 3 + 3]) for kc in range(KC)],
                      reads=[w.b, sct.b], writes=[pst.b])
            for j in range(3):
                fw.op("dve", lambda e, l=l, j=j, pst=pst: e.tensor_tensor(
                    out=modT[:, l * 144:(l + 1) * 144].rearrange("p (o j) -> p o j", j=3)[:, :, j],
                    in0=pst[:, 0:144].rearrange("p (o j) -> p o j", j=3)[:, :, j],
                    in1=modbT[:, l * 48:(l + 1) * 48], op=ALU.add),
                    reads=[pst.b, modbT.b], writes=[modT.b])
        for l in range(2):
            for n in range(2):
                grp = 1 if n == 0 else 4
                for j in range(3):
                    base_m = (l * 48 + grp * 8) * 3
                    base_a = ((l * 2 + n) * KC) * 3
                    fw.op("dve", lambda e, base_m=base_m, base_a=base_a, j=j, l=l, n=n: e.scalar_tensor_tensor(
                        out=AT[:, base_a:base_a + 24].rearrange("p (k j) -> p k j", j=3)[:, :, j],
                        in0=modT[:, base_m:base_m + 24].rearrange("p (k j) -> p k j", j=3)[:, :, j],
                        scalar=1.0,
                        in1=ngT[:, (l * 2 + n) * KC:(l * 2 + n + 1) * KC],
                        op0=ALU.add, op1=ALU.mult),
                        reads=[modT.b, ngT.b], writes=[AT.b])
        fw.barrier()

    def dbg_dump(tile_ap, rd, rows, cols, row0=0):
        s = fw.dsem()
        fw.dma("sp", dbg[row0:row0 + rows, 0:cols], tile_ap, rd, [], s)
        out_sems.append(s)

    def finish():
        fw.barrier()
        fw.final_wait(out_sems)
        ES.close()
        return nc

    if stop_after == "M":
        dbg_dump(modT[:], [modT.b], 128, 288)
        dbg_dump(AT[:], [AT.b], 128, 96, row0=128)
        return finish()

    def norm_block(es_t, x_t, n, off, k, l, nidx, j):
        sq, lnb, rs, tt = es_t["sq"], es_t["lnb"], es_t["rs"], es_t["tt"]
        fw.op("act", lambda e: e.activation(out=sq[:, :, 0:n], in_=x_t[:, :, 0:n], func=AF.Square),
              reads=[x_t.b], writes=[sq.b])
        pst = ps_next()
        fw.mm(pst[:, 0:n], [(ones_n[:], sq[:, kc, 0:n]) for kc in range(KC)],
              reads=[ones_n.b, sq.b], writes=[pst.b])
        fw.op("act", lambda e: e.activation(out=lnb[:, 0:n], in_=pst[:, 0:n], func=AF.Ln, bias=epsT[:], scale=1.0),
              reads=[pst.b, epsT.b], writes=[lnb.b])
        fw.op("act", lambda e: e.activation(out=rs[:, 0:n], in_=lnb[:, 0:n], func=AF.Exp, scale=-0.5),
              reads=[lnb.b], writes=[rs.b])
        for kc in range(KC):
            t = tt.next()
            fw.op("dve", lambda e, kc=kc, t=t: e.scalar_tensor_tensor(
                out=t[:, 0:n], in0=x_t[:, kc, 0:n], scalar=A_ap(l, nidx, kc, j), in1=rs[:, 0:n],
                op0=ALU.mult, op1=ALU.mult), reads=[x_t.b, AT.b, rs.b], writes=[t.b])
            grp = 0 if nidx == 0 else 3
            fw.op("act", lambda e, kc=kc, t=t: e.activation(
                out=hT[:, kc, off:off + n], in_=t[:, 0:n], func=AF.Identity, bias=mod_ap(l, grp, kc, j), scale=1.0),
                reads=[t.b, modT.b], writes=[hT_b[k]])

    def norm_tiles(es, pfx):
        return {
            "sq": T_(pfx + "sq", [128, KC, 512], BF16, es=es),
            "lnb": T_(pfx + "lnb", [128, 512], F32, es=es),
            "rs": T_(pfx + "rs", [128, 512], F32, es=es),
            "tt": Ring(fw, es, pfx + "tt", 2, [128, 512], F32),
        }

    def x_src(l, bi, k):
        off, n = BLKS[k]
        if l == 0:
            if k == 0:
                return ctxT[bi].rearrange("(c p) t -> p c t", p=128), None
            return xT[bi].rearrange("(c p) t -> p c t", p=128)[:, :, off - NCTX:off - NCTX + n], None
        return resT[bi].rearrange("c p t -> p c t")[:, :, off:off + n], res_b[bi][k]

    LAM0 = 0.8 - 0.6 * math.exp(-0.3 * 0)
    with ExitStack() as es:
        ltmp = T_("ltmp", [128, 128], F32, es=es)
        fw.op("dve", lambda e: e.tensor_tensor(out=ltmp[:, 0:64], in0=C(o_lamv, 64), in1=C(o_lamv + 64, 64), op=ALU.mult),
              reads=[smallc.b], writes=[ltmp.b])
        fw.op("dve", lambda e: e.tensor_tensor(out=ltmp[:, 64:128], in0=C(o_lamv + 128, 64), in1=C(o_lamv + 192, 64), op=ALU.mult),
              reads=[smallc.b], writes=[ltmp.b])
        fw.op("dve", lambda e: e.reduce_sum(out=lam_t[:, 0:1], in_=ltmp[:, 0:64], axis=mybir.AxisListType.X),
              reads=[ltmp.b], writes=[lam_t.b])
        fw.op("dve", lambda e: e.reduce_sum(out=lam_t[:, 1:2], in_=ltmp[:, 64:128], axis=mybir.AxisListType.X),
              reads=[ltmp.b], writes=[lam_t.b])
        fw.op("act", lambda e: e.activation(out=lam_t[:, 2:4], in_=lam_t[:, 0:2], func=AF.Exp), reads=[lam_t.b], writes=[lam_t.b])
        fw.op("dve", lambda e: e.tensor_tensor(out=lam_t[:, 4:5], in0=lam_t[:, 3:4], in1=lam_t[:, 2:3], op=ALU.subtract),
              reads=[lam_t.b], writes=[lam_t.b])
        fw.op("dve", lambda e: e.tensor_scalar(out=lam_t[:, 4:5], in0=lam_t[:, 4:5], scalar1=-LAM0, scalar2=None, op0=ALU.add),
              reads=[lam_t.b], writes=[lam_t.b])
        fw.op("dve", lambda e: e.tensor_scalar(out=lam_t[:, 5:6], in0=C(o_subg), scalar1=1.0 - LAM0, scalar2=None, op0=ALU.mult),
              reads=[smallc.b, lam_t.b], writes=[lam_t.b])
        fw.barrier()
    NEGLAM = lam_t[:, 4:5]
    SUBG = lam_t[:, 5:6]

    dumps = cfg.get("dumps", {})

    def dump_bf16(name, src3, rd, nch, cols):
        if name not in dumps:
            return
        fw.barrier()
        with ExitStack() as es:
            dt_ = T_("dbg_" + name, [128, cols], F32, es=es)
            for ch in range(nch):
                fw.op("dve", lambda e, ch=ch: e.tensor_copy(out=dt_[:], in_=src3[:, ch, :]), reads=rd, writes=[dt_.b])
                dbg_dump(dt_[:], [dt_.b], 128, cols, row0=dumps[name] + ch * 128)
            fw.barrier()

    def dump_res(name, bi):
        if name not in dumps:
            return
        fw.barrier()
        s = fw.dsem()
        r0 = dumps[name]
        fw.dma("sp", dbg[r0:r0 + 1024, 0:T], resT[bi].rearrange("c p t -> (c p) t"), res_b[bi], [], s)
        out_sems.append(s)
        fw.barrier()

    def ffn(l, bi, blks, jl):
        eso = ExitStack()
        fw.enabled = on("F2")
        wd = T_("wd", [128, FJ * 1024], BF16, es=eso, dma="pool")
        for jf in range(FJ):
            fw.dma("pool", wd[:, jf * 1024:(jf + 1) * 1024], w_dn[l, jf], [], [wd.b], wd.sem, merge=True)
        with ExitStack() as es:
            fw.enabled = on("F1")
            wab = Ring(fw, es, "wab", 3, [128, 2 * KC * 128], BF16, dma="pool")
            abuf = Ring(fw, es, "abuf", 2, [128, PT], F32)
            bbuf = Ring(fw, es, "bbuf", 2, [128, PT], F32)
            cbuf = T_("cbuf", [128, PT], F32, es=es)
            ubuf = Ring(fw, es, "ubuf", 2, [128, T], BF16, dma=True)
            for a in abuf.tiles:
                fw.op("dve", lambda e, a=a: e.memset(a[:], 0.0), writes=[a.b])
            lo = 1 if blks[0][0] == 0 else NCTX + 2
            hi = PT - 1
            for jf in range(FJ):
                w = wab.next()
                fw.dma("pool", w[:], w_up[l, jf], [], [w.b], w.sem)
                ab = abuf.next()
                bb = bbuf.next()
                for (off, n) in blks:
                    k = BLKS.index((off, n))
                    po = poff(off)
                    pa = ps_next()
                    fw.mm(pa[:, 0:n], [(w[:, kc * 128:(kc + 1) * 128], hT[:, kc, off:off + n]) for kc in range(KC)],
                          reads=[w.b, hT_b[k]], writes=[pa.b])
                    pb = ps_next()
                    fw.mm(pb[:, 0:n], [(w[:, 1024 + kc * 128:1024 + (kc + 1) * 128], hT[:, kc, off:off + n]) for kc in range(KC)],
                          reads=[w.b, hT_b[k]], writes=[pb.b])
                    fw.op("act", lambda e, pa=pa, ab=ab, po=po, n=n: e.activation(out=ab[:, po:po + n], in_=pa[:, 0:n], func=AF.Identity),
                          reads=[pa.b], writes=[ab.b])
                    fw.op("act", lambda e, pb=pb, bb=bb, po=po, n=n: e.activation(out=bb[:, po:po + n], in_=pb[:, 0:n], func=AF.Identity),
                          reads=[pb.b], writes=[bb.b])
                wbase = o_fcw + (l * FJ + jf) * 3
                fw.op("dve", lambda e, ab=ab, wbase=wbase: e.tensor_scalar(
                    out=cbuf[:, lo:hi], in0=ab[:, lo - 1:hi - 1], scalar1=C(wbase), scalar2=None, op0=ALU.mult),
                    reads=[ab.b, smallc.b], writes=[cbuf.b])
                fw.op("dve", lambda e, ab=ab, wbase=wbase: e.scalar_tensor_tensor(
                    out=cbuf[:, lo:hi], in0=ab[:, lo:hi], scalar=C(wbase + 1), in1=cbuf[:, lo:hi], op0=ALU.mult, op1=ALU.add),
                    reads=[ab.b, smallc.b, cbuf.b], writes=[cbuf.b])
                fw.op("dve", lambda e, ab=ab, wbase=wbase: e.scalar_tensor_tensor(
                    out=cbuf[:, lo:hi], in0=ab[:, lo + 1:hi + 1], scalar=C(wbase + 2), in1=cbuf[:, lo:hi], op0=ALU.mult, op1=ALU.add),
                    reads=[ab.b, smallc.b, cbuf.b], writes=[cbuf.b])
                if "F1silu" not in skip:
                    fw.op("act", lambda e, jf=jf: e.activation(out=cbuf[:, lo:hi], in_=cbuf[:, lo:hi], func=AF.Silu,
                                                               bias=C(o_fcb + l * FJ + jf), scale=1.0),
                          reads=[cbuf.b, smallc.b], writes=[cbuf.b])
                u = ubuf.next()
                if lo == 1:
                    fw.op("dve", lambda e, u=u, bb=bb: e.tensor_tensor(out=u[:, 0:NCTX], in0=cbuf[:, 1:1 + NCTX], in1=bb[:, 1:1 + NCTX], op=ALU.mult),
                          reads=[cbuf.b, bb.b], writes=[u.b])
                fw.op("dve", lambda e, u=u, bb=bb: e.tensor_tensor(out=u[:, NCTX:T], in0=cbuf[:, NCTX + 2:PT - 1], in1=bb[:, NCTX + 2:PT - 1], op=ALU.mult),
                      reads=[cbuf.b, bb.b], writes=[u.b])
                t0 = 0 if lo == 1 else NCTX
                if "F1store" not in skip:
                    fw.dma("sp", uT[bi, jf][:, t0:T], u[:, t0:T], [u.b], [u_b[bi][BLKS.index(b)] for b in blks], u.sem, merge=True)
            fw.barrier()
        with ExitStack() as es:
            fw.enabled = on("F2")
            us = Ring(fw, es, "us", 2, [128, FJ, 512], BF16, dma=True)
            xs = Ring(fw, es, "f2x", 2, [128, KC, 512], F32, dma=True, dma2=True)
            nt = norm_tiles(es, "f2") if l == 0 else None
            pendn = [None]
            for (off, n) in blks:
                k = BLKS.index((off, n))
                j = 2 if k == 0 else jl
                u_t = us.next()
                fw.dma("sp", u_t[:, :, 0:n], uT[bi].rearrange("j p t -> p j t")[:, :, off:off + n], [u_b[bi][k]], [u_t.b], u_t.sem)
                x_t = xs.next()
                fw.dma("sp", x_t[:, :, 0:n], resT[bi].rearrange("c p t -> p c t")[:, :, off:off + n], [res_b[bi][k]], [x_t.b], x_t.sem)
                for oc in range(KC):
                    pst = ps_next()
                    fw.mm(pst[:, 0:n], [(wd[:, jf * 1024 + oc * 128:jf * 1024 + (oc + 1) * 128], u_t[:, jf, 0:n]) for jf in range(FJ)],
                          reads=[wd.b, u_t.b], writes=[pst.b])
                    fw.op("dve", lambda e, pst=pst, oc=oc, x_t=x_t, n=n, j=j: e.scalar_tensor_tensor(
                        out=x_t[:, oc, 0:n], in0=pst[:, 0:n], scalar=mod_ap(l, 5, oc, j), in1=x_t[:, oc, 0:n], op0=ALU.mult, op1=ALU.add),
                        reads=[pst.b, modT.b, x_t.b], writes=[x_t.b])
                if l == 0:
                    fw.dma("sp", resT[bi].rearrange("c p t -> p c t")[:, :, off:off + n], x_t[:, :, 0:n], [x_t.b], [res_b[bi][k]], x_t.sem2, merge=True)
                    if pendn[0] is not None:
                        pendn[0]()
                    pendn[0] = (lambda x_t=x_t, n=n, off=off, k=k, j=j: norm_block(nt, x_t, n, off, k, 1, 0, j))
                else:
                    fw.dma("sp", outT[bi].rearrange("(c p) t -> p c t", p=128)[:, :, off - NCTX:off - NCTX + n], x_t[:, :, 0:n],
                           [x_t.b], [], x_t.sem2, merge=True)
                    if x_t.sem2 not in out_sems:
                        out_sems.append(x_t.sem2)
            if pendn[0] is not None:
                pendn[0]()
            fw.barrier()
        eso.close()
        fw.enabled = True

    def load_wo(l, es):
        wo = T_("wo", [128, KC * 1024], BF16, es=es, dma="pool")
        fw.dma("pool", wo[:], w_out[l], [], [wo.b], wo.sem)
        return wo

    def outproj(l, bi, blks, jl, yT, yT_b, wo):
        with ExitStack() as es:
            xs = Ring(fw, es, "p4x", 2, [128, KC, 512], F32, dma=True, dma2=True)
            nt = norm_tiles(es, "p4")
            pendn = [None]
            for (off, n) in blks:
                k = BLKS.index((off, n))
                j = 2 if k == 0 else jl
                x_t = xs.next()
                src, sb = x_src(l, bi, k)
                fw.dma("sp", x_t[:, :, 0:n], src, [sb] if sb else [], [x_t.b], x_t.sem)
                for oc in range(KC):
                    pst = ps_next()
                    fw.mm(pst[:, 0:n], [(wo[:, kc * 1024 + oc * 128:kc * 1024 + (oc + 1) * 128], yT[:, kc, off:off + n]) for kc in range(KC)],
                          reads=[wo.b] + yT_b, writes=[pst.b])
                    fw.op("dve", lambda e, pst=pst, oc=oc, x_t=x_t, n=n, j=j: e.scalar_tensor_tensor(
                        out=x_t[:, oc, 0:n], in0=pst[:, 0:n], scalar=mod_ap(l, 2, oc, j), in1=x_t[:, oc, 0:n], op0=ALU.mult, op1=ALU.add),
                        reads=[pst.b, modT.b, x_t.b], writes=[x_t.b])
                fw.dma("sp", resT[bi].rearrange("c p t -> p c t")[:, :, off:off + n], x_t[:, :, 0:n], [x_t.b], [res_b[bi][k]], x_t.sem2, merge=True)
                if pendn[0] is not None:
                    pendn[0]()
                pendn[0] = (lambda x_t=x_t, n=n, off=off, k=k, j=j: norm_block(nt, x_t, n, off, k, l, 1, j))
            if pendn[0] is not None:
                pendn[0]()
            fw.barrier()

    def mixer1(bi, jl):
        RT = 2 * 128 + 2 + 5 * 128
        o_pq0, o_pq1, o_kpos, o_dpos, o_dneg, o_lge, o_ule, o_id = 0, 128, 256, 258, 386, 514, 642, 770
        with ExitStack() as esm:
            yT = T_("yT1", [128, KC, T], BF16, es=esm)
            yT_b = [Buf(f"yT1_{c}") for c in range(KC)]
            wch = Ring(fw, esm, "wch1", 3, [128, KC * 128], BF16, dma="pool")
            with ExitStack() as es:
                NE = NLAT + 16
                pv = T_("pv", [128, NE], F32, es=es)
                A = [T_("pA0", [128, NE], F32, es=es), T_("pA1", [128, NE], F32, es=es)]
                ws = T_("pws", [128, NLAT], F32, es=es)
                yp = T_("pyp", [128, NLAT], BF16, es=es)
                pw = T_("ppw", [128, 512], BF16, es=es, dma="pool")
                fw.dma("pool", pw[:], poolw, [], [pw.b], pw.sem)
                pct = T_("pct", [128, 68], F32, es=es, dma=True)
                fw.dma("sp", pct[:, 0:64], pcorr, [], [pct.b], pct.sem, merge=True)
                fw.dma("sp", pct[:, 64:68], pools, [], [pct.b], pct.sem, merge=True)
                fw.op("dve", lambda e: e.memset(pv[:], 0.0), writes=[pv.b])
                for g in range(4):
                    r = (1, 2, 4, 8)[g]
                    w = wch.next()
                    fw.dma("pool", w[:], w_in1[g], [], [w.b], w.sem)
                    for (off, n) in LATBLKS:
                        k = BLKS.index((off, n))
                        pst = ps_next()
                        fw.mm(pst[:, 0:n], [(w[:, kc * 128:(kc + 1) * 128], hT[:, kc, off:off + n]) for kc in range(KC)],
                              reads=[w.b, hT_b[k]], writes=[pst.b])
                        c0 = 8 + off - NCTX
                        fw.op("act", lambda e, pst=pst, c0=c0, n=n: e.activation(out=pv[:, c0:c0 + n], in_=pst[:, 0:n], func=AF.Identity),
                              reads=[pst.b], writes=[pv.b])
                    src = pv
                    width = 1
                    ai = 0
                    while width < 2 * r:
                        dst = A[ai % 2]
                        ai += 1
                        cnt = NE - 2 * width + 1
                        fw.op("dve", lambda e, dst=dst, src=src, cnt=cnt, width=width: e.tensor_tensor(
                            out=dst[:, 0:cnt], in0=src[:, 0:cnt], in1=src[:, width:width + cnt], op=ALU.add),
                            reads=[src.b], writes=[dst.b])
                        src = dst
                        width *= 2
                    fw.op("dve", lambda e, src=src, r=r: e.tensor_tensor(
                        out=ws[:], in0=src[:, 8 - r:8 - r + NLAT], in1=pv[:, 8 + r:8 + r + NLAT], op=ALU.add),
                        reads=[src.b, pv.b], writes=[ws.b])
                    fw.op("dve", lambda e, g=g: e.tensor_tensor(out=ws[:, 0:8], in0=ws[:, 0:8], in1=pct[:, (g * 2) * 8:(g * 2) * 8 + 8], op=ALU.mult),
                          reads=[ws.b, pct.b], writes=[ws.b])
                    fw.op("dve", lambda e, g=g: e.tensor_tensor(out=ws[:, NLAT - 8:NLAT], in0=ws[:, NLAT - 8:NLAT],
                                                                in1=pct[:, (g * 2 + 1) * 8:(g * 2 + 1) * 8 + 8], op=ALU.mult),
                          reads=[ws.b, pct.b], writes=[ws.b])
                    fw.op("dve", lambda e, r=r: e.scalar_tensor_tensor(out=yp[:], in0=ws[:], scalar=1.0 / (2 * r + 1), in1=pv[:, 8:8 + NLAT],
                                                                       op0=ALU.mult, op1=ALU.subtract),
                          reads=[ws.b, pv.b], writes=[yp.b])
                    for (off, n) in LATBLKS:
                        pst = ps_next()
                        fw.mm(pst[:, 0:n], [(pw[:, g * 128:(g + 1) * 128], yp[:, off - NCTX:off - NCTX + n])], reads=[pw.b, yp.b], writes=[pst.b])
                        fw.op("act", lambda e, pst=pst, off=off, n=n, g=g: e.activation(
                            out=yT[:, g, off:off + n], in_=pst[:, 0:n], func=AF.Identity, scale=pct[:, 64 + g:65 + g]),
                            reads=[pst.b, pct.b], writes=[yT_b[g]])
                fw.barrier()
            with ExitStack() as es:
                qT = T_("rqT", [128, 2, NLAT], BF16, es=es)
                qdf = T_("rqdf", [128, 2, NLAT], BF16, es=es)
                qdb = T_("rqdb", [128, 2, NLAT], BF16, es=es)
                kT = T_("rkT", [128, 2, T], BF16, es=es)
                Vtok = T_("rVtok", [128, 18, 512], BF16, es=es)
                Sall = [T_("rSf", [128, 2, 16, 128], BF16, es=es), T_("rSb", [128, 2, 16, 128], BF16, es=es)]
                q_b = [Buf() for _ in range(2)]
                k_b = [Buf() for _ in range(2)]
                rt = T_("rt", [128, RT], F32, es=es, dma=True)
                fw.dma("sp", rt[:], rtab, [], [rt.b], rt.sem)
                dct = T_("dct", [128, 13], F32, es=es, dma=True)
                fw.dma("sp", dct[:, 0:4], decp, [], [dct.b], dct.sem, merge=True)
                fw.dma("sp", dct[:, 4:12], decr, [], [dct.b], dct.sem, merge=True)
                fw.dma("sp", dct[:, 12:13], gng, [], [dct.b], dct.sem, merge=True)
                lg = T_("lg", [128, 12], F32, es=es)
                dk = T_("dk", [128, 12], F32, es=es)
                qdecr = T_("qdecr", [128, 4, 512], F32, es=es)
                maskr = T_("maskr", [128, 4, 512], F32, es=es)
                ident = T_("identb", [128, 128], BF16, es=es)
                fw.op("act", lambda e: e.activation(out=lg[:], in_=dct[:, 0:12], func=AF.Exp), reads=[dct.b], writes=[lg.b])
                fw.op("dve", lambda e: e.tensor_scalar(out=lg[:], in0=lg[:], scalar1=-1.0, scalar2=None, op0=ALU.mult), reads=[lg.b], writes=[lg.b])
                fw.op("dve", lambda e: e.tensor_copy(out=ident[:], in_=rt[:, o_id:o_id + 128]), reads=[rt.b], writes=[ident.b])
                for d_ in range(2):
                    fw.op("act", lambda e, d_=d_: e.activation(out=dk[:, d_ * 4:d_ * 4 + 4], in_=lg[:, 4 + d_ * 4:8 + d_ * 4], func=AF.Exp,
                                                               scale=rt[:, o_kpos + d_:o_kpos + d_ + 1]),
                          reads=[lg.b, rt.b], writes=[dk.b])
                fw.op("act", lambda e: e.activation(out=dk[:, 8:12], in_=lg[:, 0:4], func=AF.Exp, scale=128.0), reads=[lg.b], writes=[dk.b])
                for d_ in range(2):
                    for hp in range(2):
                        for rep in range(4):
                            fw.op("act", lambda e, d_=d_, hp=hp, rep=rep: e.activation(
                                out=qdecr[:, d_ * 2 + hp, rep * 128:(rep + 1) * 128], in_=rt[:, o_pq0 + d_ * 128:o_pq0 + (d_ + 1) * 128],
                                func=AF.Exp, scale=lg[:, d_ * 2 + hp:d_ * 2 + hp + 1]), reads=[rt.b, lg.b], writes=[qdecr.b])
                with ExitStack() as es2:
                    m1 = T_("m1", [128, 128], F32, es=es2)
                    m2 = T_("m2", [128, 128], F32, es=es2)
                    for h in range(4):
                        fw.op("act", lambda e, h=h: e.activation(out=m1[:], in_=rt[:, o_dpos:o_dpos + 128], func=AF.Exp, scale=lg[:, 4 + h:5 + h]),
                              reads=[rt.b, lg.b], writes=[m1.b])
                        fw.op("dve", lambda e: e.tensor_tensor(out=m1[:], in0=m1[:], in1=rt[:, o_lge:o_lge + 128], op=ALU.mult),
                              reads=[m1.b, rt.b], writes=[m1.b])
                        fw.op("act", lambda e, h=h: e.activation(out=m2[:], in_=rt[:, o_dneg:o_dneg + 128], func=AF.Exp, scale=lg[:, 8 + h:9 + h]),
                              reads=[rt.b, lg.b], writes=[m2.b])
                        fw.op("dve", lambda e: e.tensor_tensor(out=m2[:], in0=m2[:], in1=rt[:, o_ule:o_ule + 128], op=ALU.mult),
                              reads=[m2.b, rt.b], writes=[m2.b])
                        for rep in range(4):
                            fw.op("dve", lambda e, h=h, rep=rep: e.tensor_tensor(out=maskr[:, h, rep * 128:(rep + 1) * 128], in0=m1[:], in1=m2[:], op=ALU.add),
                                  reads=[m1.b, m2.b], writes=[maskr.b])
                    fw.barrier()
                with ExitStack() as es2:
                    cosT = T_("rcos", [128, T], F32, es=es2, dma=True)
                    sinT = T_("rsin", [128, T], F32, es=es2, dma=True)
                    fw.dma("sp", cosT[:], ropec, [], [cosT.b], cosT.sem)
                    fw.dma("sp", sinT[:], ropes, [], [sinT.b], sinT.sem)
                    t1r = Ring(fw, es2, "rt1", 2, [128, 512], F32)
                    t2r = Ring(fw, es2, "rt2", 2, [128, 512], F32)
                    for which in ("q", "k"):
                        base = 4 if which == "q" else 8
                        for hp in range(2):
                            wm = wch.next()
                            fw.dma("pool", wm[:], w_in1[base + hp], [], [wm.b], wm.sem)
                            wr = wch.next()
                            fw.dma("pool", wr[:], w_in1[base + 2 + hp], [], [wr.b], wr.sem)
                            for (off, n) in (LATBLKS if which == "q" else BLKS):
                                k = BLKS.index((off, n))
                                p1 = ps_next()
                                fw.mm(p1[:, 0:n], [(wm[:, kc * 128:(kc + 1) * 128], hT[:, kc, off:off + n]) for kc in range(KC)],
                                      reads=[wm.b, hT_b[k]], writes=[p1.b])
                                p2 = ps_next()
                                fw.mm(p2[:, 0:n], [(wr[:, kc * 128:(kc + 1) * 128], hT[:, kc, off:off + n]) for kc in range(KC)],
                                      reads=[wr.b, hT_b[k]], writes=[p2.b])
                                t1 = t1r.next()
                                t2 = t2r.next()
                                sc_ = 0.125 if which == "q" else 1.0
                                fw.op("dve", lambda e, t1=t1, p1=p1, n=n, off=off, sc_=sc_: e.scalar_tensor_tensor(
                                    out=t1[:, 0:n], in0=p1[:, 0:n], scalar=sc_, in1=cosT[:, off:off + n], op0=ALU.mult, op1=ALU.mult),
                                    reads=[p1.b, cosT.b], writes=[t1.b])
                                fw.op("dve", lambda e, t2=t2, p2=p2, n=n, off=off, sc_=sc_: e.scalar_tensor_tensor(
                                    out=t2[:, 0:n], in0=p2[:, 0:n], scalar=sc_, in1=sinT[:, off:off + n], op0=ALU.mult, op1=ALU.mult),
                                    reads=[p2.b, sinT.b], writes=[t2.b])
                                if which == "k":
                                    fw.op("dve", lambda e, t1=t1, t2=t2, n=n, off=off, hp=hp: e.tensor_tensor(
                                        out=kT[:, hp, off:off + n], in0=t1[:, 0:n], in1=t2[:, 0:n], op=ALU.add),
                                        reads=[t1.b, t2.b], writes=[k_b[hp]])
                                else:
                                    lo_ = off - NCTX
                                    fw.op("dve", lambda e, t1=t1, t2=t2, n=n: e.tensor_tensor(out=t1[:, 0:n], in0=t1[:, 0:n], in1=t2[:, 0:n], op=ALU.add),
                                          reads=[t1.b, t2.b], writes=[t1.b])
                                    fw.op("act", lambda e, t1=t1, n=n, lo_=lo_, hp=hp: e.activation(out=qT[:, hp, lo_:lo_ + n], in_=t1[:, 0:n], func=AF.Identity),
                                          reads=[t1.b], writes=[q_b[hp]])
                                    fw.op("dve", lambda e, t1=t1, n=n, lo_=lo_, hp=hp: e.tensor_tensor(
                                        out=qdf[:, hp, lo_:lo_ + n], in0=t1[:, 0:n], in1=qdecr[:, hp, 0:n], op=ALU.mult),
                                        reads=[t1.b, qdecr.b], writes=[q_b[hp]])
                                    fw.op("dve", lambda e, t1=t1, n=n, lo_=lo_, hp=hp: e.tensor_tensor(
                                        out=qdb[:, hp, lo_:lo_ + n], in0=t1[:, 0:n], in1=qdecr[:, 2 + hp, 0:n], op=ALU.mult),
                                        reads=[t1.b, qdecr.b], writes=[q_b[hp]])
                    wv = T_("rwv", [128, KC * 512], BF16, es=es2, dma="pool")
                    fw.dma("pool", wv[:], w_v1, [], [wv.b], wv.sem)
                    for tt_ in range(18):
                        kb = 0 if tt_ < 2 else 1 + (tt_ - 2) // 4
                        pst = ps_next()
                        fw.mm(pst[:, 0:512], [(hT[:, kc, tt_ * 128:(tt_ + 1) * 128], wv[:, kc * 512:(kc + 1) * 512]) for kc in range(KC)],
                              reads=[wv.b, hT_b[kb]], writes=[pst.b])
                        fw.op("act", lambda e, pst=pst, tt_=tt_: e.activation(out=Vtok[:, tt_, :], in_=pst[:, 0:512], func=AF.Identity),
                              reads=[pst.b], writes=[Vtok.b])
                    fw.barrier()
                with ExitStack() as es2:
                    Ktok = [T_("rKf", [128, 18, 256], BF16, es=es2), T_("rKb", [128, 18, 256], BF16, es=es2)]
                    for tt_ in range(18):
                        for hp in range(2):
                            pst = ps_next()
                            fw.mm(pst[:, 0:128], [(kT[:, hp, tt_ * 128:(tt_ + 1) * 128], ident[:])], reads=[k_b[hp], ident.b], writes=[pst.b])
                            for d_ in range(2):
                                for hh in range(2):
                                    h = hp * 2 + hh
                                    eng = "act" if d_ == 0 else "dve"
                                    if eng == "act":
                                        fw.op("act", lambda e, pst=pst, tt_=tt_, h=h, hh=hh, d_=d_: e.activation(
                                            out=Ktok[d_][:, tt_, h * 64:(h + 1) * 64], in_=pst[:, hh * 64:(hh + 1) * 64], func=AF.Identity,
                                            scale=dk[:, d_ * 4 + h:d_ * 4 + h + 1]), reads=[pst.b, dk.b], writes=[Ktok[d_].b])
                                    else:
                                        fw.op("dve", lambda e, pst=pst, tt_=tt_, h=h, hh=hh, d_=d_: e.tensor_scalar(
                                            out=Ktok[d_][:, tt_, h * 64:(h + 1) * 64], in0=pst[:, hh * 64:(hh + 1) * 64],
                                            scalar1=dk[:, d_ * 4 + h:d_ * 4 + h + 1], scalar2=None, op0=ALU.mult),
                                            reads=[pst.b, dk.b], writes=[Ktok[d_].b])
                    St = [[T_(f"rS{d_}{hp}", [128, 128], F32, es=es2) for hp in range(2)] for d_ in range(2)]
                    orders = {0: list(range(18)), 1: [1, 0] + list(range(17, 1, -1))}
                    chains = [(d_, hp) for d_ in range(2) for hp in range(2)]
                    for (d_, hp) in chains:
                        S = St[d_][hp]
                        fw.op("dve", lambda e, S=S: e.memset(S[:], 0.0), writes=[S.b])
                    for step in range(18):
                        for (d_, hp) in chains:
                            S = St[d_][hp]
                            n_ = orders[d_][step]
                            if n_ >= 2:
                                fw.op("act", lambda e, S=S, d_=d_, hp=hp, n_=n_: e.activation(out=Sall[d_][:, hp, n_ - 2, :], in_=S[:], func=AF.Identity),
                                      reads=[S.b], writes=[Sall[d_].b])
                            if step == 17:
                                continue
                            pkv = ps_next()
                            fw.mm(pkv[:, 0:256], [(Ktok[d_][:, n_, hp * 128:(hp + 1) * 128], Vtok[:, n_, hp * 256:(hp + 1) * 256])],
                                  reads=[Ktok[d_].b, Vtok.b], writes=[pkv.b])
                            cd = dk[:, 8 + d_ * 2 + hp:9 + d_ * 2 + hp]
                            fw.op("dve", lambda e, S=S, pkv=pkv, cd=cd: e.scalar_tensor_tensor(
                                out=S[0:64, :], in0=S[0:64, :], scalar=cd[0:64, :], in1=pkv[0:64, 0:128], op0=ALU.mult, op1=ALU.add),
                                reads=[S.b, pkv.b, dk.b], writes=[S.b])
                            fw.op("dve", lambda e, S=S, pkv=pkv, cd=cd: e.scalar_tensor_tensor(
                                out=S[64:128, :], in0=S[64:128, :], scalar=cd[64:128, :], in1=pkv[64:128, 128:256], op0=ALU.mult, op1=ALU.add),
                                reads=[S.b, pkv.b, dk.b], writes=[S.b])
                    fw.barrier()
                with ExitStack() as es2:
                    wg = [T_(f"rwg{h}", [128, KC * 128], BF16, es=es2, dma="pool") for h in range(4)]
                    for h in range(4):
                        fw.dma("pool", wg[h][:], w_in1[12 + h], [], [wg[h].b], wg[h].sem)
                    Abf = Ring(fw, es2, "rAbf", 3, [128, 512], BF16)
                    sqrr = Ring(fw, es2, "rsq", 2, [128, 512], BF16)
                    lnr = T_("rln", [128, 512], F32, es=es2)
                    rsr = T_("rrs", [128, 512], F32, es=es2)
                    yrr = T_("ryr", [128, 512], F32, es=es2)
                    sgr = T_("rsg", [128, 512], F32, es=es2)
                    R3U = [(h, bidx) for h in range(4) for bidx in range(len(LATBLKS))]
                    st3 = {}

                    def s12(u):
                        h, bidx = R3U[u]
                        hp, r0 = h // 2, (h % 2) * 64
                        pa = ps_next()
                        for cc in range(4):
                            c = bidx * 4 + cc
                            n_ = c + 2
                            fw.mm(pa[:, cc * 128:(cc + 1) * 128], [(kT[r0:r0 + 64, hp, n_ * 128:(n_ + 1) * 128], qT[r0:r0 + 64, hp, c * 128:(c + 1) * 128])],
                                  reads=[k_b[hp], q_b[hp]], writes=[pa.b])
                        ab = Abf.next()
                        fw.op("dve", lambda e: e.tensor_tensor(out=ab[:], in0=pa[:], in1=maskr[:, h, :], op=ALU.mult),
                              reads=[pa.b, maskr.b], writes=[ab.b])
                        st3[u] = {"ab": ab}

                    def s34(u):
                        h, bidx = R3U[u]
                        hp, r0 = h // 2, (h % 2) * 64
                        ab = st3[u]["ab"]
                        po = ps_next()
                        for cc in range(4):
                            c = bidx * 4 + cc
                            n_ = c + 2
                            fw.mm(po[:, cc * 128:(cc + 1) * 128],
                                  [(Vtok[:, n_, h * 128:(h + 1) * 128], ab[:, cc * 128:(cc + 1) * 128]),
                                   (Sall[0][r0:r0 + 64, hp, c, :], qdf[r0:r0 + 64, hp, c * 128:(c + 1) * 128]),
                                   (Sall[1][r0:r0 + 64, hp, c, :], qdb[r0:r0 + 64, hp, c * 128:(c + 1) * 128])],
                                  reads=[Vtok.b, ab.b, Sall[0].b, Sall[1].b, q_b[hp]], writes=[po.b])
                        sq = sqrr.next()
                        fw.op("act", lambda e: e.activation(out=sq[:], in_=po[:], func=AF.Square), reads=[po.b], writes=[sq.b])
                        st3[u]["po"] = po
                        st3[u]["sq"] = sq

                    def s56(u):
                        h, bidx = R3U[u]
                        off, n = LATBLKS[bidx]
                        k = BLKS.index((off, n))
                        po, sq = st3[u]["po"], st3[u]["sq"]
                        pm = ps_next()
                        fw.mm(pm[:], [(ones_h[:], sq[:])], reads=[ones_h.b, sq.b], writes=[pm.b])
                        pg = ps_next()
                        fw.mm(pg[:, 0:n], [(wg[h][:, kc * 128:(kc + 1) * 128], hT[:, kc, off:off + n]) for kc in range(KC)],
                              reads=[wg[h].b, hT_b[k]], writes=[pg.b])
                        fw.op("act", lambda e: e.activation(out=lnr[:], in_=pm[:], func=AF.Ln, bias=epsT[:], scale=1.0),
                              reads=[pm.b, epsT.b], writes=[lnr.b])
                        fw.op("act", lambda e: e.activation(out=rsr[:], in_=lnr[:], func=AF.Exp, scale=-0.5), reads=[lnr.b], writes=[rsr.b])
                        fw.op("dve", lambda e: e.scalar_tensor_tensor(out=yrr[:], in0=po[:], scalar=dct[:, 12:13], in1=rsr[:],
                                                                      op0=ALU.mult, op1=ALU.mult),
                              reads=[po.b, dct.b, rsr.b], writes=[yrr.b])
                        fw.op("act", lambda e: e.activation(out=sgr[:], in_=pg[:], func=AF.Silu), reads=[pg.b], writes=[sgr.b])
                        fw.op("dve", lambda e: e.tensor_tensor(out=yT[:, 4 + h, off:off + n], in0=yrr[:], in1=sgr[:], op=ALU.mult),
                              reads=[yrr.b, sgr.b], writes=[yT_b[4 + h]])
                        del st3[u]

                    NU = len(R3U)
                    for it in range(NU + 2):
                        if 0 <= it - 2 < NU:
                            s56(it - 2)
                        if 0 <= it - 1 < NU:
                            s34(it - 1)
                        if it < NU:
                            s12(it)
                    fw.barrier()
            dump_bf16("y1", yT, yT_b, 8, T)
            wo1 = load_wo(1, esm)
            outproj(1, bi, LATBLKS, jl, yT, yT_b, wo1)

    for bi in range(nb):
        jl = bi
        with ExitStack() as es:
            xs = Ring(fw, es, "p1x", 2, [128, KC, 512], F32, dma=True)
            nt = norm_tiles(es, "p1")
            for k, (off, n) in enumerate(BLKS):
                x_t = xs.next()
                src, sb = x_src(0, bi, k)
                fw.dma("sp", x_t[:, :, 0:n], src, [sb] if sb else [], [x_t.b], x_t.sem)
                norm_block(nt, x_t, n, off, k, 0, 0, 2 if k == 0 else jl)
            fw.barrier()
        with ExitStack() as esm:
            yT = T_("yT", [128, KC, T], BF16, es=esm)
            yT_b = [Buf(f"yT{c}") for c in range(KC)]
            wch = Ring(fw, esm, "wch", 3, [128, KC * 128], BF16, dma="pool")
            with ExitStack() as es:
                fw.enabled = on("P2a")
                mbuf = T_("mbuf", [128, PT], F32, es=es)
                cgf = T_("cgf", [128, PT], F32, es=es)
                bgf = T_("bgf", [128, PT], F32, es=es)
                ybuf = T_("ybuf", [128, PT], F32, es=es)
                fw.op("dve", lambda e: e.memset(mbuf[:], 0.0), writes=[mbuf.b])
                for jc in range(4):
                    for typ, c in (("cg", 4 + jc), ("xv", 8 + jc), ("bg", jc)):
                        w = wch.next()
                        fw.dma("pool", w[:], w_in0[c], [], [w.b], w.sem)
                        for k, (off, n) in enumerate(BLKS):
                            po = poff(off)
                            pst = ps_next()
                            fw.mm(pst[:, 0:n], [(w[:, kc * 128:(kc + 1) * 128], hT[:, kc, off:off + n]) for kc in range(KC)],
                                  reads=[w.b, hT_b[k]], writes=[pst.b])
                            if typ == "cg":
                                fw.op("act", lambda e, pst=pst, po=po, n=n: e.activation(out=cgf[:, po:po + n], in_=pst[:, 0:n], func=AF.Identity),
                                      reads=[pst.b], writes=[cgf.b])
                            elif typ == "xv":
                                fw.op("dve", lambda e, pst=pst, po=po, n=n: e.tensor_tensor(out=mbuf[:, po:po + n], in0=pst[:, 0:n], in1=cgf[:, po:po + n], op=ALU.mult),
                                      reads=[pst.b, cgf.b], writes=[mbuf.b])
                            else:
                                fw.op("act", lambda e, pst=pst, po=po, n=n: e.activation(out=bgf[:, po:po + n], in_=pst[:, 0:n], func=AF.Identity),
                                      reads=[pst.b], writes=[bgf.b])
                    wb = o_scw + jc * 3
                    fw.op("dve", lambda e, wb=wb: e.tensor_scalar(out=ybuf[:, 1:PT - 1], in0=mbuf[:, 0:PT - 2], scalar1=C(wb), scalar2=None, op0=ALU.mult),
                          reads=[mbuf.b, smallc.b], writes=[ybuf.b])
                    fw.op("dve", lambda e, wb=wb: e.scalar_tensor_tensor(out=ybuf[:, 1:PT - 1], in0=mbuf[:, 1:PT - 1], scalar=C(wb + 1), in1=ybuf[:, 1:PT - 1],
                                                                         op0=ALU.mult, op1=ALU.add), reads=[mbuf.b, smallc.b, ybuf.b], writes=[ybuf.b])
                    fw.op("dve", lambda e, wb=wb: e.scalar_tensor_tensor(out=ybuf[:, 1:PT - 1], in0=mbuf[:, 2:PT], scalar=C(wb + 2), in1=ybuf[:, 1:PT - 1],
                                                                         op0=ALU.mult, op1=ALU.add), reads=[mbuf.b, smallc.b, ybuf.b], writes=[ybuf.b])
                    fw.op("dve", lambda e, jc=jc: e.tensor_tensor(out=yT[:, jc, 0:NCTX], in0=ybuf[:, 1:1 + NCTX], in1=bgf[:, 1:1 + NCTX], op=ALU.mult),
                          reads=[ybuf.b, bgf.b], writes=[yT_b[jc]])
                    fw.op("dve", lambda e, jc=jc: e.tensor_tensor(out=yT[:, jc, NCTX:T], in0=ybuf[:, NCTX + 2:PT - 1], in1=bgf[:, NCTX + 2:PT - 1], op=ALU.mult),
                          reads=[ybuf.b, bgf.b], writes=[yT_b[jc]])
                fw.barrier()
            fw.enabled = on("P4")
            wo0 = load_wo(0, esm)
            with ExitStack() as es:
                fw.enabled = on("P2b")
                qT = T_("qT", [128, 4, T], BF16, es=es)
                kT = T_("kT", [128, 4, T], BF16, es=es)
                qT_b = [Buf() for _ in range(4)]
                kT_b = [Buf() for _ in range(4)]
                Vtok = T_("Vtok", [128, 18, 512], BF16, es=es)
                with ExitStack() as es2:
                    cosT = T_("cosT", [128, T], F32, es=es2, dma=True)
                    sinT = T_("sinT", [128, T], F32, es=es2, dma=True)
                    fw.dma("sp", cosT[:], ropec, [], [cosT.b], cosT.sem)
                    fw.dma("sp", sinT[:], ropes, [], [sinT.b], sinT.sem)
                    sqk = Ring(fw, es2, "sqk", 2, [128, 512], BF16)
                    lnk = Ring(fw, es2, "lnk", 2, [128, 512], F32)
                    rsk = Ring(fw, es2, "rsk", 2, [128, 512], F32)
                    t1r = Ring(fw, es2, "t1r", 2, [128, 512], F32)
                    t2r = Ring(fw, es2, "t2r", 2, [128, 512], F32)
                    for (cm, cr, gcol, dst, dst_b) in (() if "P2bqk" in skip else ((12, 16, o_qkg, qT, qT_b), (20, 24, o_qkg + 2, kT, kT_b))):
                        for h in range(4):
                            wm = wch.next()
                            fw.dma("pool", wm[:], w_in0[cm + h], [], [wm.b], wm.sem)
                            wr = wch.next()
                            fw.dma("pool", wr[:], w_in0[cr + h], [], [wr.b], wr.sem)
                            def stage_a(k, off, n, wm=wm, wr=wr):
                                p1 = ps_next()
                                fw.mm(p1[:, 0:n], [(wm[:, kc * 128:(kc + 1) * 128], hT[:, kc, off:off + n]) for kc in range(KC)],
                                      reads=[wm.b, hT_b[k]], writes=[p1.b])
                                p2 = ps_next()
                                fw.mm(p2[:, 0:n], [(wr[:, kc * 128:(kc + 1) * 128], hT[:, kc, off:off + n]) for kc in range(KC)],
                                      reads=[wr.b, hT_b[k]], writes=[p2.b])
                                sq = sqk.next()
                                fw.op("act", lambda e: e.activation(out=sq[:, 0:n], in_=p1[:, 0:n], func=AF.Square),
                                      reads=[p1.b], writes=[sq.b])
                                return p1, p2, sq

                            def stage_b(k, off, n, p1, p2, sq, h=h, gcol=gcol, dst=dst, dst_b=dst_b):
                                p3 = ps_next()
                                fw.mm(p3[:, 0:n], [(ones_g[:], sq[:, 0:n])], reads=[ones_g.b, sq.b], writes=[p3.b])
                                ln = lnk.next()
                                rs = rsk.next()
                                fw.op("act", lambda e: e.activation(out=ln[:, 0:n], in_=p3[:, 0:n], func=AF.Ln, bias=epsT[:], scale=1.0),
                                      reads=[p3.b, epsT.b], writes=[ln.b])
                                fw.op("act", lambda e: e.activation(out=rs[:, 0:n], in_=ln[:, 0:n], func=AF.Exp, scale=-0.5),
                                      reads=[ln.b], writes=[rs.b])
                                t1 = t1r.next()
                                t2 = t2r.next()
                                fw.op("dve", lambda e: e.scalar_tensor_tensor(
                                    out=t1[:, 0:n], in0=p1[:, 0:n], scalar=C(gcol), in1=cosT[:, off:off + n], op0=ALU.mult, op1=ALU.mult),
                                    reads=[p1.b, smallc.b, cosT.b], writes=[t1.b])
                                fw.op("dve", lambda e: e.scalar_tensor_tensor(
                                    out=t2[:, 0:n], in0=p2[:, 0:n], scalar=C(gcol + 1), in1=sinT[:, off:off + n], op0=ALU.mult, op1=ALU.mult),
                                    reads=[p2.b, smallc.b, sinT.b], writes=[t2.b])
                                fw.op("dve", lambda e: e.tensor_tensor(out=t1[:, 0:n], in0=t1[:, 0:n], in1=t2[:, 0:n], op=ALU.add),
                                      reads=[t1.b, t2.b], writes=[t1.b])
                                fw.op("dve", lambda e: e.tensor_tensor(
                                    out=dst[:, h, off:off + n], in0=t1[:, 0:n], in1=rs[:, 0:n], op=ALU.mult),
                                    reads=[t1.b, rs.b], writes=[dst_b[h]])

                            prev = stage_a(0, *BLKS[0])
                            for k, (off, n) in enumerate(BLKS):
                                nxt = stage_a(k + 1, *BLKS[k + 1]) if k + 1 < len(BLKS) else None
                                stage_b(k, off, n, *prev)
                                prev = nxt
                    wv = T_("wv", [128, KC * 512], BF16, es=es2, dma="pool")
                    fw.dma("pool", wv[:], w_v0, [], [wv.b], wv.sem)
                    for tt_ in (() if "P2bv" in skip else range(18)):
                        kb = 0 if tt_ < 2 else 1 + (tt_ - 2) // 4
                        pst = ps_next()
                        fw.mm(pst[:, 0:512], [(hT[:, kc, tt_ * 128:(tt_ + 1) * 128], wv[:, kc * 512:(kc + 1) * 512]) for kc in range(KC)],
                              reads=[wv.b, hT_b[kb]], writes=[pst.b])
                        fw.op("act", lambda e, pst=pst, tt_=tt_: e.activation(out=Vtok[:, tt_, :], in_=pst[:, 0:512], func=AF.Identity),
                              reads=[pst.b], writes=[Vtok.b])
                    fw.barrier()
                dump_bf16("qT", qT, qT_b, 4, T)
                dump_bf16("kT", kT, kT_b, 4, T)
                with ExitStack() as es2:
                    fw.enabled = on("P3")
                    Er = Ring(fw, es2, "Er", 8, [128, 512], BF16)
                    r1 = T_("r1", [128, 512], F32, es=es2)
                    r2 = T_("r2", [128, 512], F32, es=es2)
                    o1 = T_("o1", [128, 512], F32, es=es2)
                    o2 = T_("o2", [128, 512], F32, es=es2)
                    osq = T_("osq", [128, 512], BF16, es=es2)
                    oln = T_("oln", [128, 512], F32, es=es2)
                    ors = T_("ors", [128, 512], F32, es=es2)
                    STP = [(PS[0], PS[1]), (PS[6], PS[7])]
                    stc = [0]
                    pend = [None]
                    accE = [T_("accE0", [128, 512], F32, es=es2), T_("accE1", [128, 512], F32, es=es2)]
                    ones_f = T_("ones_f", [128, 128], F32, es=es2)
                    fw.op("dve", lambda e: e.memset(ones_f[:], 1.0), writes=[ones_f.b])

                    def epi_a(n):
                        fw.op("dve", lambda e: e.reciprocal(out=r1[:, 0:n], in_=PS[4][:, 0:n]), reads=[PS[4].b], writes=[r1.b])
                        fw.op("dve", lambda e: e.reciprocal(out=r2[:, 0:n], in_=PS[5][:, 0:n]), reads=[PS[5].b], writes=[r2.b])
                        fw.op("dve", lambda e: e.tensor_tensor(out=o1[:, 0:n], in0=PS[2][:, 0:n], in1=r1[:, 0:n], op=ALU.mult),
                              reads=[PS[2].b, r1.b], writes=[o1.b])
                        fw.op("dve", lambda e: e.tensor_tensor(out=o2[:, 0:n], in0=PS[3][:, 0:n], in1=r2[:, 0:n], op=ALU.mult),
                              reads=[PS[3].b, r2.b], writes=[o2.b])
                        fw.op("dve", lambda e: e.scalar_tensor_tensor(out=o1[:, 0:n], in0=o2[:, 0:n], scalar=NEGLAM, in1=o1[:, 0:n],
                                                                      op0=ALU.mult, op1=ALU.add),
                              reads=[o1.b, o2.b, lam_t.b], writes=[o1.b])
                        fw.op("act", lambda e: e.activation(out=osq[:, 0:n], in_=o1[:, 0:n], func=AF.Square), reads=[o1.b], writes=[osq.b])

                    def epi_b(h, off, n):
                        fw.mm(PS[4][:, 0:n], [(ones_h[:], osq[:, 0:n])], reads=[ones_h.b, osq.b], writes=[PS[4].b])
                        fw.op("act", lambda e: e.activation(out=oln[:, 0:n], in_=PS[4][:, 0:n], func=AF.Ln, bias=epsT[:], scale=1.0),
                              reads=[PS[4].b, epsT.b], writes=[oln.b])
                        fw.op("act", lambda e: e.activation(out=ors[:, 0:n], in_=oln[:, 0:n], func=AF.Exp, scale=-0.5),
                              reads=[oln.b], writes=[ors.b])
                        fw.op("dve", lambda e: e.scalar_tensor_tensor(
                            out=yT[:, 4 + h, off:off + n], in0=o1[:, 0:n], scalar=SUBG, in1=ors[:, 0:n], op0=ALU.mult, op1=ALU.mult),
                            reads=[o1.b, lam_t.b, ors.b], writes=[yT_b[4 + h]])

                    for h in range(4):
                        for k, (off, n) in enumerate(BLKS):
                            kts = [0, 1] if k == 0 else list(range(18))
                            sts = {}

                            def emit_st(i, h=h, off=off, n=n, kts=kts, sts=sts):
                                kt = kts[i]
                                pair = STP[stc[0] % 2]
                                stc[0] += 1
                                for m in range(2):
                                    fw.mm(pair[m][:, 0:n], [(kT[m * 64:(m + 1) * 64, h, kt * 128:(kt + 1) * 128], qT[m * 64:(m + 1) * 64, h, off:off + n])],
                                          reads=[kT_b[h], qT_b[h]], writes=[pair[m].b])
                                sts[i] = pair

                            emit_st(0)
                            trig = min(4, len(kts) - 1)
                            for i, kt in enumerate(kts):
                                if i + 1 < len(kts):
                                    emit_st(i + 1)
                                pair = sts.pop(i)
                                ees = []
                                for m in range(2):
                                    ee = Er.next()
                                    fw.op("act", lambda e, ee=ee, st=pair[m], n=n: e.activation(out=ee[:, 0:n], in_=st[:, 0:n], func=AF.Exp, scale=0.125),
                                          reads=[pair[m].b], writes=[ee.b])
                                    ees.append(ee)
                                for m in range(2):
                                    fw.mm(PS[2 + m][:, 0:n], [(Vtok[:, kt, h * 128:(h + 1) * 128], ees[m][:, 0:n])],
                                          reads=[Vtok.b, ees[m].b], writes=[PS[2 + m].b], start=(i == 0), stop=(i == len(kts) - 1))
                                for m in range(2):
                                    fw.mm(PS[4 + m][:, 0:n], [(ones_1[:], ees[m][:, 0:n])],
                                          reads=[ones_1.b, ees[m].b], writes=[PS[4 + m].b], start=(i == 0), stop=(i == len(kts) - 1))
                            epi_a(n)
                            epi_b(h, off, n)
                    if pend[0] is not None:
                        pend[0]()
                        pend[0] = None
                    fw.barrier()
            dump_bf16("yT", yT, yT_b, 8, T)
            fw.enabled = on("P4")
            outproj(0, bi, BLKS, jl, yT, yT_b, wo0)
        fw.enabled = True
        dump_res("xmid0", bi)
        ffn(0, bi, BLKS, jl)
        dump_res("xout0", bi)
        if nlayers == 1:
            continue
        mixer1(bi, jl)
        dump_res("xmid1", bi)
        ffn(1, bi, LATBLKS, jl)

    return finish()


def _chunk_cols(w, cols):
    sel = w[:, cols]
    n = sel.shape[1] // 128
    t = sel.reshape(KC, 128, n, 128).transpose(2, 1, 0, 3)
    return np.ascontiguousarray(t.reshape(n, 128, KC * 128))


def _rope_partner(d):
    return d + 16 if (d % 32) < 16 else d - 16


def _rope_tables():
    inv = (10000.0 ** (-np.arange(16, dtype=np.float32) / 16)).astype(np.float32)
    t = np.arange(NLAT)
    row = (t // 64).astype(np.float32)
    col = (t % 64).astype(np.float32)
    c = np.ones((128, T), np.float32)
    s = np.zeros((128, T), np.float32)
    for p in range(128):
        d = p % 64
        axis, half, f = d // 32, (d % 32) // 16, d % 16
        ang = (row if axis == 0 else col) * inv[f]
        c[p, NCTX:] = np.cos(ang.astype(np.float32))
        sn = np.sin(ang.astype(np.float32))
        s[p, NCTX:] = -sn if half == 0 else sn
    return c, s


def _ret_tables():
    i = np.arange(128, dtype=np.float32)
    posq = np.concatenate([np.tile(i + 1.0, (128, 1)), np.tile(128.0 - i, (128, 1))], axis=1)
    kpos = np.stack([127.0 - i, i], axis=1)
    jj = i[:, None]
    ii = i[None, :]
    dpos = np.maximum(ii - jj, 0.0)
    dneg = np.maximum(jj - ii, 0.0)
    lge = (ii >= jj).astype(np.float32)
    ule = (jj >= ii).astype(np.float32)
    return np.ascontiguousarray(np.concatenate([posq, kpos, dpos, dneg, lge, ule, np.eye(128, dtype=np.float32)], axis=1).astype(np.float32))


def _pool_corr():
    out = np.ones((128, 4, 2, 8), np.float32)
    for g, r in enumerate((1, 2, 4, 8)):
        for t in range(r):
            cnt = (t + r) - 0 + 1
            out[:, g, 0, t] = (2 * r + 1) / cnt
            out[:, g, 1, 7 - t] = (2 * r + 1) / cnt
    return out.reshape(128, 64)


def prepare_inputs(inputs):
    f = lambda a: np.ascontiguousarray(np.asarray(a, dtype=np.float32))
    x, c, ctx, c_ctx = f(inputs["x"]), f(inputs["c"]), f(inputs["ctx"]), f(inputs["c_ctx"])
    sh = {}
    mod_w = f(inputs["mod_w"])
    sh["modw"] = np.ascontiguousarray(
        mod_w.reshape(2, KC, 128, 48, 128).transpose(0, 3, 2, 1, 4).reshape(2, 48, 128, KC * 128))
    sh["modb"] = np.ascontiguousarray(f(inputs["mod_b"]).reshape(2, 48, 128).transpose(2, 0, 1).reshape(128, 96))
    n1, n2 = f(inputs["norm1_g"]), f(inputs["norm2_g"])
    ngs = np.stack([n1, n2], axis=1)
    sh["ng"] = np.ascontiguousarray(ngs.reshape(2, 2, KC, 128).transpose(3, 0, 1, 2).reshape(128, 32))
    w0 = f(inputs["ev_w_in"])[0]
    partner = np.array([_rope_partner(d) for d in range(64)])
    cols = list(range(0, 1536))
    qcols = np.arange(1536, 2048)
    kcols = np.arange(2048, 2560)
    perm512 = np.concatenate([g * 64 + partner for g in range(8)])
    allcols = np.concatenate([np.arange(0, 1536), qcols, qcols[0] + perm512, kcols, kcols[0] + perm512])
    sh["w_in0"] = _chunk_cols(w0, allcols)
    sh["w_v0"] = np.ascontiguousarray(w0[:, 2560:3072].reshape(KC, 128, 512).transpose(1, 0, 2).reshape(128, KC * 512))
    wo = np.stack([f(inputs["ev_w_out"])[0], f(inputs["od_w_out"])[0]])
    sh["w_out"] = np.ascontiguousarray(wo.reshape(2, KC, 128, 1024).transpose(0, 2, 1, 3).reshape(2, 128, KC * 1024))
    sh["scw"] = np.ascontiguousarray(f(inputs["sc_conv_w"])[0].reshape(3, 4, 128).transpose(2, 1, 0).reshape(128, 12))
    gq, gk = f(inputs["da_q_norm"])[0], f(inputs["da_k_norm"])[0]
    pidx = np.arange(128) % 64
    sh["qkg"] = np.ascontiguousarray(np.stack([gq[pidx], gq[partner[pidx]], gk[pidx], gk[partner[pidx]]], axis=1))
    lv = np.stack([f(inputs["da_lq1"])[0], f(inputs["da_lk1"])[0], f(inputs["da_lq2"])[0], f(inputs["da_lk2"])[0]])
    sh["lamv"] = np.ascontiguousarray(np.broadcast_to(lv.reshape(1, 256), (128, 256)))
    sh["subg"] = np.ascontiguousarray(f(inputs["da_subln_g"])[0].reshape(128, 1))
    rc, rs = _rope_tables()
    sh["ropec"], sh["ropes"] = rc, rs
    wu = f(inputs["ffn_w_up"])
    wu = wu.reshape(2, KC, 128, 2, FJ, 128).transpose(0, 4, 2, 3, 1, 5)
    sh["w_up"] = np.ascontiguousarray(wu.reshape(2, FJ, 128, 2 * KC * 128))
    wd = f(inputs["ffn_w_down"])
    sh["w_dn"] = np.ascontiguousarray(wd.reshape(2, FJ, 128, 1024))
    sh["fcw"] = np.ascontiguousarray(f(inputs["ffn_conv_w"]).reshape(2, 3, FJ, 128).transpose(3, 0, 2, 1).reshape(128, 132))
    sh["fcb"] = np.ascontiguousarray(f(inputs["ffn_conv_b"]).reshape(2, FJ, 128).transpose(2, 0, 1).reshape(128, 44))
    w1 = f(inputs["od_w_in"])[0]
    partner256 = np.concatenate([g * 64 + partner for g in range(4)])
    cols1 = np.concatenate([np.arange(0, 512), np.arange(512, 768), 512 + partner256, np.arange(768, 1024),
                            768 + partner256, np.arange(1536, 2048)])
    sh["w_in1"] = _chunk_cols(w1, cols1)
    sh["w_v1"] = np.ascontiguousarray(w1[:, 1024:1536].reshape(KC, 128, 512).transpose(1, 0, 2).reshape(128, KC * 512))
    sh["poolw"] = np.ascontiguousarray(f(inputs["pool_w"])[0].transpose(1, 0, 2).reshape(128, 512))
    sh["pools"] = np.ascontiguousarray(f(inputs["pool_scale"])[0].reshape(4, 128).T)
    df, db = f(inputs["ret_decay_f"])[0], f(inputs["ret_decay_b"])[0]
    hp_of = np.arange(128) // 64
    decp = np.zeros((128, 2, 2), np.float32)
    for hp in range(2):
        decp[:, 0, hp] = df[2 * hp + hp_of]
        decp[:, 1, hp] = db[2 * hp + hp_of]
    sh["decp"] = decp.reshape(128, 4)
    sh["decr"] = np.ascontiguousarray(np.broadcast_to(np.concatenate([df, db]).reshape(1, 8), (128, 8)))
    sh["gng"] = np.ascontiguousarray(f(inputs["ret_gn_g"])[0].reshape(128, 1))
    sh["rtab"] = _ret_tables()
    sh["pcorr"] = _pool_corr()
    per_core = []
    for i in range(8):
        b0 = 2 * i
        d = dict(sh)
        d["xT"] = np.ascontiguousarray(x[b0:b0 + 2].transpose(0, 2, 1))
        d["ctxT"] = np.ascontiguousarray(ctx[b0:b0 + 2].transpose(0, 2, 1))
        cs = np.stack([c[b0], c[b0 + 1], c_ctx])
        d["cT"] = np.ascontiguousarray(cs.reshape(3, KC, 128).transpose(2, 1, 0).reshape(128, KC * 3))
        per_core.append(d)
    return per_core


def kernel(**inputs):
    per_core = prepare_inputs(inputs)
    nc = build_program()
    res = run_bass_kernel_spmd(nc, per_core, core_ids=list(range(8)))
    out = np.empty((16, NLAT, D), np.float32)
    for i in range(8):
        o = res.results[i]["outT"]
        out[2 * i:2 * i + 2] = o.transpose(0, 2, 1)
    return out
```

```python
import math
from contextlib import ExitStack

import numpy as np
import concourse.bass as bass
import concourse.mybir as mybir
from concourse.bass_utils import run_bass_kernel_spmd

F32 = mybir.dt.float32
BF16 = mybir.dt.bfloat16
AF = mybir.ActivationFunctionType
ALU = mybir.AluOpType

D = 1024
KC = 8
NCTX = 256
NLAT = 2048
T = NCTX + NLAT
PT = T + 3
BLKS = [(0, 256), (256, 512), (768, 512), (1280, 512), (1792, 512)]
LATBLKS = BLKS[1:]
FF = 2816
FJ = 22
EPS = 1e-6
NB = 2


def poff(off):
    return off + 1 if off < NCTX else off + 2


class Sem:
    def __init__(self, nc, name):
        self.h = nc.alloc_semaphore(name=name)
        self.cnt = 0
        self.name = name


class Buf:
    def __init__(self, name="", excl=False):
        self.w = {}
        self.r = {}
        self.name = name
        self.excl = excl


class Eng:
    def __init__(self, nc, name, obj, is_pe=False):
        self.obj = obj
        self.sem = Sem(nc, "e_" + name)
        self.waited = {}
        self.is_pe = is_pe
        self.name = name


class FW:
    def __init__(self, nc):
        self.nc = nc
        self.E = {
            "pe": Eng(nc, "pe", nc.tensor, True),
            "act": Eng(nc, "act", nc.scalar),
            "dve": Eng(nc, "dve", nc.vector),
            "pool": Eng(nc, "pool", nc.gpsimd),
            "sp": Eng(nc, "sp", nc.sync),
        }
        self.enabled = True
        self.dsems = []
        self.free_sems = {"sp": [], "pool": []}
        self.nsem = 0

    def dsem(self, q="sp"):
        if self.free_sems[q]:
            return self.free_sems[q].pop()
        self.nsem += 1
        x = Sem(self.nc, f"d{q}{self.nsem}")
        x.q = q
        self.dsems.append(x)
        return x

    def _wait(self, E, reads, writes, skip_sem=None):
        deps = {}
        for b in reads:
            for k, v in b.w.items():
                if v > deps.get(k, 0):
                    deps[k] = v
            if b.excl:
                for k, v in b.r.items():
                    if k is not E.sem and v > deps.get(k, 0):
                        deps[k] = v
        for b in writes:
            for k, v in b.w.items():
                if v > deps.get(k, 0):
                    deps[k] = v
            for k, v in b.r.items():
                if v > deps.get(k, 0):
                    deps[k] = v
        for k, v in deps.items():
            if k is skip_sem:
                continue
            if E.is_pe and k is E.sem:
                continue
            if E.waited.get(k, 0) >= v:
                continue
            E.obj.wait_ge(k.h, v)
            E.waited[k] = v

    def op(self, eng, fn, reads=(), writes=()):
        if not self.enabled:
            return None
        E = self.E[eng]
        self._wait(E, reads, writes)
        ins = fn(E.obj)
        E.sem.cnt += 1
        ins.then_inc(E.sem.h, 1)
        c = E.sem.cnt
        for b in reads:
            b.r[E.sem] = c
        for b in writes:
            b.w = {E.sem: c}
            b.r = {}
        return ins

    def mm(self, out_ap, pairs, reads, writes, start=True, stop=True):
        if not self.enabled:
            return
        E = self.E["pe"]
        self._wait(E, reads, writes)
        n = len(pairs)
        ins = None
        for i, (l, r) in enumerate(pairs):
            ins = E.obj.matmul(out_ap, l, r, start=(start and i == 0), stop=(stop and i == n - 1))
        E.sem.cnt += 1
        ins.then_inc(E.sem.h, 1)
        c = E.sem.cnt
        for b in reads:
            b.r[E.sem] = c
        for b in writes:
            b.w = {E.sem: c}
            b.r = {}

    def dma(self, q, out_ap, in_ap, reads, writes, sem, merge=False):
        if not self.enabled:
            return
        E = self.E[q]
        assert getattr(sem, "q", q) == q, (sem.name, q)
        self._wait(E, reads, writes, skip_sem=sem if merge else None)
        ins = E.obj.dma_start(out=out_ap, in_=in_ap)
        sem.cnt += 16
        ins.then_inc(sem.h, 16)
        for b in reads:
            b.r[sem] = sem.cnt
        for b in writes:
            if merge:
                b.w[sem] = sem.cnt
            else:
                b.w = {sem: sem.cnt}
                b.r = {}

    def barrier(self):
        allsems = [E.sem for E in self.E.values()] + self.dsems
        for E in self.E.values():
            for k in allsems:
                if k is E.sem:
                    continue
                if k.cnt > E.waited.get(k, 0):
                    E.obj.wait_ge(k.h, k.cnt)
                    E.waited[k] = k.cnt

    def final_wait(self, sems):
        E = self.E["sp"]
        for k in sems:
            if k.cnt > E.waited.get(k, 0):
                E.obj.wait_ge(k.h, k.cnt)
                E.waited[k] = k.cnt


class Tile:
    def __init__(self, fw, es, name, shape, dt, psum=False, dma=False, dma2=False):
        nc = fw.nc
        fw.ntile = getattr(fw, "ntile", 0) + 1
        name = f"{name}_{fw.ntile}"
        if psum:
            self.t = es.enter_context(nc.psum_tensor(name, shape, dt))
        else:
            self.t = es.enter_context(nc.sbuf_tensor(name, shape, dt))
        self.b = Buf(name, excl=psum)
        self.sem = None
        self.sem2 = None
        if dma:
            q = dma if isinstance(dma, str) else "sp"
            self.sem = fw.dsem(q)
            es.callback(fw.free_sems[q].append, self.sem)
        if dma2:
            self.sem2 = fw.dsem("sp")
            es.callback(fw.free_sems["sp"].append, self.sem2)

    def __getitem__(self, idx):
        return self.t[idx]


class Ring:
    def __init__(self, fw, es, name, n, shape, dt, psum=False, dma=False, dma2=False):
        self.tiles = [Tile(fw, es, f"{name}{i}", shape, dt, psum=psum, dma=dma, dma2=dma2) for i in range(n)]
        self.i = 0

    def next(self):
        t = self.tiles[self.i % len(self.tiles)]
        self.i += 1
        return t


def build_program(cfg=None):
    cfg = cfg or {}
    nlayers = cfg.get("nlayers", 2)
    nb = cfg.get("nb", NB)
    stop_after = cfg.get("stop_after", None)
    only = cfg.get("only", None)

    def on(ph):
        return only is None or ph in only
    skip = set(cfg.get("skip", []))
    nc = bass.Bass("TRN2", target_bir_lowering=False)
    fw = FW(nc)
    ES = ExitStack()

    def din(name, shape, dt=F32):
        return nc.dram_tensor(name, list(shape), dt, kind="ExternalInput").ap()

    xT = din("xT", [NB, D, NLAT])
    ctxT = din("ctxT", [NB, D, NCTX])
    cT = din("cT", [128, KC * 3])
    modw = din("modw", [2, 12, 128, 4 * KC * 128])
    modb = din("modb", [128, 2 * 48])
    ng = din("ng", [128, 2 * 2 * KC])
    w_in0 = din("w_in0", [28, 128, KC * 128])
    w_v0 = din("w_v0", [128, KC * 512])
    w_out = din("w_out", [2, 128, KC * 1024])
    scw = din("scw", [128, 4 * 3])
    qkg = din("qkg", [128, 4])
    lamv = din("lamv", [128, 4 * 64])
    subg = din("subg", [128, 1])
    ropec = din("ropec", [128, T])
    ropes = din("ropes", [128, T])
    w_up = din("w_up", [2, FJ, 128, 2 * KC * 128])
    w_dn = din("w_dn", [2, FJ, 128, 1024])
    fcw = din("fcw", [128, 2 * FJ * 3])
    fcb = din("fcb", [128, 2 * FJ])
    w_in1 = din("w_in1", [16, 128, KC * 128])
    w_v1 = din("w_v1", [128, KC * 512])
    poolw = din("poolw", [128, 4 * 128])
    pools = din("pools", [128, 4])
    decp = din("decp", [128, 4])
    decr = din("decr", [128, 8])
    gng = din("gng", [128, 1])
    rtab = din("rtab", [128, 2 * 128 + 2 + 5 * 128])
    pcorr = din("pcorr", [128, 4 * 2 * 8])
    outT = nc.dram_tensor("outT", [NB, D, NLAT], F32, kind="ExternalOutput").ap()
    dbg = None
    if cfg.get("dbg"):
        dbg = nc.dram_tensor("dbg", list(cfg["dbg"]), F32, kind="ExternalOutput").ap()
    resT = nc.dram_tensor("resT", [NB, KC, 128, T], F32).ap()
    uT = nc.dram_tensor("uT", [NB, FJ, 128, T], BF16).ap()
    res_b = [[Buf(f"res{bi}_{k}") for k in range(len(BLKS))] for bi in range(NB)]
    u_b = [[Buf(f"u{bi}_{k}") for k in range(len(BLKS))] for bi in range(NB)]
    out_sems = []

    def T_(name, shape, dt, es=ES, **kw):
        return Tile(fw, es, name, shape, dt, **kw)

    PS = [T_(f"ps{i}", [128, 512], F32, psum=True) for i in range(8)]
    psi = [0]

    def ps_next():
        t = PS[psi[0] % 8]
        psi[0] += 1
        return t

    hT = T_("hT", [128, KC, T], BF16)
    hT_b = [Buf(f"hT{k}") for k in range(len(BLKS))]
    ones_n = T_("ones_n", [128, 128], BF16)
    ones_h = T_("ones_h", [128, 128], BF16)
    ones_1 = T_("ones_1", [128, 128], BF16)
    ones_g = T_("ones_g", [128, 128], BF16)
    epsT = T_("epsT", [128, 1], F32)
    cst = T_("cst", [128, KC * 3], F32, dma=True)
    sct = T_("sct", [128, KC * 3], F32)
    modT = T_("modT", [128, 2 * 48 * 3], F32)
    modbT = T_("modbT", [128, 96], F32, dma=True)
    ngT = T_("ngT", [128, 32], F32, dma=True)
    AT = T_("AT", [128, 2 * 2 * KC * 3], F32)
    smallc = T_("smallc", [128, 12 + 4 + 256 + 1 + 132 + 44], F32, dma=True)
    o_scw, o_qkg, o_lamv, o_subg, o_fcw, o_fcb = 0, 12, 16, 272, 273, 405
    lam_t = T_("lam_t", [128, 8], F32)

    def C(o, n=1):
        return smallc[:, o:o + n]

    def mod_ap(l, grp, kc, j):
        i = (l * 48 + grp * 8 + kc) * 3 + j
        return modT[:, i:i + 1]

    def A_ap(l, n, kc, j):
        i = ((l * 2 + n) * KC + kc) * 3 + j
        return AT[:, i:i + 1]

    fw.op("dve", lambda e: e.memset(ones_n[:], 1.0 / 1024.0), writes=[ones_n.b])
    fw.op("dve", lambda e: e.memset(ones_h[:], 1.0 / 128.0), writes=[ones_h.b])
    fw.op("dve", lambda e: e.memset(ones_1[:], 1.0), writes=[ones_1.b])
    fw.op("dve", lambda e: e.memset(ones_g[:], 0.0), writes=[ones_g.b])
    fw.op("dve", lambda e: e.memset(ones_g[0:64, 0:64], 1.0 / 64.0), writes=[ones_g.b])
    fw.op("dve", lambda e: e.memset(ones_g[64:128, 64:128], 1.0 / 64.0), writes=[ones_g.b])
    fw.op("dve", lambda e: e.memset(epsT[:], EPS), writes=[epsT.b])
    fw.dma("sp", cst[:], cT, [], [cst.b], cst.sem)
    fw.dma("sp", modbT[:], modb, [], [modbT.b], modbT.sem)
    fw.dma("sp", ngT[:], ng, [], [ngT.b], ngT.sem)
    for (o, src, n) in [(o_scw, scw, 12), (o_qkg, qkg, 4), (o_lamv, lamv, 256), (o_subg, subg, 1),
                        (o_fcw, fcw, 132), (o_fcb, fcb, 44)]:
        fw.dma("sp", smallc[:, o:o + n], src, [], [smallc.b], smallc.sem, merge=True)

    fw.op("act", lambda e: e.activation(out=sct[:], in_=cst[:], func=AF.Silu), reads=[cst.b], writes=[sct.b])
    with ExitStack() as es:
        mw = Ring(fw, es, "mw", 3, [128, 4 * KC * 128], F32, dma=True)
        for l in range(2):
            pst = ps_next()
            for og in range(12):
                w = mw.next()
                fw.dma("sp", w[:], modw[l, og], [], [w.b], w.sem)
                for o4 in range(4):
                    oc = og * 4 + o4
                    fw.mm(pst[:, oc * 3:oc * 3 + 3],
                          [(w[:, (o4 * KC + kc) * 128:(o4 * KC + kc + 1) * 128], sct[:, kc * 3:kc * 3 + 3]) for kc in range(KC)],
                          reads=[w.b, sct.b], writes=[pst.b])
            for j in range(3):
                fw.op("dve", lambda e, l=l, j=j, pst=pst: e.tensor_tensor(
                    out=modT[:, l * 144:(l + 1) * 144].rearrange("p (o j) -> p o j", j=3)[:, :, j],
                    in0=pst[:, 0:144].rearrange("p (o j) -> p o j", j=3)[:, :, j],
                    in1=modbT[:, l * 48:(l + 1) * 48], op=ALU.add),
                    reads=[pst.b, modbT.b], writes=[modT.b])
        for l in range(2):
            for n in range(2):
                grp = 1 if n == 0 else 4
                for j in range(3):
                    base_m = (l * 48 + grp * 8) * 3
                    base_a = ((l * 2 + n) * KC) * 3
                    fw.op("dve", lambda e, base_m=base_m, base_a=base_a, j=j, l=l, n=n: e.scalar_tensor_tensor(
                        out=AT[:, base_a:base_a + 24].rearrange("p (k j) -> p k j", j=3)[:, :, j],
                        in0=modT[:, base_m:base_m + 24].rearrange("p (k j) -> p k j", j=3)[:, :, j],
                        scalar=1.0,
                        in1=ngT[:, (l * 2 + n) * KC:(l * 2 + n + 1) * KC],
                        op0=ALU.add, op1=ALU.mult),
                        reads=[modT.b, ngT.b], writes=[AT.b])
        fw.barrier()

    def dbg_dump(tile_ap, rd, rows, cols, row0=0):
        s = fw.dsem()
        fw.dma("sp", dbg[row0:row0 + rows, 0:cols], tile_ap, rd, [], s)
        out_sems.append(s)

    def finish():
        fw.barrier()
        fw.final_wait(out_sems)
        ES.close()
        return nc

    if stop_after == "M":
        dbg_dump(modT[:], [modT.b], 128, 288)
        dbg_dump(AT[:], [AT.b], 128, 96, row0=128)
        return finish()

    def norm_block(es_t, x_t, n, off, k, l, nidx, j):
        sq, lnb, rs, tt = es_t["sq"], es_t["lnb"], es_t["rs"], es_t["tt"]
        fw.op("act", lambda e: e.activation(out=sq[:, :, 0:n], in_=x_t[:, :, 0:n], func=AF.Square),
              reads=[x_t.b], writes=[sq.b])
        pst = ps_next()
        fw.mm(pst[:, 0:n], [(ones_n[:], sq[:, kc, 0:n]) for kc in range(KC)],
              reads=[ones_n.b, sq.b], writes=[pst.b])
        fw.op("act", lambda e: e.activation(out=lnb[:, 0:n], in_=pst[:, 0:n], func=AF.Ln, bias=epsT[:], scale=1.0),
              reads=[pst.b, epsT.b], writes=[lnb.b])
        fw.op("act", lambda e: e.activation(out=rs[:, 0:n], in_=lnb[:, 0:n], func=AF.Exp, scale=-0.5),
              reads=[lnb.b], writes=[rs.b])
        for kc in range(KC):
            t = tt.next()
            fw.op("dve", lambda e, kc=kc, t=t: e.scalar_tensor_tensor(
                out=t[:, 0:n], in0=x_t[:, kc, 0:n], scalar=A_ap(l, nidx, kc, j), in1=rs[:, 0:n],
                op0=ALU.mult, op1=ALU.mult), reads=[x_t.b, AT.b, rs.b], writes=[t.b])
            grp = 0 if nidx == 0 else 3
            fw.op("act", lambda e, kc=kc, t=t: e.activation(
                out=hT[:, kc, off:off + n], in_=t[:, 0:n], func=AF.Identity, bias=mod_ap(l, grp, kc, j), scale=1.0),
                reads=[t.b, modT.b], writes=[hT_b[k]])

    def norm_tiles(es, pfx):
        return {
            "sq": T_(pfx + "sq", [128, KC, 512], BF16, es=es),
            "lnb": T_(pfx + "lnb", [128, 512], F32, es=es),
            "rs": T_(pfx + "rs", [128, 512], F32, es=es),
            "tt": Ring(fw, es, pfx + "tt", 2, [128, 512], F32),
        }

    def x_src(l, bi, k):
        off, n = BLKS[k]
        if l == 0:
            if k == 0:
                return ctxT[bi].rearrange("(c p) t -> p c t", p=128), None
            return xT[bi].rearrange("(c p) t -> p c t", p=128)[:, :, off - NCTX:off - NCTX + n], None
        return resT[bi].rearrange("c p t -> p c t")[:, :, off:off + n], res_b[bi][k]

    LAM0 = 0.8 - 0.6 * math.exp(-0.3 * 0)
    with ExitStack() as es:
        ltmp = T_("ltmp", [128, 128], F32, es=es)
        fw.op("dve", lambda e: e.tensor_tensor(out=ltmp[:, 0:64], in0=C(o_lamv, 64), in1=C(o_lamv + 64, 64), op=ALU.mult),
              reads=[smallc.b], writes=[ltmp.b])
        fw.op("dve", lambda e: e.tensor_tensor(out=ltmp[:, 64:128], in0=C(o_lamv + 128, 64), in1=C(o_lamv + 192, 64), op=ALU.mult),
              reads=[smallc.b], writes=[ltmp.b])
        fw.op("dve", lambda e: e.reduce_sum(out=lam_t[:, 0:1], in_=ltmp[:, 0:64], axis=mybir.AxisListType.X),
              reads=[ltmp.b], writes=[lam_t.b])
        fw.op("dve", lambda e: e.reduce_sum(out=lam_t[:, 1:2], in_=ltmp[:, 64:128], axis=mybir.AxisListType.X),
              reads=[ltmp.b], writes=[lam_t.b])
        fw.op("act", lambda e: e.activation(out=lam_t[:, 2:4], in_=lam_t[:, 0:2], func=AF.Exp), reads=[lam_t.b], writes=[lam_t.b])
        fw.op("dve", lambda e: e.tensor_tensor(out=lam_t[:, 4:5], in0=lam_t[:, 3:4], in1=lam_t[:, 2:3], op=ALU.subtract),
              reads=[lam_t.b], writes=[lam_t.b])
        fw.op("dve", lambda e: e.tensor_scalar(out=lam_t[:, 4:5], in0=lam_t[:, 4:5], scalar1=-LAM0, scalar2=None, op0=ALU.add),
              reads=[lam_t.b], writes=[lam_t.b])
        fw.op("dve", lambda e: e.tensor_scalar(out=lam_t[:, 5:6], in0=C(o_subg), scalar1=1.0 - LAM0, scalar2=None, op0=ALU.mult),
              reads=[smallc.b, lam_t.b], writes=[lam_t.b])
        fw.barrier()
    NEGLAM = lam_t[:, 4:5]
    SUBG = lam_t[:, 5:6]

    dumps = cfg.get("dumps", {})

    def dump_bf16(name, src3, rd, nch, cols):
        if name not in dumps:
            return
        fw.barrier()
        with ExitStack() as es:
            dt_ = T_("dbg_" + name, [128, cols], F32, es=es)
            for ch in range(nch):
                fw.op("dve", lambda e, ch=ch: e.tensor_copy(out=dt_[:], in_=src3[:, ch, :]), reads=rd, writes=[dt_.b])
                dbg_dump(dt_[:], [dt_.b], 128, cols, row0=dumps[name] + ch * 128)
            fw.barrier()

    def dump_res(name, bi):
        if name not in dumps:
            return
        fw.barrier()
        s = fw.dsem()
        r0 = dumps[name]
        fw.dma("sp", dbg[r0:r0 + 1024, 0:T], resT[bi].rearrange("c p t -> (c p) t"), res_b[bi], [], s)
        out_sems.append(s)
        fw.barrier()

    def ffn(l, bi, blks, jl):
        eso = ExitStack()
        fw.enabled = on("F2")
        wd = T_("wd", [128, FJ * 1024], BF16, es=eso, dma="pool")
        for jf in range(FJ):
            fw.dma("pool", wd[:, jf * 1024:(jf + 1) * 1024], w_dn[l, jf], [], [wd.b], wd.sem, merge=True)
        with ExitStack() as es:
            fw.enabled = on("F1")
            wab = Ring(fw, es, "wab", 3, [128, 2 * KC * 128], BF16, dma="pool")
            abuf = Ring(fw, es, "abuf", 2, [128, PT], F32)
            bbuf = Ring(fw, es, "bbuf", 2, [128, PT], F32)
            cbuf = T_("cbuf", [128, PT], F32, es=es)
            ubuf = Ring(fw, es, "ubuf", 2, [128, T], BF16, dma=True)
            for a in abuf.tiles:
                fw.op("dve", lambda e, a=a: e.memset(a[:], 0.0), writes=[a.b])
            lo = 1 if blks[0][0] == 0 else NCTX + 2
            hi = PT - 1
            for jf in range(FJ):
                w = wab.next()
                fw.dma("pool", w[:], w_up[l, jf], [], [w.b], w.sem)
                ab = abuf.next()
                bb = bbuf.next()
                for (off, n) in blks:
                    k = BLKS.index((off, n))
                    po = poff(off)
                    pa = ps_next()
                    fw.mm(pa[:, 0:n], [(w[:, kc * 128:(kc + 1) * 128], hT[:, kc, off:off + n]) for kc in range(KC)],
                          reads=[w.b, hT_b[k]], writes=[pa.b])
                    pb = ps_next()
                    fw.mm(pb[:, 0:n], [(w[:, 1024 + kc * 128:1024 + (kc + 1) * 128], hT[:, kc, off:off + n]) for kc in range(KC)],
                          reads=[w.b, hT_b[k]], writes=[pb.b])
                    fw.op("act", lambda e, pa=pa, ab=ab, po=po, n=n: e.activation(out=ab[:, po:po + n], in_=pa[:, 0:n], func=AF.Identity),
                          reads=[pa.b], writes=[ab.b])
                    fw.op("act", lambda e, pb=pb, bb=bb, po=po, n=n: e.activation(out=bb[:, po:po + n], in_=pb[:, 0:n], func=AF.Identity),
                          reads=[pb.b], writes=[bb.b])
                wbase = o_fcw + (l * FJ + jf) * 3
                fw.op("dve", lambda e, ab=ab, wbase=wbase: e.tensor_scalar(
                    out=cbuf[:, lo:hi], in0=ab[:, lo - 1:hi - 1], scalar1=C(wbase), scalar2=None, op0=ALU.mult),
                    reads=[ab.b, smallc.b], writes=[cbuf.b])
                fw.op("dve", lambda e, ab=ab, wbase=wbase: e.scalar_tensor_tensor(
                    out=cbuf[:, lo:hi], in0=ab[:, lo:hi], scalar=C(wbase + 1), in1=cbuf[:, lo:hi], op0=ALU.mult, op1=ALU.add),
                    reads=[ab.b, smallc.b, cbuf.b], writes=[cbuf.b])
                fw.op("dve", lambda e, ab=ab, wbase=wbase: e.scalar_tensor_tensor(
                    out=cbuf[:, lo:hi], in0=ab[:, lo + 1:hi + 1], scalar=C(wbase + 2), in1=cbuf[:, lo:hi], op0=ALU.mult, op1=ALU.add),
                    reads=[ab.b, smallc.b, cbuf.b], writes=[cbuf.b])
                if "F1silu" not in skip:
                    fw.op("act", lambda e, jf=jf: e.activation(out=cbuf[:, lo:hi], in_=cbuf[:, lo:hi], func=AF.Silu,
                                                               bias=C(o_fcb + l * FJ + jf), scale=1.0),
                          reads=[cbuf.b, smallc.b], writes=[cbuf.b])
                u = ubuf.next()
                if lo == 1:
                    fw.op("dve", lambda e, u=u, bb=bb: e.tensor_tensor(out=u[:, 0:NCTX], in0=cbuf[:, 1:1 + NCTX], in1=bb[:, 1:1 + NCTX], op=ALU.mult),
                          reads=[cbuf.b, bb.b], writes=[u.b])
                fw.op("dve", lambda e, u=u, bb=bb: e.tensor_tensor(out=u[:, NCTX:T], in0=cbuf[:, NCTX + 2:PT - 1], in1=bb[:, NCTX + 2:PT - 1], op=ALU.mult),
                      reads=[cbuf.b, bb.b], writes=[u.b])
                t0 = 0 if lo == 1 else NCTX
                if "F1store" not in skip:
                    fw.dma("sp", uT[bi, jf][:, t0:T], u[:, t0:T], [u.b], [u_b[bi][BLKS.index(b)] for b in blks], u.sem, merge=True)
            fw.barrier()
        with ExitStack() as es:
            fw.enabled = on("F2")
            us = Ring(fw, es, "us", 2, [128, FJ, 512], BF16, dma=True)
            xs = Ring(fw, es, "f2x", 2, [128, KC, 512], F32, dma=True, dma2=True)
            nt = norm_tiles(es, "f2") if l == 0 else None
            pendn = [None]
            for (off, n) in blks:
                k = BLKS.index((off, n))
                j = 2 if k == 0 else jl
                u_t = us.next()
                fw.dma("sp", u_t[:, :, 0:n], uT[bi].rearrange("j p t -> p j t")[:, :, off:off + n], [u_b[bi][k]], [u_t.b], u_t.sem)
                x_t = xs.next()
                fw.dma("sp", x_t[:, :, 0:n], resT[bi].rearrange("c p t -> p c t")[:, :, off:off + n], [res_b[bi][k]], [x_t.b], x_t.sem)
                for oc in range(KC):
                    pst = ps_next()
                    fw.mm(pst[:, 0:n], [(wd[:, jf * 1024 + oc * 128:jf * 1024 + (oc + 1) * 128], u_t[:, jf, 0:n]) for jf in range(FJ)],
                          reads=[wd.b, u_t.b], writes=[pst.b])
                    fw.op("dve", lambda e, pst=pst, oc=oc, x_t=x_t, n=n, j=j: e.scalar_tensor_tensor(
                        out=x_t[:, oc, 0:n], in0=pst[:, 0:n], scalar=mod_ap(l, 5, oc, j), in1=x_t[:, oc, 0:n], op0=ALU.mult, op1=ALU.add),
                        reads=[pst.b, modT.b, x_t.b], writes=[x_t.b])
                if l == 0:
                    fw.dma("sp", resT[bi].rearrange("c p t -> p c t")[:, :, off:off + n], x_t[:, :, 0:n], [x_t.b], [res_b[bi][k]], x_t.sem2, merge=True)
                    if pendn[0] is not None:
                        pendn[0]()
                    pendn[0] = (lambda x_t=x_t, n=n, off=off, k=k, j=j: norm_block(nt, x_t, n, off, k, 1, 0, j))
                else:
                    fw.dma("sp", outT[bi].rearrange("(c p) t -> p c t", p=128)[:, :, off - NCTX:off - NCTX + n], x_t[:, :, 0:n],
                           [x_t.b], [], x_t.sem2, merge=True)
                    if x_t.sem2 not in out_sems:
                        out_sems.append(x_t.sem2)
            if pendn[0] is not None:
                pendn[0]()
            fw.barrier()
        eso.close()
        fw.enabled = True

    def load_wo(l, es):
        wo = T_("wo", [128, KC * 1024], BF16, es=es, dma="pool")
        fw.dma("pool", wo[:], w_out[l], [], [wo.b], wo.sem)
        return wo

    def outproj(l, bi, blks, jl, yT, yT_b, wo):
        with ExitStack() as es:
            xs = Ring(fw, es, "p4x", 2, [128, KC, 512], F32, dma=True, dma2=True)
            nt = norm_tiles(es, "p4")
            pendn = [None]
            for (off, n) in blks:
                k = BLKS.index((off, n))
                j = 2 if k == 0 else jl
                x_t = xs.next()
                src, sb = x_src(l, bi, k)
                fw.dma("sp", x_t[:, :, 0:n], src, [sb] if sb else [], [x_t.b], x_t.sem)
                for oc in range(KC):
                    pst = ps_next()
                    fw.mm(pst[:, 0:n], [(wo[:, kc * 1024 + oc * 128:kc * 1024 + (oc + 1) * 128], yT[:, kc, off:off + n]) for kc in range(KC)],
                          reads=[wo.b] + yT_b, writes=[pst.b])
                    fw.op("dve", lambda e, pst=pst, oc=oc, x_t=x_t, n=n, j=j: e.scalar_tensor_tensor(
                        out=x_t[:, oc, 0:n], in0=pst[:, 0:n], scalar=mod_ap(l, 2, oc, j), in1=x_t[:, oc, 0:n], op0=ALU.mult, op1=ALU.add),
                        reads=[pst.b, modT.b, x_t.b], writes=[x_t.b])
                fw.dma("sp", resT[bi].rearrange("c p t -> p c t")[:, :, off:off + n], x_t[:, :, 0:n], [x_t.b], [res_b[bi][k]], x_t.sem2, merge=True)
                if pendn[0] is not None:
                    pendn[0]()
                pendn[0] = (lambda x_t=x_t, n=n, off=off, k=k, j=j: norm_block(nt, x_t, n, off, k, l, 1, j))
            if pendn[0] is not None:
                pendn[0]()
            fw.barrier()

    def mixer1(bi, jl):
        RT = 2 * 128 + 2 + 5 * 128
        o_pq0, o_pq1, o_kpos, o_dpos, o_dneg, o_lge, o_ule, o_id = 0, 128, 256, 258, 386, 514, 642, 770
        with ExitStack() as esm:
            yT = T_("yT1", [128, KC, T], BF16, es=esm)
            yT_b = [Buf(f"yT1_{c}") for c in range(KC)]
            wch = Ring(fw, esm, "wch1", 3, [128, KC * 128], BF16, dma="pool")
            with ExitStack() as es:
                NE = NLAT + 16
                pv = T_("pv", [128, NE], F32, es=es)
                A = [T_("pA0", [128, NE], F32, es=es), T_("pA1", [128, NE], F32, es=es)]
                ws = T_("pws", [128, NLAT], F32, es=es)
                yp = T_("pyp", [128, NLAT], BF16, es=es)
                pw = T_("ppw", [128, 512], BF16, es=es, dma="pool")
                fw.dma("pool", pw[:], poolw, [], [pw.b], pw.sem)
                pct = T_("pct", [128, 68], F32, es=es, dma=True)
                fw.dma("sp", pct[:, 0:64], pcorr, [], [pct.b], pct.sem, merge=True)
                fw.dma("sp", pct[:, 64:68], pools, [], [pct.b], pct.sem, merge=True)
                fw.op("dve", lambda e: e.memset(pv[:], 0.0), writes=[pv.b])
                for g in range(4):
                    r = (1, 2, 4, 8)[g]
                    w = wch.next()
                    fw.dma("pool", w[:], w_in1[g], [], [w.b], w.sem)
                    for (off, n) in LATBLKS:
                        k = BLKS.index((off, n))
                        pst = ps_next()
                        fw.mm(pst[:, 0:n], [(w[:, kc * 128:(kc + 1) * 128], hT[:, kc, off:off + n]) for kc in range(KC)],
                              reads=[w.b, hT_b[k]], writes=[pst.b])
                        c0 = 8 + off - NCTX
                        fw.op("act", lambda e, pst=pst, c0=c0, n=n: e.activation(out=pv[:, c0:c0 + n], in_=pst[:, 0:n], func=AF.Identity),
                              reads=[pst.b], writes=[pv.b])
                    src = pv
                    width = 1
                    ai = 0
                    while width < 2 * r:
                        dst = A[ai % 2]
                        ai += 1
                        cnt = NE - 2 * width + 1
                        fw.op("dve", lambda e, dst=dst, src=src, cnt=cnt, width=width: e.tensor_tensor(
                            out=dst[:, 0:cnt], in0=src[:, 0:cnt], in1=src[:, width:width + cnt], op=ALU.add),
                            reads=[src.b], writes=[dst.b])
                        src = dst
                        width *= 2
                    fw.op("dve", lambda e, src=src, r=r: e.tensor_tensor(
                        out=ws[:], in0=src[:, 8 - r:8 - r + NLAT], in1=pv[:, 8 + r:8 + r + NLAT], op=ALU.add),
                        reads=[src.b, pv.b], writes=[ws.b])
                    fw.op("dve", lambda e, g=g: e.tensor_tensor(out=ws[:, 0:8], in0=ws[:, 0:8], in1=pct[:, (g * 2) * 8:(g * 2) * 8 + 8], op=ALU.mult),
                          reads=[ws.b, pct.b], writes=[ws.b])
                    fw.op("dve", lambda e, g=g: e.tensor_tensor(out=ws[:, NLAT - 8:NLAT], in0=ws[:, NLAT - 8:NLAT],
                                                                in1=pct[:, (g * 2 + 1) * 8:(g * 2 + 1) * 8 + 8], op=ALU.mult),
                          reads=[ws.b, pct.b], writes=[ws.b])
                    fw.op("dve", lambda e, r=r: e.scalar_tensor_tensor(out=yp[:], in0=ws[:], scalar=1.0 / (2 * r + 1), in1=pv[:, 8:8 + NLAT],
                                                                       op0=ALU.mult, op1=ALU.subtract),
                          reads=[ws.b, pv.b], writes=[yp.b])
                    for (off, n) in LATBLKS:
                        pst = ps_next()
                        fw.mm(pst[:, 0:n], [(pw[:, g * 128:(g + 1) * 128], yp[:, off - NCTX:off - NCTX + n])], reads=[pw.b, yp.b], writes=[pst.b])
                        fw.op("act", lambda e, pst=pst, off=off, n=n, g=g: e.activation(
                            out=yT[:, g, off:off + n], in_=pst[:, 0:n], func=AF.Identity, scale=pct[:, 64 + g:65 + g]),
                            reads=[pst.b, pct.b], writes=[yT_b[g]])
                fw.barrier()
            with ExitStack() as es:
                qT = T_("rqT", [128, 2, NLAT], BF16, es=es)
                qdf = T_("rqdf", [128, 2, NLAT], BF16, es=es)
                qdb = T_("rqdb", [128, 2, NLAT], BF16, es=es)
                kT = T_("rkT", [128, 2, T], BF16, es=es)
                Vtok = T_("rVtok", [128, 18, 512], BF16, es=es)
                Sall = [T_("rSf", [128, 2, 16, 128], BF16, es=es), T_("rSb", [128, 2, 16, 128], BF16, es=es)]
                q_b = [Buf() for _ in range(2)]
                k_b = [Buf() for _ in range(2)]
                rt = T_("rt", [128, RT], F32, es=es, dma=True)
                fw.dma("sp", rt[:], rtab, [], [rt.b], rt.sem)
                dct = T_("dct", [128, 13], F32, es=es, dma=True)
                fw.dma("sp", dct[:, 0:4], decp, [], [dct.b], dct.sem, merge=True)
                fw.dma("sp", dct[:, 4:12], decr, [], [dct.b], dct.sem, merge=True)
                fw.dma("sp", dct[:, 12:13], gng, [], [dct.b], dct.sem, merge=True)
                lg = T_("lg", [128, 12], F32, es=es)
                dk = T_("dk", [128, 12], F32, es=es)
                qdecr = T_("qdecr", [128, 4, 512], F32, es=es)
                maskr = T_("maskr", [128, 4, 512], F32, es=es)
                ident = T_("identb", [128, 128], BF16, es=es)
                fw.op("act", lambda e: e.activation(out=lg[:], in_=dct[:, 0:12], func=AF.Exp), reads=[dct.b], writes=[lg.b])
                fw.op("dve", lambda e: e.tensor_scalar(out=lg[:], in0=lg[:], scalar1=-1.0, scalar2=None, op0=ALU.mult), reads=[lg.b], writes=[lg.b])
                fw.op("dve", lambda e: e.tensor_copy(out=ident[:], in_=rt[:, o_id:o_id + 128]), reads=[rt.b], writes=[ident.b])
                for d_ in range(2):
                    fw.op("act", lambda e, d_=d_: e.activation(out=dk[:, d_ * 4:d_ * 4 + 4], in_=lg[:, 4 + d_ * 4:8 + d_ * 4], func=AF.Exp,
                                                               scale=rt[:, o_kpos + d_:o_kpos + d_ + 1]),
                          reads=[lg.b, rt.b], writes=[dk.b])
                fw.op("act", lambda e: e.activation(out=dk[:, 8:12], in_=lg[:, 0:4], func=AF.Exp, scale=128.0), reads=[lg.b], writes=[dk.b])
                for d_ in range(2):
                    for hp in range(2):
                        for rep in range(4):
                            fw.op("act", lambda e, d_=d_, hp=hp, rep=rep: e.activation(
                                out=qdecr[:, d_ * 2 + hp, rep * 128:(rep + 1) * 128], in_=rt[:, o_pq0 + d_ * 128:o_pq0 + (d_ + 1) * 128],
                                func=AF.Exp, scale=lg[:, d_ * 2 + hp:d_ * 2 + hp + 1]), reads=[rt.b, lg.b], writes=[qdecr.b])
                with ExitStack() as es2:
                    m1 = T_("m1", [128, 128], F32, es=es2)
                    m2 = T_("m2", [128, 128], F32, es=es2)
                    for h in range(4):
                        fw.op("act", lambda e, h=h: e.activation(out=m1[:], in_=rt[:, o_dpos:o_dpos + 128], func=AF.Exp, scale=lg[:, 4 + h:5 + h]),
                              reads=[rt.b, lg.b], writes=[m1.b])
                        fw.op("dve", lambda e: e.tensor_tensor(out=m1[:], in0=m1[:], in1=rt[:, o_lge:o_lge + 128], op=ALU.mult),
                              reads=[m1.b, rt.b], writes=[m1.b])
                        fw.op("act", lambda e, h=h: e.activation(out=m2[:], in_=rt[:, o_dneg:o_dneg + 128], func=AF.Exp, scale=lg[:, 8 + h:9 + h]),
                              reads=[rt.b, lg.b], writes=[m2.b])
                        fw.op("dve", lambda e: e.tensor_tensor(out=m2[:], in0=m2[:], in1=rt[:, o_ule:o_ule + 128], op=ALU.mult),
                              reads=[m2.b, rt.b], writes=[m2.b])
                        for rep in range(4):
                            fw.op("dve", lambda e, h=h, rep=rep: e.tensor_tensor(out=maskr[:, h, rep * 128:(rep + 1) * 128], in0=m1[:], in1=m2[:], op=ALU.add),
                                  reads=[m1.b, m2.b], writes=[maskr.b])
                    fw.barrier()
                with ExitStack() as es2:
                    cosT = T_("rcos", [128, T], F32, es=es2, dma=True)
                    sinT = T_("rsin", [128, T], F32, es=es2, dma=True)
                    fw.dma("sp", cosT[:], ropec, [], [cosT.b], cosT.sem)
                    fw.dma("sp", sinT[:], ropes, [], [sinT.b], sinT.sem)
                    t1r = Ring(fw, es2, "rt1", 2, [128, 512], F32)
                    t2r = Ring(fw, es2, "rt2", 2, [128, 512], F32)
                    for which in ("q", "k"):
                        base = 4 if which == "q" else 8
                        for hp in range(2):
                            wm = wch.next()
                            fw.dma("pool", wm[:], w_in1[base + hp], [], [wm.b], wm.sem)
                            wr = wch.next()
                            fw.dma("pool", wr[:], w_in1[base + 2 + hp], [], [wr.b], wr.sem)
                            for (off, n) in (LATBLKS if which == "q" else BLKS):
                                k = BLKS.index((off, n))
                                p1 = ps_next()
                                fw.mm(p1[:, 0:n], [(wm[:, kc * 128:(kc + 1) * 128], hT[:, kc, off:off + n]) for kc in range(KC)],
                                      reads=[wm.b, hT_b[k]], writes=[p1.b])
                                p2 = ps_next()
                                fw.mm(p2[:, 0:n], [(wr[:, kc * 128:(kc + 1) * 128], hT[:, kc, off:off + n]) for kc in range(KC)],
                                      reads=[wr.b, hT_b[k]], writes=[p2.b])
                                t1 = t1r.next()
                                t2 = t2r.next()
                                sc_ = 0.125 if which == "q" else 1.0
                                fw.op("dve", lambda e, t1=t1, p1=p1, n=n, off=off, sc_=sc_: e.scalar_tensor_tensor(
                                    out=t1[:, 0:n], in0=p1[:, 0:n], scalar=sc_, in1=cosT[:, off:off + n], op0=ALU.mult, op1=ALU.mult),
                                    reads=[p1.b, cosT.b], writes=[t1.b])
                                fw.op("dve", lambda e, t2=t2, p2=p2, n=n, off=off, sc_=sc_: e.scalar_tensor_tensor(
                                    out=t2[:, 0:n], in0=p2[:, 0:n], scalar=sc_, in1=sinT[:, off:off + n], op0=ALU.mult, op1=ALU.mult),
                                    reads=[p2.b, sinT.b], writes=[t2.b])
                                if which == "k":
                                    fw.op("dve", lambda e, t1=t1, t2=t2, n=n, off=off, hp=hp: e.tensor_tensor(
                                        out=kT[:, hp, off:off + n], in0=t1[:, 0:n], in1=t2[:, 0:n], op=ALU.add),
                                        reads=[t1.b, t2.b], writes=[k_b[hp]])
                                else:
                                    lo_ = off - NCTX
                                    fw.op("dve", lambda e, t1=t1, t2=t2, n=n: e.tensor_tensor(out=t1[:, 0:n], in0=t1[:, 0:n], in1=t2[:, 0:n], op=ALU.add),
                                          reads=[t1.b, t2.b], writes=[t1.b])
                                    fw.op("act", lambda e, t1=t1, n=n, lo_=lo_, hp=hp: e.activation(out=qT[:, hp, lo_:lo_ + n], in_=t1[:, 0:n], func=AF.Identity),
                                          reads=[t1.b], writes=[q_b[hp]])
                                    fw.op("dve", lambda e, t1=t1, n=n, lo_=lo_, hp=hp: e.tensor_tensor(
                                        out=qdf[:, hp, lo_:lo_ + n], in0=t1[:, 0:n], in1=qdecr[:, hp, 0:n], op=ALU.mult),
                                        reads=[t1.b, qdecr.b], writes=[q_b[hp]])
                                    fw.op("dve", lambda e, t1=t1, n=n, lo_=lo_, hp=hp: e.tensor_tensor(
                                        out=qdb[:, hp, lo_:lo_ + n], in0=t1[:, 0:n], in1=qdecr[:, 2 + hp, 0:n], op=ALU.mult),
                                        reads=[t1.b, qdecr.b], writes=[q_b[hp]])
                    wv = T_("rwv", [128, KC * 512], BF16, es=es2, dma="pool")
                    fw.dma("pool", wv[:], w_v1, [], [wv.b], wv.sem)
                    for tt_ in range(18):
                        kb = 0 if tt_ < 2 else 1 + (tt_ - 2) // 4
                        pst = ps_next()
                        fw.mm(pst[:, 0:512], [(hT[:, kc, tt_ * 128:(tt_ + 1) * 128], wv[:, kc * 512:(kc + 1) * 512]) for kc in range(KC)],
                              reads=[wv.b, hT_b[kb]], writes=[pst.b])
                        fw.op("act", lambda e, pst=pst, tt_=tt_: e.activation(out=Vtok[:, tt_, :], in_=pst[:, 0:512], func=AF.Identity),
                              reads=[pst.b], writes=[Vtok.b])
                    fw.barrier()
                with ExitStack() as es2:
                    Ktok = [T_("rKf", [128, 18, 256], BF16, es=es2), T_("rKb", [128, 18, 256], BF16, es=es2)]
                    for tt_ in range(18):
                        for hp in range(2):
                            pst = ps_next()
                            fw.mm(pst[:, 0:128], [(kT[:, hp, tt_ * 128:(tt_ + 1) * 128], ident[:])], reads=[k_b[hp], ident.b], writes=[pst.b])
                            for d_ in range(2):
                                for hh in range(2):
                                    h = hp * 2 + hh
                                    eng = "act" if d_ == 0 else "dve"
                                    if eng == "act":
                                        fw.op("act", lambda e, pst=pst, tt_=tt_, h=h, hh=hh, d_=d_: e.activation(
                                            out=Ktok[d_][:, tt_, h * 64:(h + 1) * 64], in_=pst[:, hh * 64:(hh + 1) * 64], func=AF.Identity,
                                            scale=dk[:, d_ * 4 + h:d_ * 4 + h + 1]), reads=[pst.b, dk.b], writes=[Ktok[d_].b])
                                    else:
                                        fw.op("dve", lambda e, pst=pst, tt_=tt_, h=h, hh=hh, d_=d_: e.tensor_scalar(
                                            out=Ktok[d_][:, tt_, h * 64:(h + 1) * 64], in0=pst[:, hh * 64:(hh + 1) * 64],
                                            scalar1=dk[:, d_ * 4 + h:d_ * 4 + h + 1], scalar2=None, op0=ALU.mult),
                                            reads=[pst.b, dk.b], writes=[Ktok[d_].b])
                    St = [[T_(f"rS{d_}{hp}", [128, 128], F32, es=es2) for hp in range(2)] for d_ in range(2)]
                    orders = {0: list(range(18)), 1: [1, 0] + list(range(17, 1, -1))}
                    chains = [(d_, hp) for d_ in range(2) for hp in range(2)]
                    for (d_, hp) in chains:
                        S = St[d_][hp]
                        fw.op("dve", lambda e, S=S: e.memset(S[:], 0.0), writes=[S.b])
                    for step in range(18):
                        for (d_, hp) in chains:
                            S = St[d_][hp]
                            n_ = orders[d_][step]
                            if n_ >= 2:
                                fw.op("act", lambda e, S=S, d_=d_, hp=hp, n_=n_: e.activation(out=Sall[d_][:, hp, n_ - 2, :], in_=S[:], func=AF.Identity),
                                      reads=[S.b], writes=[Sall[d_].b])
                            if step == 17:
                                continue
                            pkv = ps_next()
                            fw.mm(pkv[:, 0:256], [(Ktok[d_][:, n_, hp * 128:(hp + 1) * 128], Vtok[:, n_, hp * 256:(hp + 1) * 256])],
                                  reads=[Ktok[d_].b, Vtok.b], writes=[pkv.b])
                            cd = dk[:, 8 + d_ * 2 + hp:9 + d_ * 2 + hp]
                            fw.op("dve", lambda e, S=S, pkv=pkv, cd=cd: e.scalar_tensor_tensor(
                                out=S[0:64, :], in0=S[0:64, :], scalar=cd[0:64, :], in1=pkv[0:64, 0:128], op0=ALU.mult, op1=ALU.add),
                                reads=[S.b, pkv.b, dk.b], writes=[S.b])
                            fw.op("dve", lambda e, S=S, pkv=pkv, cd=cd: e.scalar_tensor_tensor(
                                out=S[64:128, :], in0=S[64:128, :], scalar=cd[64:128, :], in1=pkv[64:128, 128:256], op0=ALU.mult, op1=ALU.add),
                                reads=[S.b, pkv.b, dk.b], writes=[S.b])
                    fw.barrier()
                with ExitStack() as es2:
                    wg = [T_(f"rwg{h}", [128, KC * 128], BF16, es=es2, dma="pool") for h in range(4)]
                    for h in range(4):
                        fw.dma("pool", wg[h][:], w_in1[12 + h], [], [wg[h].b], wg[h].sem)
                    Abf = Ring(fw, es2, "rAbf", 3, [128, 512], BF16)
                    sqrr = Ring(fw, es2, "rsq", 2, [128, 512], BF16)
                    lnr = T_("rln", [128, 512], F32, es=es2)
                    rsr = T_("rrs", [128, 512], F32, es=es2)
                    yrr = T_("ryr", [128, 512], F32, es=es2)
                    sgr = T_("rsg", [128, 512], F32, es=es2)
                    R3U = [(h, bidx) for h in range(4) for bidx in range(len(LATBLKS))]
                    st3 = {}

                    def s12(u):
                        h, bidx = R3U[u]
                        hp, r0 = h // 2, (h % 2) * 64
                        pa = ps_next()
                        for cc in range(4):
                            c = bidx * 4 + cc
                            n_ = c + 2
                            fw.mm(pa[:, cc * 128:(cc + 1) * 128], [(kT[r0:r0 + 64, hp, n_ * 128:(n_ + 1) * 128], qT[r0:r0 + 64, hp, c * 128:(c + 1) * 128])],
                                  reads=[k_b[hp], q_b[hp]], writes=[pa.b])
                        ab = Abf.next()
                        fw.op("dve", lambda e: e.tensor_tensor(out=ab[:], in0=pa[:], in1=maskr[:, h, :], op=ALU.mult),
                              reads=[pa.b, maskr.b], writes=[ab.b])
                        st3[u] = {"ab": ab}

                    def s34(u):
                        h, bidx = R3U[u]
                        hp, r0 = h // 2, (h % 2) * 64
                        ab = st3[u]["ab"]
                        po = ps_next()
                        for cc in range(4):
                            c = bidx * 4 + cc
                            n_ = c + 2
                            fw.mm(po[:, cc * 128:(cc + 1) * 128],
                                  [(Vtok[:, n_, h * 128:(h + 1) * 128], ab[:, cc * 128:(cc + 1) * 128]),
                                   (Sall[0][r0:r0 + 64, hp, c, :], qdf[r0:r0 + 64, hp, c * 128:(c + 1) * 128]),
                                   (Sall[1][r0:r0 + 64, hp, c, :], qdb[r0:r0 + 64, hp, c * 128:(c + 1) * 128])],
                                  reads=[Vtok.b, ab.b, Sall[0].b, Sall[1].b, q_b[hp]], writes=[po.b])
                        sq = sqrr.next()
                        fw.op("act", lambda e: e.activation(out=sq[:], in_=po[:], func=AF.Square), reads=[po.b], writes=[sq.b])
                        st3[u]["po"] = po
                        st3[u]["sq"] = sq

                    def s56(u):
                        h, bidx = R3U[u]
                        off, n = LATBLKS[bidx]
                        k = BLKS.index((off, n))
                        po, sq = st3[u]["po"], st3[u]["sq"]
                        pm = ps_next()
                        fw.mm(pm[:], [(ones_h[:], sq[:])], reads=[ones_h.b, sq.b], writes=[pm.b])
                        pg = ps_next()
                        fw.mm(pg[:, 0:n], [(wg[h][:, kc * 128:(kc + 1) * 128], hT[:, kc, off:off + n]) for kc in range(KC)],
                              reads=[wg[h].b, hT_b[k]], writes=[pg.b])
                        fw.op("act", lambda e: e.activation(out=lnr[:], in_=pm[:], func=AF.Ln, bias=epsT[:], scale=1.0),
                              reads=[pm.b, epsT.b], writes=[lnr.b])
                        fw.op("act", lambda e: e.activation(out=rsr[:], in_=lnr[:], func=AF.Exp, scale=-0.5), reads=[lnr.b], writes=[rsr.b])
                        fw.op("dve", lambda e: e.scalar_tensor_tensor(out=yrr[:], in0=po[:], scalar=dct[:, 12:13], in1=rsr[:],
                                                                      op0=ALU.mult, op1=ALU.mult),
                              reads=[po.b, dct.b, rsr.b], writes=[yrr.b])
                        fw.op("act", lambda e: e.activation(out=sgr[:], in_=pg[:], func=AF.Silu), reads=[pg.b], writes=[sgr.b])
                        fw.op("dve", lambda e: e.tensor_tensor(out=yT[:, 4 + h, off:off + n], in0=yrr[:], in1=sgr[:], op=ALU.mult),
                              reads=[yrr.b, sgr.b], writes=[yT_b[4 + h]])
                        del st3[u]

                    NU = len(R3U)
                    for it in range(NU + 2):
                        if 0 <= it - 2 < NU:
                            s56(it - 2)
                        if 0 <= it - 1 < NU:
                            s34(it - 1)
                        if it < NU:
                            s12(it)
                    fw.barrier()
            dump_bf16("y1", yT, yT_b, 8, T)
            wo1 = load_wo(1, esm)
            outproj(1, bi, LATBLKS, jl, yT, yT_b, wo1)

    for bi in range(nb):
        jl = bi
        with ExitStack() as es:
            xs = Ring(fw, es, "p1x", 2, [128, KC, 512], F32, dma=True)
            nt = norm_tiles(es, "p1")
            for k, (off, n) in enumerate(BLKS):
                x_t = xs.next()
                src, sb = x_src(0, bi, k)
                fw.dma("sp", x_t[:, :, 0:n], src, [sb] if sb else [], [x_t.b], x_t.sem)
                norm_block(nt, x_t, n, off, k, 0, 0, 2 if k == 0 else jl)
            fw.barrier()
        with ExitStack() as esm:
            yT = T_("yT", [128, KC, T], BF16, es=esm)
            yT_b = [Buf(f"yT{c}") for c in range(KC)]
            wch = Ring(fw, esm, "wch", 3, [128, KC * 128], BF16, dma="pool")
            with ExitStack() as es:
                fw.enabled = on("P2a")
                mbuf = T_("mbuf", [128, PT], F32, es=es)
                cgf = T_("cgf", [128, PT], F32, es=es)
                bgf = T_("bgf", [128, PT], F32, es=es)
                ybuf = T_("ybuf", [128, PT], F32, es=es)
                fw.op("dve", lambda e: e.memset(mbuf[:], 0.0), writes=[mbuf.b])
                for jc in range(4):
                    for typ, c in (("cg", 4 + jc), ("xv", 8 + jc), ("bg", jc)):
                        w = wch.next()
                        fw.dma("pool", w[:], w_in0[c], [], [w.b], w.sem)
                        for k, (off, n) in enumerate(BLKS):
                            po = poff(off)
                            pst = ps_next()
                            fw.mm(pst[:, 0:n], [(w[:, kc * 128:(kc + 1) * 128], hT[:, kc, off:off + n]) for kc in range(KC)],
                                  reads=[w.b, hT_b[k]], writes=[pst.b])
                            if typ == "cg":
                                fw.op("act", lambda e, pst=pst, po=po, n=n: e.activation(out=cgf[:, po:po + n], in_=pst[:, 0:n], func=AF.Identity),
                                      reads=[pst.b], writes=[cgf.b])
                            elif typ == "xv":
                                fw.op("dve", lambda e, pst=pst, po=po, n=n: e.tensor_tensor(out=mbuf[:, po:po + n], in0=pst[:, 0:n], in1=cgf[:, po:po + n], op=ALU.mult),
                                      reads=[pst.b, cgf.b], writes=[mbuf.b])
                            else:
                                fw.op("act", lambda e, pst=pst, po=po, n=n: e.activation(out=bgf[:, po:po + n], in_=pst[:, 0:n], func=AF.Identity),
                                      reads=[pst.b], writes=[bgf.b])
                    wb = o_scw + jc * 3
                    fw.op("dve", lambda e, wb=wb: e.tensor_scalar(out=ybuf[:, 1:PT - 1], in0=mbuf[:, 0:PT - 2], scalar1=C(wb), scalar2=None, op0=ALU.mult),
                          reads=[mbuf.b, smallc.b], writes=[ybuf.b])
                    fw.op("dve", lambda e, wb=wb: e.scalar_tensor_tensor(out=ybuf[:, 1:PT - 1], in0=mbuf[:, 1:PT - 1], scalar=C(wb + 1), in1=ybuf[:, 1:PT - 1],
                                                                         op0=ALU.mult, op1=ALU.add), reads=[mbuf.b, smallc.b, ybuf.b], writes=[ybuf.b])
                    fw.op("dve", lambda e, wb=wb: e.scalar_tensor_tensor(out=ybuf[:, 1:PT - 1], in0=mbuf[:, 2:PT], scalar=C(wb + 2), in1=ybuf[:, 1:PT - 1],
                                                                         op0=ALU.mult, op1=ALU.add), reads=[mbuf.b, smallc.b, ybuf.b], writes=[ybuf.b])
                    fw.op("dve", lambda e, jc=jc: e.tensor_tensor(out=yT[:, jc, 0:NCTX], in0=ybuf[:, 1:1 + NCTX], in1=bgf[:, 1:1 + NCTX], op=ALU.mult),
                          reads=[ybuf.b, bgf.b], writes=[yT_b[jc]])
                    fw.op("dve", lambda e, jc=jc: e.tensor_tensor(out=yT[:, jc, NCTX:T], in0=ybuf[:, NCTX + 2:PT - 1], in1=bgf[:, NCTX + 2:PT - 1], op=ALU.mult),
                          reads=[ybuf.b, bgf.b], writes=[yT_b[jc]])
                fw.barrier()
            fw.enabled = on("P4")
            wo0 = load_wo(0, esm)
            with ExitStack() as es:
                fw.enabled = on("P2b")
                qT = T_("qT", [128, 4, T], BF16, es=es)
                kT = T_("kT", [128, 4, T], BF16, es=es)
                qT_b = [Buf() for _ in range(4)]
                kT_b = [Buf() for _ in range(4)]
                Vtok = T_("Vtok", [128, 18, 512], BF16, es=es)
                with ExitStack() as es2:
                    cosT = T_("cosT", [128, T], F32, es=es2, dma=True)
                    sinT = T_("sinT", [128, T], F32, es=es2, dma=True)
                    fw.dma("sp", cosT[:], ropec, [], [cosT.b], cosT.sem)
                    fw.dma("sp", sinT[:], ropes, [], [sinT.b], sinT.sem)
                    sqk = Ring(fw, es2, "sqk", 2, [128, 512], BF16)
                    lnk = Ring(fw, es2, "lnk", 2, [128, 512], F32)
                    rsk = Ring(fw, es2, "rsk", 2, [128, 512], F32)
                    t1r = Ring(fw, es2, "t1r", 2, [128, 512], F32)
                    t2r = Ring(fw, es2, "t2r", 2, [128, 512], F32)
                    for (cm, cr, gcol, dst, dst_b) in (() if "P2bqk" in skip else ((12, 16, o_qkg, qT, qT_b), (20, 24, o_qkg + 2, kT, kT_b))):
                        for h in range(4):
                            wm = wch.next()
                            fw.dma("pool", wm[:], w_in0[cm + h], [], [wm.b], wm.sem)
                            wr = wch.next()
                            fw.dma("pool", wr[:], w_in0[cr + h], [], [wr.b], wr.sem)
                            def stage_a(k, off, n, wm=wm, wr=wr):
                                p1 = ps_next()
                                fw.mm(p1[:, 0:n], [(wm[:, kc * 128:(kc + 1) * 128], hT[:, kc, off:off + n]) for kc in range(KC)],
                                      reads=[wm.b, hT_b[k]], writes=[p1.b])
                                p2 = ps_next()
                                fw.mm(p2[:, 0:n], [(wr[:, kc * 128:(kc + 1) * 128], hT[:, kc, off:off + n]) for kc in range(KC)],
                                      reads=[wr.b, hT_b[k]], writes=[p2.b])
                                sq = sqk.next()
                                fw.op("act", lambda e: e.activation(out=sq[:, 0:n], in_=p1[:, 0:n], func=AF.Square),
                                      reads=[p1.b], writes=[sq.b])
                                return p1, p2, sq

                            def stage_b(k, off, n, p1, p2, sq, h=h, gcol=gcol, dst=dst, dst_b=dst_b):
                                p3 = ps_next()
                                fw.mm(p3[:, 0:n], [(ones_g[:], sq[:, 0:n])], reads=[ones_g.b, sq.b], writes=[p3.b])
                                ln = lnk.next()
                                rs = rsk.next()
                                fw.op("act", lambda e: e.activation(out=ln[:, 0:n], in_=p3[:, 0:n], func=AF.Ln, bias=epsT[:], scale=1.0),
                                      reads=[p3.b, epsT.b], writes=[ln.b])
                                fw.op("act", lambda e: e.activation(out=rs[:, 0:n], in_=ln[:, 0:n], func=AF.Exp, scale=-0.5),
                                      reads=[ln.b], writes=[rs.b])
                                t1 = t1r.next()
                                t2 = t2r.next()
                                fw.op("dve", lambda e: e.scalar_tensor_tensor(
                                    out=t1[:, 0:n], in0=p1[:, 0:n], scalar=C(gcol), in1=cosT[:, off:off + n], op0=ALU.mult, op1=ALU.mult),
                                    reads=[p1.b, smallc.b, cosT.b], writes=[t1.b])
                                fw.op("dve", lambda e: e.scalar_tensor_tensor(
                                    out=t2[:, 0:n], in0=p2[:, 0:n], scalar=C(gcol + 1), in1=sinT[:, off:off + n], op0=ALU.mult, op1=ALU.mult),
                                    reads=[p2.b, smallc.b, sinT.b], writes=[t2.b])
                                fw.op("dve", lambda e: e.tensor_tensor(out=t1[:, 0:n], in0=t1[:, 0:n], in1=t2[:, 0:n], op=ALU.add),
                                      reads=[t1.b, t2.b], writes=[t1.b])
                                fw.op("dve", lambda e: e.tensor_tensor(
                                    out=dst[:, h, off:off + n], in0=t1[:, 0:n], in1=rs[:, 0:n], op=ALU.mult),
                                    reads=[t1.b, rs.b], writes=[dst_b[h]])

                            prev = stage_a(0, *BLKS[0])
                            for k, (off, n) in enumerate(BLKS):
                                nxt = stage_a(k + 1, *BLKS[k + 1]) if k + 1 < len(BLKS) else None
                                stage_b(k, off, n, *prev)
                                prev = nxt
                    wv = T_("wv", [128, KC * 512], BF16, es=es2, dma="pool")
                    fw.dma("pool", wv[:], w_v0, [], [wv.b], wv.sem)
                    for tt_ in (() if "P2bv" in skip else range(18)):
                        kb = 0 if tt_ < 2 else 1 + (tt_ - 2) // 4
                        pst = ps_next()
                        fw.mm(pst[:, 0:512], [(hT[:, kc, tt_ * 128:(tt_ + 1) * 128], wv[:, kc * 512:(kc + 1) * 512]) for kc in range(KC)],
                              reads=[wv.b, hT_b[kb]], writes=[pst.b])
                        fw.op("act", lambda e, pst=pst, tt_=tt_: e.activation(out=Vtok[:, tt_, :], in_=pst[:, 0:512], func=AF.Identity),
                              reads=[pst.b], writes=[Vtok.b])
                    fw.barrier()
                dump_bf16("qT", qT, qT_b, 4, T)
                dump_bf16("kT", kT, kT_b, 4, T)
                with ExitStack() as es2:
                    fw.enabled = on("P3")
                    Er = Ring(fw, es2, "Er", 8, [128, 512], BF16)
                    r1 = T_("r1", [128, 512], F32, es=es2)
                    r2 = T_("r2", [128, 512], F32, es=es2)
                    o1 = T_("o1", [128, 512], F32, es=es2)
                    o2 = T_("o2", [128, 512], F32, es=es2)
                    osq = T_("osq", [128, 512], BF16, es=es2)
                    oln = T_("oln", [128, 512], F32, es=es2)
                    ors = T_("ors", [128, 512], F32, es=es2)
                    STP = [(PS[0], PS[1]), (PS[6], PS[7])]
                    stc = [0]
                    pend = [None]
                    accE = [T_("accE0", [128, 512], F32, es=es2), T_("accE1", [128, 512], F32, es=es2)]
                    ones_f = T_("ones_f", [128, 128], F32, es=es2)
                    fw.op("dve", lambda e: e.memset(ones_f[:], 1.0), writes=[ones_f.b])

                    def epi_a(n):
                        fw.op("dve", lambda e: e.reciprocal(out=r1[:, 0:n], in_=PS[4][:, 0:n]), reads=[PS[4].b], writes=[r1.b])
                        fw.op("dve", lambda e: e.reciprocal(out=r2[:, 0:n], in_=PS[5][:, 0:n]), reads=[PS[5].b], writes=[r2.b])
                        fw.op("dve", lambda e: e.tensor_tensor(out=o1[:, 0:n], in0=PS[2][:, 0:n], in1=r1[:, 0:n], op=ALU.mult),
                              reads=[PS[2].b, r1.b], writes=[o1.b])
                        fw.op("dve", lambda e: e.tensor_tensor(out=o2[:, 0:n], in0=PS[3][:, 0:n], in1=r2[:, 0:n], op=ALU.mult),
                              reads=[PS[3].b, r2.b], writes=[o2.b])
                        fw.op("dve", lambda e: e.scalar_tensor_tensor(out=o1[:, 0:n], in0=o2[:, 0:n], scalar=NEGLAM, in1=o1[:, 0:n],
                                                                      op0=ALU.mult, op1=ALU.add),
                              reads=[o1.b, o2.b, lam_t.b], writes=[o1.b])
                        fw.op("act", lambda e: e.activation(out=osq[:, 0:n], in_=o1[:, 0:n], func=AF.Square), reads=[o1.b], writes=[osq.b])

                    def epi_b(h, off, n):
                        fw.mm(PS[4][:, 0:n], [(ones_h[:], osq[:, 0:n])], reads=[ones_h.b, osq.b], writes=[PS[4].b])
                        fw.op("act", lambda e: e.activation(out=oln[:, 0:n], in_=PS[4][:, 0:n], func=AF.Ln, bias=epsT[:], scale=1.0),
                              reads=[PS[4].b, epsT.b], writes=[oln.b])
                        fw.op("act", lambda e: e.activation(out=ors[:, 0:n], in_=oln[:, 0:n], func=AF.Exp, scale=-0.5),
                              reads=[oln.b], writes=[ors.b])
                        fw.op("dve", lambda e: e.scalar_tensor_tensor(
                            out=yT[:, 4 + h, off:off + n], in0=o1[:, 0:n], scalar=SUBG, in1=ors[:, 0:n], op0=ALU.mult, op1=ALU.mult),
                            reads=[o1.b, lam_t.b, ors.b], writes=[yT_b[4 + h]])

                    for h in range(4):
                        for k, (off, n) in enumerate(BLKS):
                            kts = [0, 1] if k == 0 else list(range(18))
                            sts = {}

                            def emit_st(i, h=h, off=off, n=n, kts=kts, sts=sts):
                                kt = kts[i]
                                pair = STP[stc[0] % 2]
                                stc[0] += 1
                                for m in range(2):
                                    fw.mm(pair[m][:, 0:n], [(kT[m * 64:(m + 1) * 64, h, kt * 128:(kt + 1) * 128], qT[m * 64:(m + 1) * 64, h, off:off + n])],
                                          reads=[kT_b[h], qT_b[h]], writes=[pair[m].b])
                                sts[i] = pair

                            emit_st(0)
                            trig = min(4, len(kts) - 1)
                            for i, kt in enumerate(kts):
                                if i + 1 < len(kts):
                                    emit_st(i + 1)
                                pair = sts.pop(i)
                                ees = []
                                for m in range(2):
                                    ee = Er.next()
                                    fw.op("act", lambda e, ee=ee, st=pair[m], n=n: e.activation(out=ee[:, 0:n], in_=st[:, 0:n], func=AF.Exp, scale=0.125),
                                          reads=[pair[m].b], writes=[ee.b])
                                    ees.append(ee)
                                for m in range(2):
                                    fw.mm(PS[2 + m][:, 0:n], [(Vtok[:, kt, h * 128:(h + 1) * 128], ees[m][:, 0:n])],
                                          reads=[Vtok.b, ees[m].b], writes=[PS[2 + m].b], start=(i == 0), stop=(i == len(kts) - 1))
                                for m in range(2):
                                    fw.mm(PS[4 + m][:, 0:n], [(ones_1[:], ees[m][:, 0:n])],
                                          reads=[ones_1.b, ees[m].b], writes=[PS[4 + m].b], start=(i == 0), stop=(i == len(kts) - 1))
                            epi_a(n)
                            epi_b(h, off, n)
                    if pend[0] is not None:
                        pend[0]()
                        pend[0] = None
                    fw.barrier()
            dump_bf16("yT", yT, yT_b, 8, T)
            fw.enabled = on("P4")
            outproj(0, bi, BLKS, jl, yT, yT_b, wo0)
        fw.enabled = True
        dump_res("xmid0", bi)
        ffn(0, bi, BLKS, jl)
        dump_res("xout0", bi)
        if nlayers == 1:
            continue
        mixer1(bi, jl)
        dump_res("xmid1", bi)
        ffn(1, bi, LATBLKS, jl)

    return finish()


def _chunk_cols(w, cols):
    sel = w[:, cols]
    n = sel.shape[1] // 128
    t = sel.reshape(KC, 128, n, 128).transpose(2, 1, 0, 3)
    return np.ascontiguousarray(t.reshape(n, 128, KC * 128))


def _rope_partner(d):
    return d + 16 if (d % 32) < 16 else d - 16


def _rope_tables():
    inv = (10000.0 ** (-np.arange(16, dtype=np.float32) / 16)).astype(np.float32)
    t = np.arange(NLAT)
    row = (t // 64).astype(np.float32)
    col = (t % 64).astype(np.float32)
    c = np.ones((128, T), np.float32)
    s = np.zeros((128, T), np.float32)
    for p in range(128):
        d = p % 64
        axis, half, f = d // 32, (d % 32) // 16, d % 16
        ang = (row if axis == 0 else col) * inv[f]
        c[p, NCTX:] = np.cos(ang.astype(np.float32))
        sn = np.sin(ang.astype(np.float32))
        s[p, NCTX:] = -sn if half == 0 else sn
    return c, s


def _ret_tables():
    i = np.arange(128, dtype=np.float32)
    posq = np.concatenate([np.tile(i + 1.0, (128, 1)), np.tile(128.0 - i, (128, 1))], axis=1)
    kpos = np.stack([127.0 - i, i], axis=1)
    jj = i[:, None]
    ii = i[None, :]
    dpos = np.maximum(ii - jj, 0.0)
    dneg = np.maximum(jj - ii, 0.0)
    lge = (ii >= jj).astype(np.float32)
    ule = (jj >= ii).astype(np.float32)
    return np.ascontiguousarray(np.concatenate([posq, kpos, dpos, dneg, lge, ule, np.eye(128, dtype=np.float32)], axis=1).astype(np.float32))


def _pool_corr():
    out = np.ones((128, 4, 2, 8), np.float32)
    for g, r in enumerate((1, 2, 4, 8)):
        for t in range(r):
            cnt = (t + r) - 0 + 1
            out[:, g, 0, t] = (2 * r + 1) / cnt
            out[:, g, 1, 7 - t] = (2 * r + 1) / cnt
    return out.reshape(128, 64)


def prepare_inputs(inputs):
    f = lambda a: np.ascontiguousarray(np.asarray(a, dtype=np.float32))
    x, c, ctx, c_ctx = f(inputs["x"]), f(inputs["c"]), f(inputs["ctx"]), f(inputs["c_ctx"])
    sh = {}
    mod_w = f(inputs["mod_w"])
    sh["modw"] = np.ascontiguousarray(
        mod_w.reshape(2, KC, 128, 12, 4, 128).transpose(0, 3, 2, 4, 1, 5).reshape(2, 12, 128, 4 * KC * 128))
    sh["modb"] = np.ascontiguousarray(f(inputs["mod_b"]).reshape(2, 48, 128).transpose(2, 0, 1).reshape(128, 96))
    n1, n2 = f(inputs["norm1_g"]), f(inputs["norm2_g"])
    ngs = np.stack([n1, n2], axis=1)
    sh["ng"] = np.ascontiguousarray(ngs.reshape(2, 2, KC, 128).transpose(3, 0, 1, 2).reshape(128, 32))
    w0 = f(inputs["ev_w_in"])[0]
    partner = np.array([_rope_partner(d) for d in range(64)])
    cols = list(range(0, 1536))
    qcols = np.arange(1536, 2048)
    kcols = np.arange(2048, 2560)
    perm512 = np.concatenate([g * 64 + partner for g in range(8)])
    allcols = np.concatenate([np.arange(0, 1536), qcols, qcols[0] + perm512, kcols, kcols[0] + perm512])
    sh["w_in0"] = _chunk_cols(w0, allcols)
    sh["w_v0"] = np.ascontiguousarray(w0[:, 2560:3072].reshape(KC, 128, 512).transpose(1, 0, 2).reshape(128, KC * 512))
    wo = np.stack([f(inputs["ev_w_out"])[0], f(inputs["od_w_out"])[0]])
    sh["w_out"] = np.ascontiguousarray(wo.reshape(2, KC, 128, 1024).transpose(0, 2, 1, 3).reshape(2, 128, KC * 1024))
    sh["scw"] = np.ascontiguousarray(f(inputs["sc_conv_w"])[0].reshape(3, 4, 128).transpose(2, 1, 0).reshape(128, 12))
    gq, gk = f(inputs["da_q_norm"])[0], f(inputs["da_k_norm"])[0]
    pidx = np.arange(128) % 64
    sh["qkg"] = np.ascontiguousarray(np.stack([gq[pidx], gq[partner[pidx]], gk[pidx], gk[partner[pidx]]], axis=1))
    lv = np.stack([f(inputs["da_lq1"])[0], f(inputs["da_lk1"])[0], f(inputs["da_lq2"])[0], f(inputs["da_lk2"])[0]])
    sh["lamv"] = np.ascontiguousarray(np.broadcast_to(lv.reshape(1, 256), (128, 256)))
    sh["subg"] = np.ascontiguousarray(f(inputs["da_subln_g"])[0].reshape(128, 1))
    rc, rs = _rope_tables()
    sh["ropec"], sh["ropes"] = rc, rs
    wu = f(inputs["ffn_w_up"])
    wu = wu.reshape(2, KC, 128, 2, FJ, 128).transpose(0, 4, 2, 3, 1, 5)
    sh["w_up"] = np.ascontiguousarray(wu.reshape(2, FJ, 128, 2 * KC * 128))
    wd = f(inputs["ffn_w_down"])
    sh["w_dn"] = np.ascontiguousarray(wd.reshape(2, FJ, 128, 1024))
    sh["fcw"] = np.ascontiguousarray(f(inputs["ffn_conv_w"]).reshape(2, 3, FJ, 128).transpose(3, 0, 2, 1).reshape(128, 132))
    sh["fcb"] = np.ascontiguousarray(f(inputs["ffn_conv_b"]).reshape(2, FJ, 128).transpose(2, 0, 1).reshape(128, 44))
    w1 = f(inputs["od_w_in"])[0]
    partner256 = np.concatenate([g * 64 + partner for g in range(4)])
    cols1 = np.concatenate([np.arange(0, 512), np.arange(512, 768), 512 + partner256, np.arange(768, 1024),
                            768 + partner256, np.arange(1536, 2048)])
    sh["w_in1"] = _chunk_cols(w1, cols1)
    sh["w_v1"] = np.ascontiguousarray(w1[:, 1024:1536].reshape(KC, 128, 512).transpose(1, 0, 2).reshape(128, KC * 512))
    sh["poolw"] = np.ascontiguousarray(f(inputs["pool_w"])[0].transpose(1, 0, 2).reshape(128, 512))
    sh["pools"] = np.ascontiguousarray(f(inputs["pool_scale"])[0].reshape(4, 128).T)
    df, db = f(inputs["ret_decay_f"])[0], f(inputs["ret_decay_b"])[0]
    hp_of = np.arange(128) // 64
    decp = np.zeros((128, 2, 2), np.float32)
    for hp in range(2):
        decp[:, 0, hp] = df[2 * hp + hp_of]
        decp[:, 1, hp] = db[2 * hp + hp_of]
    sh["decp"] = decp.reshape(128, 4)
    sh["decr"] = np.ascontiguousarray(np.broadcast_to(np.concatenate([df, db]).reshape(1, 8), (128, 8)))
    sh["gng"] = np.ascontiguousarray(f(inputs["ret_gn_g"])[0].reshape(128, 1))
    sh["rtab"] = _ret_tables()
    sh["pcorr"] = _pool_corr()
    per_core = []
    for i in range(8):
        b0 = 2 * i
        d = dict(sh)
        d["xT"] = np.ascontiguousarray(x[b0:b0 + 2].transpose(0, 2, 1))
        d["ctxT"] = np.ascontiguousarray(ctx[b0:b0 + 2].transpose(0, 2, 1))
        cs = np.stack([c[b0], c[b0 + 1], c_ctx])
        d["cT"] = np.ascontiguousarray(cs.reshape(3, KC, 128).transpose(2, 1, 0).reshape(128, KC * 3))
        per_core.append(d)
    return per_core


def kernel(**inputs):
    per_core = prepare_inputs(inputs)
    nc = build_program()
    res = run_bass_kernel_spmd(nc, per_core, core_ids=list(range(8)))
    out = np.empty((16, NLAT, D), np.float32)
    for i in range(8):
        o = res.results[i]["outT"]
        out[2 * i:2 * i + 2] = o.transpose(0, 2, 1)
    return out
```

```python
import math
from contextlib import ExitStack

import numpy as np
import concourse.bass as bass
import concourse.mybir as mybir
from concourse.bass_utils import run_bass_kernel_spmd

F32 = mybir.dt.float32
BF16 = mybir.dt.bfloat16
AF = mybir.ActivationFunctionType
ALU = mybir.AluOpType

D = 1024
KC = 8
NCTX = 256
NLAT = 2048
T = NCTX + NLAT
PT = T + 3
BLKS = [(0, 256), (256, 512), (768, 512), (1280, 512), (1792, 512)]
LATBLKS = BLKS[1:]
FF = 2816
FJ = 22
EPS = 1e-6
NB = 2


def poff(off):
    return off + 1 if off < NCTX else off + 2


class Sem:
    def __init__(self, nc, name):
        self.h = nc.alloc_semaphore(name=name)
        self.cnt = 0
        self.name = name


class Buf:
    def __init__(self, name="", excl=False):
        self.w = {}
        self.r = {}
        self.name = name
        self.excl = excl


class Eng:
    def __init__(self, nc, name, obj, is_pe=False):
        self.obj = obj
        self.sem = Sem(nc, "e_" + name)
        self.waited = {}
        self.is_pe = is_pe
        self.name = name


class FW:
    def __init__(self, nc):
        self.nc = nc
        self.E = {
            "pe": Eng(nc, "pe", nc.tensor, True),
            "act": Eng(nc, "act", nc.scalar),
            "dve": Eng(nc, "dve", nc.vector),
            "pool": Eng(nc, "pool", nc.gpsimd),
            "sp": Eng(nc, "sp", nc.sync),
        }
        self.enabled = True
        self.dsems = []
        self.free_sems = {"sp": [], "pool": []}
        self.nsem = 0

    def dsem(self, q="sp"):
        if self.free_sems[q]:
            return self.free_sems[q].pop()
        self.nsem += 1
        x = Sem(self.nc, f"d{q}{self.nsem}")
        x.q = q
        self.dsems.append(x)
        return x

    def _wait(self, E, reads, writes, skip_sem=None):
        deps = {}
        for b in reads:
            for k, v in b.w.items():
                if v > deps.get(k, 0):
                    deps[k] = v
            if b.excl:
                for k, v in b.r.items():
                    if k is not E.sem and v > deps.get(k, 0):
                        deps[k] = v
        for b in writes:
            for k, v in b.w.items():
                if v > deps.get(k, 0):
                    deps[k] = v
            for k, v in b.r.items():
                if v > deps.get(k, 0):
                    deps[k] = v
        for k, v in deps.items():
            if k is skip_sem:
                continue
            if E.is_pe and k is E.sem:
                continue
            if E.waited.get(k, 0) >= v:
                continue
            E.obj.wait_ge(k.h, v)
            E.waited[k] = v

    def op(self, eng, fn, reads=(), writes=()):
        if not self.enabled:
            return None
        E = self.E[eng]
        self._wait(E, reads, writes)
        ins = fn(E.obj)
        E.sem.cnt += 1
        ins.then_inc(E.sem.h, 1)
        c = E.sem.cnt
        for b in reads:
            b.r[E.sem] = c
        for b in writes:
            b.w = {E.sem: c}
            b.r = {}
        return ins

    def mm(self, out_ap, pairs, reads, writes, start=True, stop=True):
        if not self.enabled:
            return
        E = self.E["pe"]
        self._wait(E, reads, writes)
        n = len(pairs)
        ins = None
        for i, (l, r) in enumerate(pairs):
            ins = E.obj.matmul(out_ap, l, r, start=(start and i == 0), stop=(stop and i == n - 1))
        E.sem.cnt += 1
        ins.then_inc(E.sem.h, 1)
        c = E.sem.cnt
        for b in reads:
            b.r[E.sem] = c
        for b in writes:
            b.w = {E.sem: c}
            b.r = {}

    def dma(self, q, out_ap, in_ap, reads, writes, sem, merge=False):
        if not self.enabled:
            return
        E = self.E[q]
        assert getattr(sem, "q", q) == q, (sem.name, q)
        self._wait(E, reads, writes, skip_sem=sem if merge else None)
        ins = E.obj.dma_start(out=out_ap, in_=in_ap)
        sem.cnt += 16
        ins.then_inc(sem.h, 16)
        for b in reads:
            b.r[sem] = sem.cnt
        for b in writes:
            if merge:
                b.w[sem] = sem.cnt
            else:
                b.w = {sem: sem.cnt}
                b.r = {}

    def barrier(self):
        allsems = [E.sem for E in self.E.values()] + self.dsems
        for E in self.E.values():
            for k in allsems:
                if k is E.sem:
                    continue
                if k.cnt > E.waited.get(k, 0):
                    E.obj.wait_ge(k.h, k.cnt)
                    E.waited[k] = k.cnt

    def final_wait(self, sems):
        E = self.E["sp"]
        for k in sems:
            if k.cnt > E.waited.get(k, 0):
                E.obj.wait_ge(k.h, k.cnt)
                E.waited[k] = k.cnt


class Tile:
    def __init__(self, fw, es, name, shape, dt, psum=False, dma=False, dma2=False):
        nc = fw.nc
        fw.ntile = getattr(fw, "ntile", 0) + 1
        name = f"{name}_{fw.ntile}"
        if psum:
            self.t = es.enter_context(nc.psum_tensor(name, shape, dt))
        else:
            self.t = es.enter_context(nc.sbuf_tensor(name, shape, dt))
        self.b = Buf(name, excl=psum)
        self.sem = None
        self.sem2 = None
        if dma:
            q = dma if isinstance(dma, str) else "sp"
            self.sem = fw.dsem(q)
            es.callback(fw.free_sems[q].append, self.sem)
        if dma2:
            self.sem2 = fw.dsem("sp")
            es.callback(fw.free_sems["sp"].append, self.sem2)

    def __getitem__(self, idx):
        return self.t[idx]


class Ring:
    def __init__(self, fw, es, name, n, shape, dt, psum=False, dma=False, dma2=False):
        self.tiles = [Tile(fw, es, f"{name}{i}", shape, dt, psum=psum, dma=dma, dma2=dma2) for i in range(n)]
        self.i = 0

    def next(self):
        t = self.tiles[self.i % len(self.tiles)]
        self.i += 1
        return t


def build_program(cfg=None):
    cfg = cfg or {}
    nlayers = cfg.get("nlayers", 2)
    nb = cfg.get("nb", NB)
    stop_after = cfg.get("stop_after", None)
    only = cfg.get("only", None)

    def on(ph):
        return only is None or ph in only
    skip = set(cfg.get("skip", []))
    nc = bass.Bass("TRN2", target_bir_lowering=False)
    fw = FW(nc)
    ES = ExitStack()

    def din(name, shape, dt=F32):
        return nc.dram_tensor(name, list(shape), dt, kind="ExternalInput").ap()

    xT = din("xT", [NB, D, NLAT])
    ctxT = din("ctxT", [NB, D, NCTX])
    cT = din("cT", [128, KC * 3])
    modw = din("modw", [2, 12, 128, 4 * KC * 128])
    modb = din("modb", [128, 2 * 48])
    ng = din("ng", [128, 2 * 2 * KC])
    w_in0 = din("w_in0", [28, 128, KC * 128])
    w_v0 = din("w_v0", [128, KC * 512])
    w_out = din("w_out", [2, 128, KC * 1024])
    scw = din("scw", [128, 4 * 3])
    qkg = din("qkg", [128, 4])
    lamv = din("lamv", [128, 4 * 64])
    subg = din("subg", [128, 1])
    ropec = din("ropec", [128, T])
    ropes = din("ropes", [128, T])
    w_up = din("w_up", [2, FJ, 128, 2 * KC * 128])
    w_dn = din("w_dn", [2, FJ, 128, 1024])
    fcw = din("fcw", [128, 2 * FJ * 3])
    fcb = din("fcb", [128, 2 * FJ])
    w_in1 = din("w_in1", [16, 128, KC * 128])
    w_v1 = din("w_v1", [128, KC * 512])
    poolw = din("poolw", [128, 4 * 128])
    pools = din("pools", [128, 4])
    decp = din("decp", [128, 4])
    decr = din("decr", [128, 8])
    gng = din("gng", [128, 1])
    rtab = din("rtab", [128, 2 * 128 + 2 + 5 * 128])
    pcorr = din("pcorr", [128, 4 * 2 * 8])
    outT = nc.dram_tensor("outT", [NB, D, NLAT], F32, kind="ExternalOutput").ap()
    dbg = None
    if cfg.get("dbg"):
        dbg = nc.dram_tensor("dbg", list(cfg["dbg"]), F32, kind="ExternalOutput").ap()
    resT = nc.dram_tensor("resT", [NB, KC, 128, T], F32).ap()
    uT = nc.dram_tensor("uT", [NB, FJ, 128, T], BF16).ap()
    res_b = [[Buf(f"res{bi}_{k}") for k in range(len(BLKS))] for bi in range(NB)]
    u_b = [[Buf(f"u{bi}_{k}") for k in range(len(BLKS))] for bi in range(NB)]
    out_sems = []

    def T_(name, shape, dt, es=ES, **kw):
        return Tile(fw, es, name, shape, dt, **kw)

    PS = [T_(f"ps{i}", [128, 512], F32, psum=True) for i in range(8)]
    psi = [0]

    def ps_next():
        t = PS[psi[0] % 8]
        psi[0] += 1
        return t

    hT = T_("hT", [128, KC, T], BF16)
    hT_b = [Buf(f"hT{k}") for k in range(len(BLKS))]
    ones_n = T_("ones_n", [128, 128], BF16)
    ones_h = T_("ones_h", [128, 128], BF16)
    ones_1 = T_("ones_1", [128, 128], BF16)
    ones_g = T_("ones_g", [128, 128], BF16)
    epsT = T_("epsT", [128, 1], F32)
    cst = T_("cst", [128, KC * 3], F32, dma=True)
    sct = T_("sct", [128, KC * 3], F32)
    modT = T_("modT", [128, 2 * 48 * 3], F32)
    modbT = T_("modbT", [128, 96], F32, dma=True)
    ngT = T_("ngT", [128, 32], F32, dma=True)
    AT = T_("AT", [128, 2 * 2 * KC * 3], F32)
    smallc = T_("smallc", [128, 12 + 4 + 256 + 1 + 132 + 44], F32, dma=True)
    o_scw, o_qkg, o_lamv, o_subg, o_fcw, o_fcb = 0, 12, 16, 272, 273, 405
    lam_t = T_("lam_t", [128, 8], F32)

    def C(o, n=1):
        return smallc[:, o:o + n]

    def mod_ap(l, grp, kc, j):
        i = (l * 48 + grp * 8 + kc) * 3 + j
        return modT[:, i:i + 1]

    def A_ap(l, n, kc, j):
        i = ((l * 2 + n) * KC + kc) * 3 + j
        return AT[:, i:i + 1]

    fw.op("dve", lambda e: e.memset(ones_n[:], 1.0 / 1024.0), writes=[ones_n.b])
    fw.op("dve", lambda e: e.memset(ones_h[:], 1.0 / 128.0), writes=[ones_h.b])
    fw.op("dve", lambda e: e.memset(ones_1[:], 1.0), writes=[ones_1.b])
    fw.op("dve", lambda e: e.memset(ones_g[:], 0.0), writes=[ones_g.b])
    fw.op("dve", lambda e: e.memset(ones_g[0:64, 0:64], 1.0 / 64.0), writes=[ones_g.b])
    fw.op("dve", lambda e: e.memset(ones_g[64:128, 64:128], 1.0 / 64.0), writes=[ones_g.b])
    fw.op("dve", lambda e: e.memset(epsT[:], EPS), writes=[epsT.b])
    fw.dma("sp", cst[:], cT, [], [cst.b], cst.sem)
    fw.dma("sp", modbT[:], modb, [], [modbT.b], modbT.sem)
    fw.dma("sp", ngT[:], ng, [], [ngT.b], ngT.sem)
    for (o, src, n) in [(o_scw, scw, 12), (o_qkg, qkg, 4), (o_lamv, lamv, 256), (o_subg, subg, 1),
                        (o_fcw, fcw, 132), (o_fcb, fcb, 44)]:
        fw.dma("sp", smallc[:, o:o + n], src, [], [smallc.b], smallc.sem, merge=True)

    fw.op("act", lambda e: e.activation(out=sct[:], in_=cst[:], func=AF.Silu), reads=[cst.b], writes=[sct.b])
    with ExitStack() as es:
        mw = Ring(fw, es, "mw", 3, [128, 4 * KC * 128], BF16, dma="pool")
        sctb = T_("sctb", [128, KC * 3], BF16, es=es)
        fw.op("dve", lambda e: e.tensor_copy(out=sctb[:], in_=sct[:]), reads=[sct.b], writes=[sctb.b])
        for l in range(2):
            pst = ps_next()
            for og in range(12):
                w = mw.next()
                fw.dma("pool", w[:], modw[l, og], [], [w.b], w.sem)
                for o4 in range(4):
                    oc = og * 4 + o4
                    fw.mm(pst[:, oc * 3:oc * 3 + 3],
                          [(w[:, (o4 * KC + kc) * 128:(o4 * KC + kc + 1) * 128], sctb[:, kc * 3:kc * 3 + 3]) for kc in range(KC)],
                          reads=[w.b, sctb.b], writes=[pst.b])
            for j in range(3):
                fw.op("dve", lambda e, l=l, j=j, pst=pst: e.tensor_tensor(
                    out=modT[:, l * 144:(l + 1) * 144].rearrange("p (o j) -> p o j", j=3)[:, :, j],
                    in0=pst[:, 0:144].rearrange("p (o j) -> p o j", j=3)[:, :, j],
                    in1=modbT[:, l * 48:(l + 1) * 48], op=ALU.add),
                    reads=[pst.b, modbT.b], writes=[modT.b])
        for l in range(2):
            for n in range(2):
                grp = 1 if n == 0 else 4
                for j in range(3):
                    base_m = (l * 48 + grp * 8) * 3
                    base_a = ((l * 2 + n) * KC) * 3
                    fw.op("dve", lambda e, base_m=base_m, base_a=base_a, j=j, l=l, n=n: e.scalar_tensor_tensor(
                        out=AT[:, base_a:base_a + 24].rearrange("p (k j) -> p k j", j=3)[:, :, j],
                        in0=modT[:, base_m:base_m + 24].rearrange("p (k j) -> p k j", j=3)[:, :, j],
                        scalar=1.0,
                        in1=ngT[:, (l * 2 + n) * KC:(l * 2 + n + 1) * KC],
                        op0=ALU.add, op1=ALU.mult),
                        reads=[modT.b, ngT.b], writes=[AT.b])
        fw.barrier()

    def dbg_dump(tile_ap, rd, rows, cols, row0=0):
        s = fw.dsem()
        fw.dma("sp", dbg[row0:row0 + rows, 0:cols], tile_ap, rd, [], s)
        out_sems.append(s)

    def finish():
        fw.barrier()
        fw.final_wait(out_sems)
        ES.close()
        return nc

    if stop_after == "M":
        dbg_dump(modT[:], [modT.b], 128, 288)
        dbg_dump(AT[:], [AT.b], 128, 96, row0=128)
        return finish()

    def norm_block(es_t, x_t, n, off, k, l, nidx, j):
        sq, lnb, rs, tt = es_t["sq"], es_t["lnb"], es_t["rs"], es_t["tt"]
        fw.op("act", lambda e: e.activation(out=sq[:, :, 0:n], in_=x_t[:, :, 0:n], func=AF.Square),
              reads=[x_t.b], writes=[sq.b])
        pst = ps_next()
        fw.mm(pst[:, 0:n], [(ones_n[:], sq[:, kc, 0:n]) for kc in range(KC)],
              reads=[ones_n.b, sq.b], writes=[pst.b])
        fw.op("act", lambda e: e.activation(out=lnb[:, 0:n], in_=pst[:, 0:n], func=AF.Ln, bias=epsT[:], scale=1.0),
              reads=[pst.b, epsT.b], writes=[lnb.b])
        fw.op("act", lambda e: e.activation(out=rs[:, 0:n], in_=lnb[:, 0:n], func=AF.Exp, scale=-0.5),
              reads=[lnb.b], writes=[rs.b])
        for kc in range(KC):
            t = tt.next()
            fw.op("dve", lambda e, kc=kc, t=t: e.scalar_tensor_tensor(
                out=t[:, 0:n], in0=x_t[:, kc, 0:n], scalar=A_ap(l, nidx, kc, j), in1=rs[:, 0:n],
                op0=ALU.mult, op1=ALU.mult), reads=[x_t.b, AT.b, rs.b], writes=[t.b])
            grp = 0 if nidx == 0 else 3
            fw.op("act", lambda e, kc=kc, t=t: e.activation(
                out=hT[:, kc, off:off + n], in_=t[:, 0:n], func=AF.Identity, bias=mod_ap(l, grp, kc, j), scale=1.0),
                reads=[t.b, modT.b], writes=[hT_b[k]])

    def norm_tiles(es, pfx):
        return {
            "sq": T_(pfx + "sq", [128, KC, 512], BF16, es=es),
            "lnb": T_(pfx + "lnb", [128, 512], F32, es=es),
            "rs": T_(pfx + "rs", [128, 512], F32, es=es),
            "tt": Ring(fw, es, pfx + "tt", 2, [128, 512], F32),
        }

    def x_src(l, bi, k):
        off, n = BLKS[k]
        if l == 0:
            if k == 0:
                return ctxT[bi].rearrange("(c p) t -> p c t", p=128), None
            return xT[bi].rearrange("(c p) t -> p c t", p=128)[:, :, off - NCTX:off - NCTX + n], None
        return resT[bi].rearrange("c p t -> p c t")[:, :, off:off + n], res_b[bi][k]

    LAM0 = 0.8 - 0.6 * math.exp(-0.3 * 0)
    with ExitStack() as es:
        ltmp = T_("ltmp", [128, 128], F32, es=es)
        fw.op("dve", lambda e: e.tensor_tensor(out=ltmp[:, 0:64], in0=C(o_lamv, 64), in1=C(o_lamv + 64, 64), op=ALU.mult),
              reads=[smallc.b], writes=[ltmp.b])
        fw.op("dve", lambda e: e.tensor_tensor(out=ltmp[:, 64:128], in0=C(o_lamv + 128, 64), in1=C(o_lamv + 192, 64), op=ALU.mult),
              reads=[smallc.b], writes=[ltmp.b])
        fw.op("dve", lambda e: e.reduce_sum(out=lam_t[:, 0:1], in_=ltmp[:, 0:64], axis=mybir.AxisListType.X),
              reads=[ltmp.b], writes=[lam_t.b])
        fw.op("dve", lambda e: e.reduce_sum(out=lam_t[:, 1:2], in_=ltmp[:, 64:128], axis=mybir.AxisListType.X),
              reads=[ltmp.b], writes=[lam_t.b])
        fw.op("act", lambda e: e.activation(out=lam_t[:, 2:4], in_=lam_t[:, 0:2], func=AF.Exp), reads=[lam_t.b], writes=[lam_t.b])
        fw.op("dve", lambda e: e.tensor_tensor(out=lam_t[:, 4:5], in0=lam_t[:, 3:4], in1=lam_t[:, 2:3], op=ALU.subtract),
              reads=[lam_t.b], writes=[lam_t.b])
        fw.op("dve", lambda e: e.tensor_scalar(out=lam_t[:, 4:5], in0=lam_t[:, 4:5], scalar1=-LAM0, scalar2=None, op0=ALU.add),
              reads=[lam_t.b], writes=[lam_t.b])
        fw.op("dve", lambda e: e.tensor_scalar(out=lam_t[:, 5:6], in0=C(o_subg), scalar1=1.0 - LAM0, scalar2=None, op0=ALU.mult),
              reads=[smallc.b, lam_t.b], writes=[lam_t.b])
        fw.barrier()
    NEGLAM = lam_t[:, 4:5]
    SUBG = lam_t[:, 5:6]

    dumps = cfg.get("dumps", {})

    def dump_bf16(name, src3, rd, nch, cols):
        if name not in dumps:
            return
        fw.barrier()
        with ExitStack() as es:
            dt_ = T_("dbg_" + name, [128, cols], F32, es=es)
            for ch in range(nch):
                fw.op("dve", lambda e, ch=ch: e.tensor_copy(out=dt_[:], in_=src3[:, ch, :]), reads=rd, writes=[dt_.b])
                dbg_dump(dt_[:], [dt_.b], 128, cols, row0=dumps[name] + ch * 128)
            fw.barrier()

    def dump_res(name, bi):
        if name not in dumps:
            return
        fw.barrier()
        s = fw.dsem()
        r0 = dumps[name]
        fw.dma("sp", dbg[r0:r0 + 1024, 0:T], resT[bi].rearrange("c p t -> (c p) t"), res_b[bi], [], s)
        out_sems.append(s)
        fw.barrier()

    def ffn(l, bi, blks, jl):
        eso = ExitStack()
        fw.enabled = on("F2")
        wd = T_("wd", [128, FJ * 1024], BF16, es=eso, dma="pool")
        for jf in range(FJ):
            fw.dma("pool", wd[:, jf * 1024:(jf + 1) * 1024], w_dn[l, jf], [], [wd.b], wd.sem, merge=True)
        with ExitStack() as es:
            fw.enabled = on("F1")
            wab = Ring(fw, es, "wab", 3, [128, 2 * KC * 128], BF16, dma="pool")
            abuf = Ring(fw, es, "abuf", 2, [128, PT], F32)
            bbuf = Ring(fw, es, "bbuf", 2, [128, PT], F32)
            cbuf = T_("cbuf", [128, PT], F32, es=es)
            ubuf = Ring(fw, es, "ubuf", 2, [128, T], BF16, dma=True)
            for a in abuf.tiles:
                fw.op("dve", lambda e, a=a: e.memset(a[:], 0.0), writes=[a.b])
            lo = 1 if blks[0][0] == 0 else NCTX + 2
            hi = PT - 1
            for jf in range(FJ):
                w = wab.next()
                fw.dma("pool", w[:], w_up[l, jf], [], [w.b], w.sem)
                ab = abuf.next()
                bb = bbuf.next()
                for (off, n) in blks:
                    k = BLKS.index((off, n))
                    po = poff(off)
                    pa = ps_next()
                    fw.mm(pa[:, 0:n], [(w[:, kc * 128:(kc + 1) * 128], hT[:, kc, off:off + n]) for kc in range(KC)],
                          reads=[w.b, hT_b[k]], writes=[pa.b])
                    pb = ps_next()
                    fw.mm(pb[:, 0:n], [(w[:, 1024 + kc * 128:1024 + (kc + 1) * 128], hT[:, kc, off:off + n]) for kc in range(KC)],
                          reads=[w.b, hT_b[k]], writes=[pb.b])
                    fw.op("act", lambda e, pa=pa, ab=ab, po=po, n=n: e.activation(out=ab[:, po:po + n], in_=pa[:, 0:n], func=AF.Identity),
                          reads=[pa.b], writes=[ab.b])
                    fw.op("act", lambda e, pb=pb, bb=bb, po=po, n=n: e.activation(out=bb[:, po:po + n], in_=pb[:, 0:n], func=AF.Identity),
                          reads=[pb.b], writes=[bb.b])
                wbase = o_fcw + (l * FJ + jf) * 3
                fw.op("dve", lambda e, ab=ab, wbase=wbase: e.tensor_scalar(
                    out=cbuf[:, lo:hi], in0=ab[:, lo - 1:hi - 1], scalar1=C(wbase), scalar2=None, op0=ALU.mult),
                    reads=[ab.b, smallc.b], writes=[cbuf.b])
                fw.op("dve", lambda e, ab=ab, wbase=wbase: e.scalar_tensor_tensor(
                    out=cbuf[:, lo:hi], in0=ab[:, lo:hi], scalar=C(wbase + 1), in1=cbuf[:, lo:hi], op0=ALU.mult, op1=ALU.add),
                    reads=[ab.b, smallc.b, cbuf.b], writes=[cbuf.b])
                fw.op("dve", lambda e, ab=ab, wbase=wbase: e.scalar_tensor_tensor(
                    out=cbuf[:, lo:hi], in0=ab[:, lo + 1:hi + 1], scalar=C(wbase + 2), in1=cbuf[:, lo:hi], op0=ALU.mult, op1=ALU.add),
                    reads=[ab.b, smallc.b, cbuf.b], writes=[cbuf.b])
                if "F1silu" not in skip:
                    fw.op("act", lambda e, jf=jf: e.activation(out=cbuf[:, lo:hi], in_=cbuf[:, lo:hi], func=AF.Silu,
                                                               bias=C(o_fcb + l * FJ + jf), scale=1.0),
                          reads=[cbuf.b, smallc.b], writes=[cbuf.b])
                u = ubuf.next()
                if lo == 1:
                    fw.op("dve", lambda e, u=u, bb=bb: e.tensor_tensor(out=u[:, 0:NCTX], in0=cbuf[:, 1:1 + NCTX], in1=bb[:, 1:1 + NCTX], op=ALU.mult),
                          reads=[cbuf.b, bb.b], writes=[u.b])
                fw.op("dve", lambda e, u=u, bb=bb: e.tensor_tensor(out=u[:, NCTX:T], in0=cbuf[:, NCTX + 2:PT - 1], in1=bb[:, NCTX + 2:PT - 1], op=ALU.mult),
                      reads=[cbuf.b, bb.b], writes=[u.b])
                t0 = 0 if lo == 1 else NCTX
                if "F1store" not in skip:
                    fw.dma("sp", uT[bi, jf][:, t0:T], u[:, t0:T], [u.b], [u_b[bi][BLKS.index(b)] for b in blks], u.sem, merge=True)
            fw.barrier()
        with ExitStack() as es:
            fw.enabled = on("F2")
            us = Ring(fw, es, "us", 2, [128, FJ, 512], BF16, dma=True)
            xs = Ring(fw, es, "f2x", 2, [128, KC, 512], F32, dma=True, dma2=True)
            nt = norm_tiles(es, "f2") if l == 0 else None
            pendn = [None]
            for (off, n) in blks:
                k = BLKS.index((off, n))
                j = 2 if k == 0 else jl
                u_t = us.next()
                fw.dma("sp", u_t[:, :, 0:n], uT[bi].rearrange("j p t -> p j t")[:, :, off:off + n], [u_b[bi][k]], [u_t.b], u_t.sem)
                x_t = xs.next()
                fw.dma("sp", x_t[:, :, 0:n], resT[bi].rearrange("c p t -> p c t")[:, :, off:off + n], [res_b[bi][k]], [x_t.b], x_t.sem)
                for oc in range(KC):
                    pst = ps_next()
                    fw.mm(pst[:, 0:n], [(wd[:, jf * 1024 + oc * 128:jf * 1024 + (oc + 1) * 128], u_t[:, jf, 0:n]) for jf in range(FJ)],
                          reads=[wd.b, u_t.b], writes=[pst.b])
                    fw.op("dve", lambda e, pst=pst, oc=oc, x_t=x_t, n=n, j=j: e.scalar_tensor_tensor(
                        out=x_t[:, oc, 0:n], in0=pst[:, 0:n], scalar=mod_ap(l, 5, oc, j), in1=x_t[:, oc, 0:n], op0=ALU.mult, op1=ALU.add),
                        reads=[pst.b, modT.b, x_t.b], writes=[x_t.b])
                if l == 0:
                    fw.dma("sp", resT[bi].rearrange("c p t -> p c t")[:, :, off:off + n], x_t[:, :, 0:n], [x_t.b], [res_b[bi][k]], x_t.sem2, merge=True)
                    if pendn[0] is not None:
                        pendn[0]()
                    pendn[0] = (lambda x_t=x_t, n=n, off=off, k=k, j=j: norm_block(nt, x_t, n, off, k, 1, 0, j))
                else:
                    fw.dma("sp", outT[bi].rearrange("(c p) t -> p c t", p=128)[:, :, off - NCTX:off - NCTX + n], x_t[:, :, 0:n],
                           [x_t.b], [], x_t.sem2, merge=True)
                    if x_t.sem2 not in out_sems:
                        out_sems.append(x_t.sem2)
            if pendn[0] is not None:
                pendn[0]()
            fw.barrier()
        eso.close()
        fw.enabled = True

    def load_wo(l, es):
        wo = T_("wo", [128, KC * 1024], BF16, es=es, dma="pool")
        fw.dma("pool", wo[:], w_out[l], [], [wo.b], wo.sem)
        return wo

    def outproj(l, bi, blks, jl, yT, yT_b, wo):
        with ExitStack() as es:
            xs = Ring(fw, es, "p4x", 2, [128, KC, 512], F32, dma=True, dma2=True)
            nt = norm_tiles(es, "p4")
            pendn = [None]
            for (off, n) in blks:
                k = BLKS.index((off, n))
                j = 2 if k == 0 else jl
                x_t = xs.next()
                src, sb = x_src(l, bi, k)
                fw.dma("sp", x_t[:, :, 0:n], src, [sb] if sb else [], [x_t.b], x_t.sem)
                for oc in range(KC):
                    pst = ps_next()
                    fw.mm(pst[:, 0:n], [(wo[:, kc * 1024 + oc * 128:kc * 1024 + (oc + 1) * 128], yT[:, kc, off:off + n]) for kc in range(KC)],
                          reads=[wo.b] + yT_b, writes=[pst.b])
                    fw.op("dve", lambda e, pst=pst, oc=oc, x_t=x_t, n=n, j=j: e.scalar_tensor_tensor(
                        out=x_t[:, oc, 0:n], in0=pst[:, 0:n], scalar=mod_ap(l, 2, oc, j), in1=x_t[:, oc, 0:n], op0=ALU.mult, op1=ALU.add),
                        reads=[pst.b, modT.b, x_t.b], writes=[x_t.b])
                fw.dma("sp", resT[bi].rearrange("c p t -> p c t")[:, :, off:off + n], x_t[:, :, 0:n], [x_t.b], [res_b[bi][k]], x_t.sem2, merge=True)
                if pendn[0] is not None:
                    pendn[0]()
                pendn[0] = (lambda x_t=x_t, n=n, off=off, k=k, j=j: norm_block(nt, x_t, n, off, k, l, 1, j))
            if pendn[0] is not None:
                pendn[0]()
            fw.barrier()

    def mixer1(bi, jl):
        RT = 2 * 128 + 2 + 5 * 128
        o_pq0, o_pq1, o_kpos, o_dpos, o_dneg, o_lge, o_ule, o_id = 0, 128, 256, 258, 386, 514, 642, 770
        with ExitStack() as esm:
            yT = T_("yT1", [128, KC, T], BF16, es=esm)
            yT_b = [Buf(f"yT1_{c}") for c in range(KC)]
            wch = Ring(fw, esm, "wch1", 3, [128, KC * 128], BF16, dma="pool")
            with ExitStack() as es:
                NE = NLAT + 16
                pv = T_("pv", [128, NE], F32, es=es)
                A = [T_("pA0", [128, NE], F32, es=es), T_("pA1", [128, NE], F32, es=es)]
                ws = T_("pws", [128, NLAT], F32, es=es)
                yp = T_("pyp", [128, NLAT], BF16, es=es)
                pw = T_("ppw", [128, 512], BF16, es=es, dma="pool")
                fw.dma("pool", pw[:], poolw, [], [pw.b], pw.sem)
                pct = T_("pct", [128, 68], F32, es=es, dma=True)
                fw.dma("sp", pct[:, 0:64], pcorr, [], [pct.b], pct.sem, merge=True)
                fw.dma("sp", pct[:, 64:68], pools, [], [pct.b], pct.sem, merge=True)
                fw.op("dve", lambda e: e.memset(pv[:], 0.0), writes=[pv.b])
                for g in range(4):
                    r = (1, 2, 4, 8)[g]
                    w = wch.next()
                    fw.dma("pool", w[:], w_in1[g], [], [w.b], w.sem)
                    for (off, n) in LATBLKS:
                        k = BLKS.index((off, n))
                        pst = ps_next()
                        fw.mm(pst[:, 0:n], [(w[:, kc * 128:(kc + 1) * 128], hT[:, kc, off:off + n]) for kc in range(KC)],
                              reads=[w.b, hT_b[k]], writes=[pst.b])
                        c0 = 8 + off - NCTX
                        fw.op("act", lambda e, pst=pst, c0=c0, n=n: e.activation(out=pv[:, c0:c0 + n], in_=pst[:, 0:n], func=AF.Identity),
                              reads=[pst.b], writes=[pv.b])
                    src = pv
                    width = 1
                    ai = 0
                    while width < 2 * r:
                        dst = A[ai % 2]
                        ai += 1
                        cnt = NE - 2 * width + 1
                        fw.op("dve", lambda e, dst=dst, src=src, cnt=cnt, width=width: e.tensor_tensor(
                            out=dst[:, 0:cnt], in0=src[:, 0:cnt], in1=src[:, width:width + cnt], op=ALU.add),
                            reads=[src.b], writes=[dst.b])
                        src = dst
                        width *= 2
                    fw.op("dve", lambda e, src=src, r=r: e.tensor_tensor(
                        out=ws[:], in0=src[:, 8 - r:8 - r + NLAT], in1=pv[:, 8 + r:8 + r + NLAT], op=ALU.add),
                        reads=[src.b, pv.b], writes=[ws.b])
                    fw.op("dve", lambda e, g=g: e.tensor_tensor(out=ws[:, 0:8], in0=ws[:, 0:8], in1=pct[:, (g * 2) * 8:(g * 2) * 8 + 8], op=ALU.mult),
                          reads=[ws.b, pct.b], writes=[ws.b])
                    fw.op("dve", lambda e, g=g: e.tensor_tensor(out=ws[:, NLAT - 8:NLAT], in0=ws[:, NLAT - 8:NLAT],
                                                                in1=pct[:, (g * 2 + 1) * 8:(g * 2 + 1) * 8 + 8], op=ALU.mult),
                          reads=[ws.b, pct.b], writes=[ws.b])
                    fw.op("dve", lambda e, r=r: e.scalar_tensor_tensor(out=yp[:], in0=ws[:], scalar=1.0 / (2 * r + 1), in1=pv[:, 8:8 + NLAT],
                                                                       op0=ALU.mult, op1=ALU.subtract),
                          reads=[ws.b, pv.b], writes=[yp.b])
                    for (off, n) in LATBLKS:
                        pst = ps_next()
                        fw.mm(pst[:, 0:n], [(pw[:, g * 128:(g + 1) * 128], yp[:, off - NCTX:off - NCTX + n])], reads=[pw.b, yp.b], writes=[pst.b])
                        fw.op("act", lambda e, pst=pst, off=off, n=n, g=g: e.activation(
                            out=yT[:, g, off:off + n], in_=pst[:, 0:n], func=AF.Identity, scale=pct[:, 64 + g:65 + g]),
                            reads=[pst.b, pct.b], writes=[yT_b[g]])
                fw.barrier()
            with ExitStack() as es:
                qT = T_("rqT", [128, 2, NLAT], BF16, es=es)
                qdf = T_("rqdf", [128, 2, NLAT], BF16, es=es)
                qdb = T_("rqdb", [128, 2, NLAT], BF16, es=es)
                kT = T_("rkT", [128, 2, T], BF16, es=es)
                Vtok = T_("rVtok", [128, 18, 512], BF16, es=es)
                Sall = [T_("rSf", [128, 2, 16, 128], BF16, es=es), T_("rSb", [128, 2, 16, 128], BF16, es=es)]
                q_b = [Buf() for _ in range(2)]
                k_b = [Buf() for _ in range(2)]
                rt = T_("rt", [128, RT], F32, es=es, dma=True)
                fw.dma("sp", rt[:], rtab, [], [rt.b], rt.sem)
                dct = T_("dct", [128, 13], F32, es=es, dma=True)
                fw.dma("sp", dct[:, 0:4], decp, [], [dct.b], dct.sem, merge=True)
                fw.dma("sp", dct[:, 4:12], decr, [], [dct.b], dct.sem, merge=True)
                fw.dma("sp", dct[:, 12:13], gng, [], [dct.b], dct.sem, merge=True)
                lg = T_("lg", [128, 12], F32, es=es)
                dk = T_("dk", [128, 12], F32, es=es)
                qdecr = T_("qdecr", [128, 4, 512], F32, es=es)
                maskr = T_("maskr", [128, 4, 512], F32, es=es)
                ident = T_("identb", [128, 128], BF16, es=es)
                fw.op("act", lambda e: e.activation(out=lg[:], in_=dct[:, 0:12], func=AF.Exp), reads=[dct.b], writes=[lg.b])
                fw.op("dve", lambda e: e.tensor_scalar(out=lg[:], in0=lg[:], scalar1=-1.0, scalar2=None, op0=ALU.mult), reads=[lg.b], writes=[lg.b])
                fw.op("dve", lambda e: e.tensor_copy(out=ident[:], in_=rt[:, o_id:o_id + 128]), reads=[rt.b], writes=[ident.b])
                for d_ in range(2):
                    fw.op("act", lambda e, d_=d_: e.activation(out=dk[:, d_ * 4:d_ * 4 + 4], in_=lg[:, 4 + d_ * 4:8 + d_ * 4], func=AF.Exp,
                                                               scale=rt[:, o_kpos + d_:o_kpos + d_ + 1]),
                          reads=[lg.b, rt.b], writes=[dk.b])
                fw.op("act", lambda e: e.activation(out=dk[:, 8:12], in_=lg[:, 0:4], func=AF.Exp, scale=128.0), reads=[lg.b], writes=[dk.b])
                for d_ in range(2):
                    for hp in range(2):
                        for rep in range(4):
                            fw.op("act", lambda e, d_=d_, hp=hp, rep=rep: e.activation(
                                out=qdecr[:, d_ * 2 + hp, rep * 128:(rep + 1) * 128], in_=rt[:, o_pq0 + d_ * 128:o_pq0 + (d_ + 1) * 128],
                                func=AF.Exp, scale=lg[:, d_ * 2 + hp:d_ * 2 + hp + 1]), reads=[rt.b, lg.b], writes=[qdecr.b])
                with ExitStack() as es2:
                    m1 = T_("m1", [128, 128], F32, es=es2)
                    m2 = T_("m2", [128, 128], F32, es=es2)
                    for h in range(4):
                        fw.op("act", lambda e, h=h: e.activation(out=m1[:], in_=rt[:, o_dpos:o_dpos + 128], func=AF.Exp, scale=lg[:, 4 + h:5 + h]),
                              reads=[rt.b, lg.b], writes=[m1.b])
                        fw.op("dve", lambda e: e.tensor_tensor(out=m1[:], in0=m1[:], in1=rt[:, o_lge:o_lge + 128], op=ALU.mult),
                              reads=[m1.b, rt.b], writes=[m1.b])
                        fw.op("act", lambda e, h=h: e.activation(out=m2[:], in_=rt[:, o_dneg:o_dneg + 128], func=AF.Exp, scale=lg[:, 8 + h:9 + h]),
                              reads=[rt.b, lg.b], writes=[m2.b])
                        fw.op("dve", lambda e: e.tensor_tensor(out=m2[:], in0=m2[:], in1=rt[:, o_ule:o_ule + 128], op=ALU.mult),
                              reads=[m2.b, rt.b], writes=[m2.b])
                        for rep in range(4):
                            fw.op("dve", lambda e, h=h, rep=rep: e.tensor_tensor(out=maskr[:, h, rep * 128:(rep + 1) * 128], in0=m1[:], in1=m2[:], op=ALU.add),
                                  reads=[m1.b, m2.b], writes=[maskr.b])
                    fw.barrier()
                with ExitStack() as es2:
                    cosT = T_("rcos", [128, T], F32, es=es2, dma=True)
                    sinT = T_("rsin", [128, T], F32, es=es2, dma=True)
                    fw.dma("sp", cosT[:], ropec, [], [cosT.b], cosT.sem)
                    fw.dma("sp", sinT[:], ropes, [], [sinT.b], sinT.sem)
                    t1r = Ring(fw, es2, "rt1", 2, [128, 512], F32)
                    t2r = Ring(fw, es2, "rt2", 2, [128, 512], F32)
                    for which in ("q", "k"):
                        base = 4 if which == "q" else 8
                        for hp in range(2):
                            wm = wch.next()
                            fw.dma("pool", wm[:], w_in1[base + hp], [], [wm.b], wm.sem)
                            wr = wch.next()
                            fw.dma("pool", wr[:], w_in1[base + 2 + hp], [], [wr.b], wr.sem)
                            for (off, n) in (LATBLKS if which == "q" else BLKS):
                                k = BLKS.index((off, n))
                                p1 = ps_next()
                                fw.mm(p1[:, 0:n], [(wm[:, kc * 128:(kc + 1) * 128], hT[:, kc, off:off + n]) for kc in range(KC)],
                                      reads=[wm.b, hT_b[k]], writes=[p1.b])
                                p2 = ps_next()
                                fw.mm(p2[:, 0:n], [(wr[:, kc * 128:(kc + 1) * 128], hT[:, kc, off:off + n]) for kc in range(KC)],
                                      reads=[wr.b, hT_b[k]], writes=[p2.b])
                                t1 = t1r.next()
                                t2 = t2r.next()
                                sc_ = 0.125 if which == "q" else 1.0
                                fw.op("dve", lambda e, t1=t1, p1=p1, n=n, off=off, sc_=sc_: e.scalar_tensor_tensor(
                                    out=t1[:, 0:n], in0=p1[:, 0:n], scalar=sc_, in1=cosT[:, off:off + n], op0=ALU.mult, op1=ALU.mult),
                                    reads=[p1.b, cosT.b], writes=[t1.b])
                                fw.op("dve", lambda e, t2=t2, p2=p2, n=n, off=off, sc_=sc_: e.scalar_tensor_tensor(
                                    out=t2[:, 0:n], in0=p2[:, 0:n], scalar=sc_, in1=sinT[:, off:off + n], op0=ALU.mult, op1=ALU.mult),
                                    reads=[p2.b, sinT.b], writes=[t2.b])
                                if which == "k":
                                    fw.op("dve", lambda e, t1=t1, t2=t2, n=n, off=off, hp=hp: e.tensor_tensor(
                                        out=kT[:, hp, off:off + n], in0=t1[:, 0:n], in1=t2[:, 0:n], op=ALU.add),
                                        reads=[t1.b, t2.b], writes=[k_b[hp]])
                                else:
                                    lo_ = off - NCTX
                                    fw.op("dve", lambda e, t1=t1, t2=t2, n=n: e.tensor_tensor(out=t1[:, 0:n], in0=t1[:, 0:n], in1=t2[:, 0:n], op=ALU.add),
                                          reads=[t1.b, t2.b], writes=[t1.b])
                                    fw.op("act", lambda e, t1=t1, n=n, lo_=lo_, hp=hp: e.activation(out=qT[:, hp, lo_:lo_ + n], in_=t1[:, 0:n], func=AF.Identity),
                                          reads=[t1.b], writes=[q_b[hp]])
                                    fw.op("dve", lambda e, t1=t1, n=n, lo_=lo_, hp=hp: e.tensor_tensor(
                                        out=qdf[:, hp, lo_:lo_ + n], in0=t1[:, 0:n], in1=qdecr[:, hp, 0:n], op=ALU.mult),
                                        reads=[t1.b, qdecr.b], writes=[q_b[hp]])
                                    fw.op("dve", lambda e, t1=t1, n=n, lo_=lo_, hp=hp: e.tensor_tensor(
                                        out=qdb[:, hp, lo_:lo_ + n], in0=t1[:, 0:n], in1=qdecr[:, 2 + hp, 0:n], op=ALU.mult),
                                        reads=[t1.b, qdecr.b], writes=[q_b[hp]])
                    wv = T_("rwv", [128, KC * 512], BF16, es=es2, dma="pool")
                    fw.dma("pool", wv[:], w_v1, [], [wv.b], wv.sem)
                    for tt_ in range(18):
                        kb = 0 if tt_ < 2 else 1 + (tt_ - 2) // 4
                        pst = ps_next()
                        fw.mm(pst[:, 0:512], [(hT[:, kc, tt_ * 128:(tt_ + 1) * 128], wv[:, kc * 512:(kc + 1) * 512]) for kc in range(KC)],
                              reads=[wv.b, hT_b[kb]], writes=[pst.b])
                        fw.op("act", lambda e, pst=pst, tt_=tt_: e.activation(out=Vtok[:, tt_, :], in_=pst[:, 0:512], func=AF.Identity),
                              reads=[pst.b], writes=[Vtok.b])
                    fw.barrier()
                with ExitStack() as es2:
                    Ktok = [T_("rKf", [128, 18, 256], BF16, es=es2), T_("rKb", [128, 18, 256], BF16, es=es2)]
                    for tt_ in range(18):
                        for hp in range(2):
                            pst = ps_next()
                            fw.mm(pst[:, 0:128], [(kT[:, hp, tt_ * 128:(tt_ + 1) * 128], ident[:])], reads=[k_b[hp], ident.b], writes=[pst.b])
                            for d_ in range(2):
                                for hh in range(2):
                                    h = hp * 2 + hh
                                    eng = "act" if d_ == 0 else "dve"
                                    if eng == "act":
                                        fw.op("act", lambda e, pst=pst, tt_=tt_, h=h, hh=hh, d_=d_: e.activation(
                                            out=Ktok[d_][:, tt_, h * 64:(h + 1) * 64], in_=pst[:, hh * 64:(hh + 1) * 64], func=AF.Identity,
                                            scale=dk[:, d_ * 4 + h:d_ * 4 + h + 1]), reads=[pst.b, dk.b], writes=[Ktok[d_].b])
                                    else:
                                        fw.op("dve", lambda e, pst=pst, tt_=tt_, h=h, hh=hh, d_=d_: e.tensor_scalar(
                                            out=Ktok[d_][:, tt_, h * 64:(h + 1) * 64], in0=pst[:, hh * 64:(hh + 1) * 64],
                                            scalar1=dk[:, d_ * 4 + h:d_ * 4 + h + 1], scalar2=None, op0=ALU.mult),
                                            reads=[pst.b, dk.b], writes=[Ktok[d_].b])
                    St = [[T_(f"rS{d_}{hp}", [128, 128], F32, es=es2) for hp in range(2)] for d_ in range(2)]
                    orders = {0: list(range(18)), 1: [1, 0] + list(range(17, 1, -1))}
                    chains = [(d_, hp) for d_ in range(2) for hp in range(2)]
                    for (d_, hp) in chains:
                        S = St[d_][hp]
                        fw.op("dve", lambda e, S=S: e.memset(S[:], 0.0), writes=[S.b])
                    for step in range(18):
                        for (d_, hp) in chains:
                            S = St[d_][hp]
                            n_ = orders[d_][step]
                            if n_ >= 2:
                                fw.op("act", lambda e, S=S, d_=d_, hp=hp, n_=n_: e.activation(out=Sall[d_][:, hp, n_ - 2, :], in_=S[:], func=AF.Identity),
                                      reads=[S.b], writes=[Sall[d_].b])
                            if step == 17:
                                continue
                            pkv = ps_next()
                            fw.mm(pkv[:, 0:256], [(Ktok[d_][:, n_, hp * 128:(hp + 1) * 128], Vtok[:, n_, hp * 256:(hp + 1) * 256])],
                                  reads=[Ktok[d_].b, Vtok.b], writes=[pkv.b])
                            cd = dk[:, 8 + d_ * 2 + hp:9 + d_ * 2 + hp]
                            fw.op("dve", lambda e, S=S, pkv=pkv, cd=cd: e.scalar_tensor_tensor(
                                out=S[0:64, :], in0=S[0:64, :], scalar=cd[0:64, :], in1=pkv[0:64, 0:128], op0=ALU.mult, op1=ALU.add),
                                reads=[S.b, pkv.b, dk.b], writes=[S.b])
                            fw.op("dve", lambda e, S=S, pkv=pkv, cd=cd: e.scalar_tensor_tensor(
                                out=S[64:128, :], in0=S[64:128, :], scalar=cd[64:128, :], in1=pkv[64:128, 128:256], op0=ALU.mult, op1=ALU.add),
                                reads=[S.b, pkv.b, dk.b], writes=[S.b])
                    fw.barrier()
                with ExitStack() as es2:
                    wg = [T_(f"rwg{h}", [128, KC * 128], BF16, es=es2, dma="pool") for h in range(4)]
                    for h in range(4):
                        fw.dma("pool", wg[h][:], w_in1[12 + h], [], [wg[h].b], wg[h].sem)
                    Abf = Ring(fw, es2, "rAbf", 3, [128, 512], BF16)
                    sqrr = Ring(fw, es2, "rsq", 2, [128, 512], BF16)
                    lnr = T_("rln", [128, 512], F32, es=es2)
                    rsr = T_("rrs", [128, 512], F32, es=es2)
                    yrr = T_("ryr", [128, 512], F32, es=es2)
                    sgr = T_("rsg", [128, 512], F32, es=es2)
                    R3U = [(h, bidx) for h in range(4) for bidx in range(len(LATBLKS))]
                    st3 = {}

                    def s12(u):
                        h, bidx = R3U[u]
                        hp, r0 = h // 2, (h % 2) * 64
                        pa = ps_next()
                        for cc in range(4):
                            c = bidx * 4 + cc
                            n_ = c + 2
                            fw.mm(pa[:, cc * 128:(cc + 1) * 128], [(kT[r0:r0 + 64, hp, n_ * 128:(n_ + 1) * 128], qT[r0:r0 + 64, hp, c * 128:(c + 1) * 128])],
                                  reads=[k_b[hp], q_b[hp]], writes=[pa.b])
                        ab = Abf.next()
                        fw.op("dve", lambda e: e.tensor_tensor(out=ab[:], in0=pa[:], in1=maskr[:, h, :], op=ALU.mult),
                              reads=[pa.b, maskr.b], writes=[ab.b])
                        st3[u] = {"ab": ab}

                    def s34(u):
                        h, bidx = R3U[u]
                        hp, r0 = h // 2, (h % 2) * 64
                        ab = st3[u]["ab"]
                        po = ps_next()
                        for cc in range(4):
                            c = bidx * 4 + cc
                            n_ = c + 2
                            fw.mm(po[:, cc * 128:(cc + 1) * 128],
                                  [(Vtok[:, n_, h * 128:(h + 1) * 128], ab[:, cc * 128:(cc + 1) * 128]),
                                   (Sall[0][r0:r0 + 64, hp, c, :], qdf[r0:r0 + 64, hp, c * 128:(c + 1) * 128]),
                                   (Sall[1][r0:r0 + 64, hp, c, :], qdb[r0:r0 + 64, hp, c * 128:(c + 1) * 128])],
                                  reads=[Vtok.b, ab.b, Sall[0].b, Sall[1].b, q_b[hp]], writes=[po.b])
                        sq = sqrr.next()
                        fw.op("act", lambda e: e.activation(out=sq[:], in_=po[:], func=AF.Square), reads=[po.b], writes=[sq.b])
                        st3[u]["po"] = po
                        st3[u]["sq"] = sq

                    def s56(u):
                        h, bidx = R3U[u]
                        off, n = LATBLKS[bidx]
                        k = BLKS.index((off, n))
                        po, sq = st3[u]["po"], st3[u]["sq"]
                        pm = ps_next()
                        fw.mm(pm[:], [(ones_h[:], sq[:])], reads=[ones_h.b, sq.b], writes=[pm.b])
                        pg = ps_next()
                        fw.mm(pg[:, 0:n], [(wg[h][:, kc * 128:(kc + 1) * 128], hT[:, kc, off:off + n]) for kc in range(KC)],
                              reads=[wg[h].b, hT_b[k]], writes=[pg.b])
                        fw.op("act", lambda e: e.activation(out=lnr[:], in_=pm[:], func=AF.Ln, bias=epsT[:], scale=1.0),
                              reads=[pm.b, epsT.b], writes=[lnr.b])
                        fw.op("act", lambda e: e.activation(out=rsr[:], in_=lnr[:], func=AF.Exp, scale=-0.5), reads=[lnr.b], writes=[rsr.b])
                        fw.op("dve", lambda e: e.scalar_tensor_tensor(out=yrr[:], in0=po[:], scalar=dct[:, 12:13], in1=rsr[:],
                                                                      op0=ALU.mult, op1=ALU.mult),
                              reads=[po.b, dct.b, rsr.b], writes=[yrr.b])
                        fw.op("act", lambda e: e.activation(out=sgr[:], in_=pg[:], func=AF.Silu), reads=[pg.b], writes=[sgr.b])
                        fw.op("dve", lambda e: e.tensor_tensor(out=yT[:, 4 + h, off:off + n], in0=yrr[:], in1=sgr[:], op=ALU.mult),
                              reads=[yrr.b, sgr.b], writes=[yT_b[4 + h]])
                        del st3[u]

                    NU = len(R3U)
                    for it in range(NU + 2):
                        if 0 <= it - 2 < NU:
                            s56(it - 2)
                        if 0 <= it - 1 < NU:
                            s34(it - 1)
                        if it < NU:
                            s12(it)
                    fw.barrier()
            dump_bf16("y1", yT, yT_b, 8, T)
            wo1 = load_wo(1, esm)
            outproj(1, bi, LATBLKS, jl, yT, yT_b, wo1)

    for bi in range(nb):
        jl = bi
        with ExitStack() as es:
            xs = Ring(fw, es, "p1x", 2, [128, KC, 512], F32, dma=True)
            nt = norm_tiles(es, "p1")
            for k, (off, n) in enumerate(BLKS):
                x_t = xs.next()
                src, sb = x_src(0, bi, k)
                fw.dma("sp", x_t[:, :, 0:n], src, [sb] if sb else [], [x_t.b], x_t.sem)
                norm_block(nt, x_t, n, off, k, 0, 0, 2 if k == 0 else jl)
            fw.barrier()
        with ExitStack() as esm:
            yT = T_("yT", [128, KC, T], BF16, es=esm)
            yT_b = [Buf(f"yT{c}") for c in range(KC)]
            wch = Ring(fw, esm, "wch", 3, [128, KC * 128], BF16, dma="pool")
            with ExitStack() as es:
                fw.enabled = on("P2a")
                mbuf = T_("mbuf", [128, PT], F32, es=es)
                cgf = T_("cgf", [128, PT], F32, es=es)
                bgf = T_("bgf", [128, PT], F32, es=es)
                ybuf = T_("ybuf", [128, PT], F32, es=es)
                fw.op("dve", lambda e: e.memset(mbuf[:], 0.0), writes=[mbuf.b])
                for jc in range(4):
                    for typ, c in (("cg", 4 + jc), ("xv", 8 + jc), ("bg", jc)):
                        w = wch.next()
                        fw.dma("pool", w[:], w_in0[c], [], [w.b], w.sem)
                        for k, (off, n) in enumerate(BLKS):
                            po = poff(off)
                            pst = ps_next()
                            fw.mm(pst[:, 0:n], [(w[:, kc * 128:(kc + 1) * 128], hT[:, kc, off:off + n]) for kc in range(KC)],
                                  reads=[w.b, hT_b[k]], writes=[pst.b])
                            if typ == "cg":
                                fw.op("act", lambda e, pst=pst, po=po, n=n: e.activation(out=cgf[:, po:po + n], in_=pst[:, 0:n], func=AF.Identity),
                                      reads=[pst.b], writes=[cgf.b])
                            elif typ == "xv":
                                fw.op("dve", lambda e, pst=pst, po=po, n=n: e.tensor_tensor(out=mbuf[:, po:po + n], in0=pst[:, 0:n], in1=cgf[:, po:po + n], op=ALU.mult),
                                      reads=[pst.b, cgf.b], writes=[mbuf.b])
                            else:
                                fw.op("act", lambda e, pst=pst, po=po, n=n: e.activation(out=bgf[:, po:po + n], in_=pst[:, 0:n], func=AF.Identity),
                                      reads=[pst.b], writes=[bgf.b])
                    wb = o_scw + jc * 3
                    fw.op("dve", lambda e, wb=wb: e.tensor_scalar(out=ybuf[:, 1:PT - 1], in0=mbuf[:, 0:PT - 2], scalar1=C(wb), scalar2=None, op0=ALU.mult),
                          reads=[mbuf.b, smallc.b], writes=[ybuf.b])
                    fw.op("dve", lambda e, wb=wb: e.scalar_tensor_tensor(out=ybuf[:, 1:PT - 1], in0=mbuf[:, 1:PT - 1], scalar=C(wb + 1), in1=ybuf[:, 1:PT - 1],
                                                                         op0=ALU.mult, op1=ALU.add), reads=[mbuf.b, smallc.b, ybuf.b], writes=[ybuf.b])
                    fw.op("dve", lambda e, wb=wb: e.scalar_tensor_tensor(out=ybuf[:, 1:PT - 1], in0=mbuf[:, 2:PT], scalar=C(wb + 2), in1=ybuf[:, 1:PT - 1],
                                                                         op0=ALU.mult, op1=ALU.add), reads=[mbuf.b, smallc.b, ybuf.b], writes=[ybuf.b])
                    fw.op("dve", lambda e, jc=jc: e.tensor_tensor(out=yT[:, jc, 0:NCTX], in0=ybuf[:, 1:1 + NCTX], in1=bgf[:, 1:1 + NCTX], op=ALU.mult),
                          reads=[ybuf.b, bgf.b], writes=[yT_b[jc]])
                    fw.op("dve", lambda e, jc=jc: e.tensor_tensor(out=yT[:, jc, NCTX:T], in0=ybuf[:, NCTX + 2:PT - 1], in1=bgf[:, NCTX + 2:PT - 1], op=ALU.mult),
                          reads=[ybuf.b, bgf.b], writes=[yT_b[jc]])
                fw.barrier()
            fw.enabled = on("P4")
            wo0 = load_wo(0, esm)
            with ExitStack() as es:
                fw.enabled = on("P2b")
                qT = T_("qT", [128, 4, T], BF16, es=es)
                kT = T_("kT", [128, 4, T], BF16, es=es)
                qT_b = [Buf() for _ in range(4)]
                kT_b = [Buf() for _ in range(4)]
                Vtok = T_("Vtok", [128, 18, 512], BF16, es=es)
                with ExitStack() as es2:
                    cosT = T_("cosT", [128, T], F32, es=es2, dma=True)
                    sinT = T_("sinT", [128, T], F32, es=es2, dma=True)
                    fw.dma("sp", cosT[:], ropec, [], [cosT.b], cosT.sem)
                    fw.dma("sp", sinT[:], ropes, [], [sinT.b], sinT.sem)
                    sqk = Ring(fw, es2, "sqk", 2, [128, 512], BF16)
                    lnk = Ring(fw, es2, "lnk", 2, [128, 512], F32)
                    rsk = Ring(fw, es2, "rsk", 2, [128, 512], F32)
                    t1r = Ring(fw, es2, "t1r", 2, [128, 512], F32)
                    t2r = Ring(fw, es2, "t2r", 2, [128, 512], F32)
                    for (cm, cr, gcol, dst, dst_b) in (() if "P2bqk" in skip else ((12, 16, o_qkg, qT, qT_b), (20, 24, o_qkg + 2, kT, kT_b))):
                        for h in range(4):
                            wm = wch.next()
                            fw.dma("pool", wm[:], w_in0[cm + h], [], [wm.b], wm.sem)
                            wr = wch.next()
                            fw.dma("pool", wr[:], w_in0[cr + h], [], [wr.b], wr.sem)
                            def stage_a(k, off, n, wm=wm, wr=wr):
                                p1 = ps_next()
                                fw.mm(p1[:, 0:n], [(wm[:, kc * 128:(kc + 1) * 128], hT[:, kc, off:off + n]) for kc in range(KC)],
                                      reads=[wm.b, hT_b[k]], writes=[p1.b])
                                p2 = ps_next()
                                fw.mm(p2[:, 0:n], [(wr[:, kc * 128:(kc + 1) * 128], hT[:, kc, off:off + n]) for kc in range(KC)],
                                      reads=[wr.b, hT_b[k]], writes=[p2.b])
                                sq = sqk.next()
                                fw.op("act", lambda e: e.activation(out=sq[:, 0:n], in_=p1[:, 0:n], func=AF.Square),
                                      reads=[p1.b], writes=[sq.b])
                                return p1, p2, sq

                            def stage_b(k, off, n, p1, p2, sq, h=h, gcol=gcol, dst=dst, dst_b=dst_b):
                                p3 = ps_next()
                                fw.mm(p3[:, 0:n], [(ones_g[:], sq[:, 0:n])], reads=[ones_g.b, sq.b], writes=[p3.b])
                                ln = lnk.next()
                                rs = rsk.next()
                                fw.op("act", lambda e: e.activation(out=ln[:, 0:n], in_=p3[:, 0:n], func=AF.Ln, bias=epsT[:], scale=1.0),
                                      reads=[p3.b, epsT.b], writes=[ln.b])
                                fw.op("act", lambda e: e.activation(out=rs[:, 0:n], in_=ln[:, 0:n], func=AF.Exp, scale=-0.5),
                                      reads=[ln.b], writes=[rs.b])
                                t1 = t1r.next()
                                t2 = t2r.next()
                                fw.op("dve", lambda e: e.scalar_tensor_tensor(
                                    out=t1[:, 0:n], in0=p1[:, 0:n], scalar=C(gcol), in1=cosT[:, off:off + n], op0=ALU.mult, op1=ALU.mult),
                                    reads=[p1.b, smallc.b, cosT.b], writes=[t1.b])
                                fw.op("dve", lambda e: e.scalar_tensor_tensor(
                                    out=t2[:, 0:n], in0=p2[:, 0:n], scalar=C(gcol + 1), in1=sinT[:, off:off + n], op0=ALU.mult, op1=ALU.mult),
                                    reads=[p2.b, smallc.b, sinT.b], writes=[t2.b])
                                fw.op("dve", lambda e: e.tensor_tensor(out=t1[:, 0:n], in0=t1[:, 0:n], in1=t2[:, 0:n], op=ALU.add),
                                      reads=[t1.b, t2.b], writes=[t1.b])
                                fw.op("dve", lambda e: e.tensor_tensor(
                                    out=dst[:, h, off:off + n], in0=t1[:, 0:n], in1=rs[:, 0:n], op=ALU.mult),
                                    reads=[t1.b, rs.b], writes=[dst_b[h]])

                            prev = stage_a(0, *BLKS[0])
                            for k, (off, n) in enumerate(BLKS):
                                nxt = stage_a(k + 1, *BLKS[k + 1]) if k + 1 < len(BLKS) else None
                                stage_b(k, off, n, *prev)
                                prev = nxt
                    wv = T_("wv", [128, KC * 512], BF16, es=es2, dma="pool")
                    fw.dma("pool", wv[:], w_v0, [], [wv.b], wv.sem)
                    for tt_ in (() if "P2bv" in skip else range(18)):
                        kb = 0 if tt_ < 2 else 1 + (tt_ - 2) // 4
                        pst = ps_next()
                        fw.mm(pst[:, 0:512], [(hT[:, kc, tt_ * 128:(tt_ + 1) * 128], wv[:, kc * 512:(kc + 1) * 512]) for kc in range(KC)],
                              reads=[wv.b, hT_b[kb]], writes=[pst.b])
                        fw.op("act", lambda e, pst=pst, tt_=tt_: e.activation(out=Vtok[:, tt_, :], in_=pst[:, 0:512], func=AF.Identity),
                              reads=[pst.b], writes=[Vtok.b])
                    fw.barrier()
                dump_bf16("qT", qT, qT_b, 4, T)
                dump_bf16("kT", kT, kT_b, 4, T)
                with ExitStack() as es2:
                    fw.enabled = on("P3")
                    Er = Ring(fw, es2, "Er", 8, [128, 512], BF16)
                    r1 = T_("r1", [128, 512], F32, es=es2)
                    r2 = T_("r2", [128, 512], F32, es=es2)
                    o1 = T_("o1", [128, 512], F32, es=es2)
                    o2 = T_("o2", [128, 512], F32, es=es2)
                    osq = T_("osq", [128, 512], BF16, es=es2)
                    oln = T_("oln", [128, 512], F32, es=es2)
                    ors = T_("ors", [128, 512], F32, es=es2)
                    STP = [(PS[0], PS[1]), (PS[6], PS[7])]
                    stc = [0]
                    pend = [None]
                    accE = [T_("accE0", [128, 512], F32, es=es2), T_("accE1", [128, 512], F32, es=es2)]
                    ones_f = T_("ones_f", [128, 128], F32, es=es2)
                    fw.op("dve", lambda e: e.memset(ones_f[:], 1.0), writes=[ones_f.b])

                    def epi_a(n):
                        fw.op("dve", lambda e: e.reciprocal(out=r1[:, 0:n], in_=PS[4][:, 0:n]), reads=[PS[4].b], writes=[r1.b])
                        fw.op("dve", lambda e: e.reciprocal(out=r2[:, 0:n], in_=PS[5][:, 0:n]), reads=[PS[5].b], writes=[r2.b])
                        fw.op("dve", lambda e: e.tensor_tensor(out=o1[:, 0:n], in0=PS[2][:, 0:n], in1=r1[:, 0:n], op=ALU.mult),
                              reads=[PS[2].b, r1.b], writes=[o1.b])
                        fw.op("dve", lambda e: e.tensor_tensor(out=o2[:, 0:n], in0=PS[3][:, 0:n], in1=r2[:, 0:n], op=ALU.mult),
                              reads=[PS[3].b, r2.b], writes=[o2.b])
                        fw.op("dve", lambda e: e.scalar_tensor_tensor(out=o1[:, 0:n], in0=o2[:, 0:n], scalar=NEGLAM, in1=o1[:, 0:n],
                                                                      op0=ALU.mult, op1=ALU.add),
                              reads=[o1.b, o2.b, lam_t.b], writes=[o1.b])
                        fw.op("act", lambda e: e.activation(out=osq[:, 0:n], in_=o1[:, 0:n], func=AF.Square), reads=[o1.b], writes=[osq.b])

                    def epi_b(h, off, n):
                        fw.mm(PS[4][:, 0:n], [(ones_h[:], osq[:, 0:n])], reads=[ones_h.b, osq.b], writes=[PS[4].b])
                        fw.op("act", lambda e: e.activation(out=oln[:, 0:n], in_=PS[4][:, 0:n], func=AF.Ln, bias=epsT[:], scale=1.0),
                              reads=[PS[4].b, epsT.b], writes=[oln.b])
                        fw.op("act", lambda e: e.activation(out=ors[:, 0:n], in_=oln[:, 0:n], func=AF.Exp, scale=-0.5),
                              reads=[oln.b], writes=[ors.b])
                        fw.op("dve", lambda e: e.scalar_tensor_tensor(
                            out=yT[:, 4 + h, off:off + n], in0=o1[:, 0:n], scalar=SUBG, in1=ors[:, 0:n], op0=ALU.mult, op1=ALU.mult),
                            reads=[o1.b, lam_t.b, ors.b], writes=[yT_b[4 + h]])

                    for h in range(4):
                        for k, (off, n) in enumerate(BLKS):
                            kts = [0, 1] if k == 0 else list(range(18))
                            sts = {}

                            def emit_st(i, h=h, off=off, n=n, kts=kts, sts=sts):
                                kt = kts[i]
                                pair = STP[stc[0] % 2]
                                stc[0] += 1
                                for m in range(2):
                                    fw.mm(pair[m][:, 0:n], [(kT[m * 64:(m + 1) * 64, h, kt * 128:(kt + 1) * 128], qT[m * 64:(m + 1) * 64, h, off:off + n])],
                                          reads=[kT_b[h], qT_b[h]], writes=[pair[m].b])
                                sts[i] = pair

                            emit_st(0)
                            trig = min(4, len(kts) - 1)
                            for i, kt in enumerate(kts):
                                if i + 1 < len(kts):
                                    emit_st(i + 1)
                                pair = sts.pop(i)
                                ees = []
                                for m in range(2):
                                    ee = Er.next()
                                    fw.op("act", lambda e, ee=ee, st=pair[m], n=n: e.activation(out=ee[:, 0:n], in_=st[:, 0:n], func=AF.Exp, scale=0.125),
                                          reads=[pair[m].b], writes=[ee.b])
                                    ees.append(ee)
                                for m in range(2):
                                    fw.mm(PS[2 + m][:, 0:n], [(Vtok[:, kt, h * 128:(h + 1) * 128], ees[m][:, 0:n])],
                                          reads=[Vtok.b, ees[m].b], writes=[PS[2 + m].b], start=(i == 0), stop=(i == len(kts) - 1))
                                for m in range(2):
                                    fw.mm(PS[4 + m][:, 0:n], [(ones_1[:], ees[m][:, 0:n])],
                                          reads=[ones_1.b, ees[m].b], writes=[PS[4 + m].b], start=(i == 0), stop=(i == len(kts) - 1))
                            epi_a(n)
                            epi_b(h, off, n)
                    if pend[0] is not None:
                        pend[0]()
                        pend[0] = None
                    fw.barrier()
            dump_bf16("yT", yT, yT_b, 8, T)
            fw.enabled = on("P4")
            outproj(0, bi, BLKS, jl, yT, yT_b, wo0)
        fw.enabled = True
        dump_res("xmid0", bi)
        ffn(0, bi, BLKS, jl)
        dump_res("xout0", bi)
        if nlayers == 1:
            continue
        mixer1(bi, jl)
        dump_res("xmid1", bi)
        ffn(1, bi, LATBLKS, jl)

    return finish()


def _chunk_cols(w, cols):
    sel = w[:, cols]
    n = sel.shape[1] // 128
    t = sel.reshape(KC, 128, n, 128).transpose(2, 1, 0, 3)
    return np.ascontiguousarray(t.reshape(n, 128, KC * 128))


def _rope_partner(d):
    return d + 16 if (d % 32) < 16 else d - 16


def _rope_tables():
    inv = (10000.0 ** (-np.arange(16, dtype=np.float32) / 16)).astype(np.float32)
    t = np.arange(NLAT)
    row = (t // 64).astype(np.float32)
    col = (t % 64).astype(np.float32)
    c = np.ones((128, T), np.float32)
    s = np.zeros((128, T), np.float32)
    for p in range(128):
        d = p % 64
        axis, half, f = d // 32, (d % 32) // 16, d % 16
        ang = (row if axis == 0 else col) * inv[f]
        c[p, NCTX:] = np.cos(ang.astype(np.float32))
        sn = np.sin(ang.astype(np.float32))
        s[p, NCTX:] = -sn if half == 0 else sn
    return c, s


def _ret_tables():
    i = np.arange(128, dtype=np.float32)
    posq = np.concatenate([np.tile(i + 1.0, (128, 1)), np.tile(128.0 - i, (128, 1))], axis=1)
    kpos = np.stack([127.0 - i, i], axis=1)
    jj = i[:, None]
    ii = i[None, :]
    dpos = np.maximum(ii - jj, 0.0)
    dneg = np.maximum(jj - ii, 0.0)
    lge = (ii >= jj).astype(np.float32)
    ule = (jj >= ii).astype(np.float32)
    return np.ascontiguousarray(np.concatenate([posq, kpos, dpos, dneg, lge, ule, np.eye(128, dtype=np.float32)], axis=1).astype(np.float32))


def _pool_corr():
    out = np.ones((128, 4, 2, 8), np.float32)
    for g, r in enumerate((1, 2, 4, 8)):
        for t in range(r):
            cnt = (t + r) - 0 + 1
            out[:, g, 0, t] = (2 * r + 1) / cnt
            out[:, g, 1, 7 - t] = (2 * r + 1) / cnt
    return out.reshape(128, 64)


def prepare_inputs(inputs):
    f = lambda a: np.ascontiguousarray(np.asarray(a, dtype=np.float32))
    x, c, ctx, c_ctx = f(inputs["x"]), f(inputs["c"]), f(inputs["ctx"]), f(inputs["c_ctx"])
    sh = {}
    mod_w = f(inputs["mod_w"])
    sh["modw"] = np.ascontiguousarray(
        mod_w.reshape(2, KC, 128, 12, 4, 128).transpose(0, 3, 2, 4, 1, 5).reshape(2, 12, 128, 4 * KC * 128))
    sh["modb"] = np.ascontiguousarray(f(inputs["mod_b"]).reshape(2, 48, 128).transpose(2, 0, 1).reshape(128, 96))
    n1, n2 = f(inputs["norm1_g"]), f(inputs["norm2_g"])
    ngs = np.stack([n1, n2], axis=1)
    sh["ng"] = np.ascontiguousarray(ngs.reshape(2, 2, KC, 128).transpose(3, 0, 1, 2).reshape(128, 32))
    w0 = f(inputs["ev_w_in"])[0]
    partner = np.array([_rope_partner(d) for d in range(64)])
    cols = list(range(0, 1536))
    qcols = np.arange(1536, 2048)
    kcols = np.arange(2048, 2560)
    perm512 = np.concatenate([g * 64 + partner for g in range(8)])
    allcols = np.concatenate([np.arange(0, 1536), qcols, qcols[0] + perm512, kcols, kcols[0] + perm512])
    sh["w_in0"] = _chunk_cols(w0, allcols)
    sh["w_v0"] = np.ascontiguousarray(w0[:, 2560:3072].reshape(KC, 128, 512).transpose(1, 0, 2).reshape(128, KC * 512))
    wo = np.stack([f(inputs["ev_w_out"])[0], f(inputs["od_w_out"])[0]])
    sh["w_out"] = np.ascontiguousarray(wo.reshape(2, KC, 128, 1024).transpose(0, 2, 1, 3).reshape(2, 128, KC * 1024))
    sh["scw"] = np.ascontiguousarray(f(inputs["sc_conv_w"])[0].reshape(3, 4, 128).transpose(2, 1, 0).reshape(128, 12))
    gq, gk = f(inputs["da_q_norm"])[0], f(inputs["da_k_norm"])[0]
    pidx = np.arange(128) % 64
    sh["qkg"] = np.ascontiguousarray(np.stack([gq[pidx], gq[partner[pidx]], gk[pidx], gk[partner[pidx]]], axis=1))
    lv = np.stack([f(inputs["da_lq1"])[0], f(inputs["da_lk1"])[0], f(inputs["da_lq2"])[0], f(inputs["da_lk2"])[0]])
    sh["lamv"] = np.ascontiguousarray(np.broadcast_to(lv.reshape(1, 256), (128, 256)))
    sh["subg"] = np.ascontiguousarray(f(inputs["da_subln_g"])[0].reshape(128, 1))
    rc, rs = _rope_tables()
    sh["ropec"], sh["ropes"] = rc, rs
    wu = f(inputs["ffn_w_up"])
    wu = wu.reshape(2, KC, 128, 2, FJ, 128).transpose(0, 4, 2, 3, 1, 5)
    sh["w_up"] = np.ascontiguousarray(wu.reshape(2, FJ, 128, 2 * KC * 128))
    wd = f(inputs["ffn_w_down"])
    sh["w_dn"] = np.ascontiguousarray(wd.reshape(2, FJ, 128, 1024))
    sh["fcw"] = np.ascontiguousarray(f(inputs["ffn_conv_w"]).reshape(2, 3, FJ, 128).transpose(3, 0, 2, 1).reshape(128, 132))
    sh["fcb"] = np.ascontiguousarray(f(inputs["ffn_conv_b"]).reshape(2, FJ, 128).transpose(2, 0, 1).reshape(128, 44))
    w1 = f(inputs["od_w_in"])[0]
    partner256 = np.concatenate([g * 64 + partner for g in range(4)])
    cols1 = np.concatenate([np.arange(0, 512), np.arange(512, 768), 512 + partner256, np.arange(768, 1024),
                            768 + partner256, np.arange(1536, 2048)])
    sh["w_in1"] = _chunk_cols(w1, cols1)
    sh["w_v1"] = np.ascontiguousarray(w1[:, 1024:1536].reshape(KC, 128, 512).transpose(1, 0, 2).reshape(128, KC * 512))
    sh["poolw"] = np.ascontiguousarray(f(inputs["pool_w"])[0].transpose(1, 0, 2).reshape(128, 512))
    sh["pools"] = np.ascontiguousarray(f(inputs["pool_scale"])[0].reshape(4, 128).T)
    df, db = f(inputs["ret_decay_f"])[0], f(inputs["ret_decay_b"])[0]
    hp_of = np.arange(128) // 64
    decp = np.zeros((128, 2, 2), np.float32)
    for hp in range(2):
        decp[:, 0, hp] = df[2 * hp + hp_of]
        decp[:, 1, hp] = db[2 * hp + hp_of]
    sh["decp"] = decp.reshape(128, 4)
    sh["decr"] = np.ascontiguousarray(np.broadcast_to(np.concatenate([df, db]).reshape(1, 8), (128, 8)))
    sh["gng"] = np.ascontiguousarray(f(inputs["ret_gn_g"])[0].reshape(128, 1))
    sh["rtab"] = _ret_tables()
    sh["pcorr"] = _pool_corr()
    per_core = []
    for i in range(8):
        b0 = 2 * i
        d = dict(sh)
        d["xT"] = np.ascontiguousarray(x[b0:b0 + 2].transpose(0, 2, 1))
        d["ctxT"] = np.ascontiguousarray(ctx[b0:b0 + 2].transpose(0, 2, 1))
        cs = np.stack([c[b0], c[b0 + 1], c_ctx])
        d["cT"] = np.ascontiguousarray(cs.reshape(3, KC, 128).transpose(2, 1, 0).reshape(128, KC * 3))
        per_core.append(d)
    return per_core


def kernel(**inputs):
    per_core = prepare_inputs(inputs)
    nc = build_program()
    res = run_bass_kernel_spmd(nc, per_core, core_ids=list(range(8)))
    out = np.empty((16, NLAT, D), np.float32)
    for i in range(8):
        o = res.results[i]["outT"]
        out[2 * i:2 * i + 2] = o.transpose(0, 2, 1)
    return out
```
